# Optimizing a Trainium2 kernel written in Bass

```python
import math
import jax, jax.numpy as jnp
from jax import lax

D_MODEL = 1024
BATCH = 8
SEQ = 4096
DEPTH = 2
DEC_BATCH = 128
DEC_SEQ = 1
PAST_LEN = 16384
PAGE_SIZE = 128

MIX_DIM = D_MODEL
DN_DK = 128
DN_DV = 128
DN_HEADS = (MIX_DIM // 2) // DN_DV
SWA_DH = 64
SWA_HEADS = (MIX_DIM - DN_HEADS * DN_DV) // SWA_DH
SWA_KV_HEADS = SWA_HEADS // 4
SWA_GROUP = SWA_HEADS // SWA_KV_HEADS
WINDOW = 128
DN_CONV = 4
DN_CHUNK = 64
ROPE_THETA = 10000.0
D_FF = ((8 * D_MODEL // 3 + 127) // 128) * 128
N_MOD = 9
EPS = 1e-6
DN_QK = DN_HEADS * DN_DK
DN_V = DN_HEADS * DN_DV
DN_CONV_CH = 2 * DN_QK + DN_V
SWA_Q = SWA_HEADS * SWA_DH
SWA_KV = SWA_KV_HEADS * SWA_DH
SPLIT_POINTS = (DN_CONV_CH, DN_CONV_CH + DN_V, DN_CONV_CH + DN_V + DN_HEADS,
                DN_CONV_CH + DN_V + 2 * DN_HEADS, DN_CONV_CH + DN_V + 2 * DN_HEADS + SWA_Q,
                DN_CONV_CH + DN_V + 2 * DN_HEADS + SWA_Q + SWA_KV)
IN_DIM = SPLIT_POINTS[-1] + SWA_KV

kernel_name = 'hymba_gdn_swa_macaron_adaln_step'


def _rms_norm(x, w):
    xf = x.astype(jnp.float32)
    y = xf * lax.rsqrt(jnp.mean(xf * xf, axis=-1, keepdims=True) + EPS)
    return (y * w.astype(jnp.float32)).astype(x.dtype)


def _l2norm(x):
    return x * lax.rsqrt(jnp.sum(x * x, axis=-1, keepdims=True) + EPS)


def _adaln(c, w_ada, b_ada):
    m = jax.nn.silu(c) @ w_ada + b_ada
    return m.reshape(c.shape[0], N_MOD, 1, D_MODEL)


def _modulate(x, g, shift, scale):
    return _rms_norm(x, g) * (1.0 + scale) + shift


def _swiglu(h, w_gu, w_down):
    a, b = jnp.split(h @ w_gu, 2, axis=-1)
    return (jax.nn.silu(a) * b) @ w_down


def _rope(x, pos):
    d = x.shape[-1]
    half = d // 2
    inv = jnp.power(ROPE_THETA, -jnp.arange(half, dtype=jnp.float32) * 2.0 / d)
    ang = pos.astype(jnp.float32)[:, None] * inv[None, :]
    cos = jnp.cos(ang)[:, None, :]
    sin = jnp.sin(ang)[:, None, :]
    xf = x.astype(jnp.float32)
    x1, x2 = xf[..., :half], xf[..., half:]
    return jnp.concatenate([x1 * cos - x2 * sin, x2 * cos + x1 * sin], axis=-1).astype(x.dtype)


def _causal_conv(x, buf, w):
    T = x.shape[1]
    xp = jnp.concatenate([buf.astype(x.dtype), x], axis=1)
    y = xp[:, 0:T] * w[0]
    for j in range(1, DN_CONV):
        y = y + xp[:, j:j + T] * w[j]
    return y, xp[:, T:]


def _gated_delta_chunked(q, k, v, g, beta, S0):
    B, T, H, dk = q.shape
    dv = v.shape[-1]
    C = min(DN_CHUNK, T)
    pad = (-T) % C
    Tp = T + pad
    N = Tp // C

    def chunks(a):
        a = jnp.pad(a, [(0, 0), (0, pad)] + [(0, 0)] * (a.ndim - 2))
        a = a.reshape((B, N, C) + a.shape[2:])
        return jnp.moveaxis(a, 2, 3)

    q, k, v, g, beta = chunks(q), chunks(k), chunks(v), chunks(g), chunks(beta)
    gc = jnp.cumsum(g, axis=-1)
    idx = jnp.arange(C)
    incl = idx[:, None] >= idx[None, :]
    strict = idx[:, None] > idx[None, :]
    decay = jnp.exp(jnp.where(incl, gc[..., :, None] - gc[..., None, :], -jnp.inf))
    kb = k * beta[..., None]
    A = jnp.where(strict, jnp.einsum('bnhid,bnhjd->bnhij', kb, k) * decay, 0.0)
    eye = jnp.eye(C, dtype=jnp.float32)
    Tinv = lax.linalg.triangular_solve(eye + A, jnp.broadcast_to(eye, A.shape),
                                       left_side=True, lower=True, unit_diagonal=True)
    u = jnp.einsum('bnhij,bnhjd->bnhid', Tinv, v * beta[..., None])
    w = jnp.einsum('bnhij,bnhjd->bnhid', Tinv, kb * jnp.exp(gc)[..., None])
    qk = jnp.where(incl, jnp.einsum('bnhid,bnhjd->bnhij', q, k) * decay, 0.0)
    qd = q * jnp.exp(gc)[..., None]
    g_last = gc[..., -1]
    kd = k * jnp.exp(g_last[..., None] - gc)[..., None]

    def step(S, xs):
        u_n, w_n, qk_n, qd_n, kd_n, gl_n = xs
        v_new = u_n - jnp.einsum('bhcd,bhde->bhce', w_n, S)
        o_n = jnp.einsum('bhcd,bhde->bhce', qd_n, S) + jnp.einsum('bhij,bhje->bhie', qk_n, v_new)
        S = S * jnp.exp(gl_n)[..., None, None] + jnp.einsum('bhcd,bhce->bhde', kd_n, v_new)
        return S, o_n

    xs = tuple(jnp.moveaxis(a, 1, 0) for a in (u, w, qk, qd, kd, g_last))
    S, o = lax.scan(step, S0, xs)
    o = jnp.transpose(o, (1, 0, 3, 2, 4)).reshape(B, Tp, H, dv)[:, :T]
    return o, S


def _deltanet(p_conv, p_z, p_b, p_a, conv_buf, S0, conv_w, A_log, dt_bias, norm_w):
    B, T, _ = p_conv.shape
    qkv, conv_new = _causal_conv(p_conv, conv_buf, conv_w)
    qkv = jax.nn.silu(qkv.astype(jnp.float32))
    q, k, v = jnp.split(qkv, [DN_QK, 2 * DN_QK], axis=-1)
    q = _l2norm(q.reshape(B, T, DN_HEADS, DN_DK)) * (DN_DK ** -0.5)
    k = _l2norm(k.reshape(B, T, DN_HEADS, DN_DK))
    v = v.reshape(B, T, DN_HEADS, DN_DV)
    beta = jax.nn.sigmoid(p_b.astype(jnp.float32))
    g = -jnp.exp(A_log.astype(jnp.float32)) * jax.nn.softplus(
        p_a.astype(jnp.float32) + dt_bias.astype(jnp.float32))
    o, S = _gated_delta_chunked(q, k, v, g, beta, S0.astype(jnp.float32))
    z = p_z.reshape(B, T, DN_HEADS, DN_DV).astype(jnp.float32)
    o = _rms_norm(o, norm_w) * jax.nn.silu(z)
    return o.reshape(B, T, DN_V).astype(p_conv.dtype), S, conv_new


def _sink_softmax(s, mask, sink):
    s = jnp.where(mask, s, -jnp.inf)
    m = jnp.maximum(jnp.max(s, axis=-1, keepdims=True), sink)
    p = jnp.exp(s - m)
    return p / (jnp.sum(p, axis=-1, keepdims=True) + jnp.exp(sink - m))


def _swa_banded(q, k, v, sinks):
    B, T = q.shape[0], q.shape[1]
    nb = T // WINDOW
    qb = q.reshape(B, nb, WINDOW, SWA_KV_HEADS, SWA_GROUP, SWA_DH).astype(jnp.float32)
    kb = k.reshape(B, nb, WINDOW, SWA_KV_HEADS, SWA_DH).astype(jnp.float32)
    vb = v.reshape(B, nb, WINDOW, SWA_KV_HEADS, SWA_DH).astype(jnp.float32)
    shift = ((0, 0), (1, 0), (0, 0), (0, 0), (0, 0))
    kk = jnp.concatenate([jnp.pad(kb, shift)[:, :-1], kb], axis=2)
    vv = jnp.concatenate([jnp.pad(vb, shift)[:, :-1], vb], axis=2)
    s = jnp.einsum('bnqhgd,bnkhd->bnhgqk', qb, kk) * (SWA_DH ** -0.5)
    qi = jnp.arange(WINDOW)[:, None] + WINDOW
    kj = jnp.arange(2 * WINDOW)[None, :]
    dist = qi - kj
    kglob = jnp.arange(nb)[:, None, None] * WINDOW - WINDOW + kj[None]
    mask = (dist >= 0) & (dist <= WINDOW) & (kglob >= 0)
    sink = sinks.astype(jnp.float32).reshape(1, 1, SWA_KV_HEADS, SWA_GROUP, 1, 1)
    p = _sink_softmax(s, mask[None, :, None, None], sink)
    o = jnp.einsum('bnhgqk,bnkhd->bnqhgd', p, vv)
    return o.reshape(B, T, SWA_Q).astype(q.dtype)


def _swa_step(q, k, v, k_buf, v_buf, sinks):
    B, T = q.shape[0], q.shape[1]
    W0 = k_buf.shape[1]
    kk = jnp.concatenate([k_buf.astype(k.dtype), k], axis=1)
    vv = jnp.concatenate([v_buf.astype(v.dtype), v], axis=1)
    qpos = PAST_LEN + jnp.arange(T)
    kpos = PAST_LEN - W0 + jnp.arange(W0 + T)
    dist = qpos[:, None] - kpos[None, :]
    mask = (dist >= 0) & (dist <= WINDOW)
    qg = q.reshape(B, T, SWA_KV_HEADS, SWA_GROUP, SWA_DH).astype(jnp.float32)
    s = jnp.einsum('bqhgd,bkhd->bhgqk', qg, kk.astype(jnp.float32)) * (SWA_DH ** -0.5)
    sink = sinks.astype(jnp.float32).reshape(1, SWA_KV_HEADS, SWA_GROUP, 1, 1)
    p = _sink_softmax(s, mask, sink)
    o = jnp.einsum('bhgqk,bkhd->bqhgd', p, vv.astype(jnp.float32))
    return o.reshape(B, T, SWA_Q).astype(q.dtype), kk[:, T:], vv[:, T:]


def _layer(x, c, pos, conv_buf, S0, k_buf, v_buf, prm):
    (w_ada, b_ada, g_ffn1, w_ffn1_gu, w_ffn1_down, g_mix, w_in, dn_conv_w, dn_A_log, dn_dt_bias,
     dn_norm_w, swa_q_norm, swa_k_norm, swa_sinks, w_out, g_ffn2, w_ffn2_gu, w_ffn2_down) = prm
    B, T, _ = x.shape
    m = _adaln(c, w_ada, b_ada)
    h = _modulate(x, g_ffn1, m[:, 0], m[:, 1])
    x = x + 0.5 * m[:, 2] * _swiglu(h, w_ffn1_gu, w_ffn1_down)
    h = _modulate(x, g_mix, m[:, 3], m[:, 4])
    p_conv, p_z, p_b, p_a, p_q, p_k, p_v = jnp.split(h @ w_in, SPLIT_POINTS, axis=-1)
    dn_out, S_new, conv_new = _deltanet(p_conv, p_z, p_b, p_a, conv_buf, S0, dn_conv_w,
                                        dn_A_log, dn_dt_bias, dn_norm_w)
    q = _rope(_rms_norm(p_q.reshape(B, T, SWA_HEADS, SWA_DH), swa_q_norm), pos)
    k = _rope(_rms_norm(p_k.reshape(B, T, SWA_KV_HEADS, SWA_DH), swa_k_norm), pos)
    v = p_v.reshape(B, T, SWA_KV_HEADS, SWA_DH)
    if k_buf is None:
        swa_out = _swa_banded(q, k, v, swa_sinks)
        nkeep = min(WINDOW, T)
        k_new, v_new = k[:, T - nkeep:], v[:, T - nkeep:]
    else:
        swa_out, k_new, v_new = _swa_step(q, k, v, k_buf, v_buf, swa_sinks)
    mix = jnp.concatenate([dn_out, swa_out], axis=-1) @ w_out
    x = x + m[:, 5] * mix
    h = _modulate(x, g_ffn2, m[:, 6], m[:, 7])
    x = x + 0.5 * m[:, 8] * _swiglu(h, w_ffn2_gu, w_ffn2_down)
    return x, conv_new, S_new, k_new, v_new


def setup_inputs(seed: int = 0) -> dict:
    key = jax.random.key(seed)
    ks = jax.random.split(key, 32)
    f32 = jnp.float32
    L, D = DEPTH, D_MODEL
    swa_buf = min(WINDOW, PAST_LEN)

    def nrm(k, shape, s):
        return jax.random.normal(k, shape, f32) * s

    def gain(k, shape):
        return 1.0 + 0.02 * jax.random.normal(k, shape, f32)

    A = jax.random.uniform(ks[20], (L, DN_HEADS), f32, 1.0, 16.0)
    dt = jnp.exp(jax.random.uniform(ks[21], (L, DN_HEADS), f32, math.log(1e-3), math.log(1e-1)))
    return {
        'x_prompt': nrm(ks[0], (BATCH, SEQ, D), 1.0),
        'x_sample': nrm(ks[1], (DEC_BATCH, DEC_SEQ, D), 1.0),
        'c_prompt': nrm(ks[2], (BATCH, D), 1.0),
        'c_sample': nrm(ks[3], (DEC_BATCH, D), 1.0),
        'state_dn_conv': nrm(ks[4], (L, DEC_BATCH, DN_CONV - 1, DN_CONV_CH), 1.0),
        'state_dn_S': nrm(ks[5], (L, DEC_BATCH, DN_HEADS, DN_DK, DN_DV), DN_DK ** -0.5),
        'cache_swa_k': nrm(ks[6], (L, DEC_BATCH, swa_buf, SWA_KV_HEADS, SWA_DH), 1.0),
        'cache_swa_v': nrm(ks[7], (L, DEC_BATCH, swa_buf, SWA_KV_HEADS, SWA_DH), 1.0),
        'w_ada': nrm(ks[8], (L, D, N_MOD * D), 0.5 * D ** -0.5),
        'b_ada': nrm(ks[9], (L, N_MOD * D), 0.01),
        'g_ffn1': gain(ks[10], (L, D)),
        'w_ffn1_gu': nrm(ks[11], (L, D, 2 * D_FF), D ** -0.5),
        'w_ffn1_down': nrm(ks[12], (L, D_FF, D), D_FF ** -0.5),
        'g_mix': gain(ks[13], (L, D)),
        'w_in': nrm(ks[14], (L, D, IN_DIM), D ** -0.5),
        'dn_conv_w': nrm(ks[15], (L, DN_CONV, DN_CONV_CH), DN_CONV ** -0.5),
        'dn_A_log': jnp.log(A),
        'dn_dt_bias': dt + jnp.log(-jnp.expm1(-dt)),
        'dn_norm_w': gain(ks[16], (L, DN_DV)),
        'swa_q_norm': gain(ks[17], (L, SWA_DH)),
        'swa_k_norm': gain(ks[18], (L, SWA_DH)),
        'swa_sinks': nrm(ks[19], (L, SWA_HEADS), 0.5),
        'w_out': nrm(ks[22], (L, MIX_DIM, D), MIX_DIM ** -0.5),
        'g_ffn2': gain(ks[23], (L, D)),
        'w_ffn2_gu': nrm(ks[24], (L, D, 2 * D_FF), D ** -0.5),
        'w_ffn2_down': nrm(ks[25], (L, D_FF, D), D_FF ** -0.5),
    }


def reference(x_prompt, x_sample, c_prompt, c_sample, state_dn_conv, state_dn_S, cache_swa_k,
              cache_swa_v, w_ada, b_ada, g_ffn1, w_ffn1_gu, w_ffn1_down, g_mix, w_in, dn_conv_w,
              dn_A_log, dn_dt_bias, dn_norm_w, swa_q_norm, swa_k_norm, swa_sinks, w_out, g_ffn2,
              w_ffn2_gu, w_ffn2_down):
    Bp, Tp = x_prompt.shape[0], x_prompt.shape[1]
    pos_p = jnp.arange(Tp)
    pos_s = PAST_LEN + jnp.arange(x_sample.shape[1])
    yp, ys = x_prompt, x_sample
    p_conv, p_S, p_k, p_v = [], [], [], []
    s_conv, s_S, s_k, s_v = [], [], [], []
    for l in range(DEPTH):
        prm = (w_ada[l], b_ada[l], g_ffn1[l], w_ffn1_gu[l], w_ffn1_down[l], g_mix[l], w_in[l],
               dn_conv_w[l], dn_A_log[l], dn_dt_bias[l], dn_norm_w[l], swa_q_norm[l],
               swa_k_norm[l], swa_sinks[l], w_out[l], g_ffn2[l], w_ffn2_gu[l], w_ffn2_down[l])
        conv0 = jnp.zeros((Bp, DN_CONV - 1, DN_CONV_CH), x_prompt.dtype)
        S0 = jnp.zeros((Bp, DN_HEADS, DN_DK, DN_DV), jnp.float32)
        yp, cv, S, kn, vn = _layer(yp, c_prompt, pos_p, conv0, S0, None, None, prm)
        p_conv.append(cv); p_S.append(S); p_k.append(kn); p_v.append(vn)
        ys, cv, S, kn, vn = _layer(ys, c_sample, pos_s, state_dn_conv[l], state_dn_S[l],
                                   cache_swa_k[l], cache_swa_v[l], prm)
        s_conv.append(cv); s_S.append(S); s_k.append(kn); s_v.append(vn)
    return (yp, ys, jnp.stack(p_conv), jnp.stack(p_S), jnp.stack(p_k), jnp.stack(p_v),
            jnp.stack(s_conv), jnp.stack(s_S), jnp.stack(s_k), jnp.stack(s_v))
```

```python
import math
import os
from contextlib import ExitStack

import numpy as np
import ml_dtypes

import concourse.bass as bass
import concourse.mybir as mybir
from concourse.bass_utils import run_bass_kernel_spmd

F32 = mybir.dt.float32
BF16 = mybir.dt.bfloat16
AF = mybir.ActivationFunctionType
ALU = mybir.AluOpType
AX = mybir.AxisListType

NCORES = 8
D = 1024
KD = D // 128
SEQ = 4096
NS = 16
DEPTH = 2
DFF = 2816
NJ = DFF // 128
NMOD = 9
EPS = 1e-6
DN_H = 4
DK = 128
DV = 128
CONV_CH = 1536
NCC = CONV_CH // 128
SW_H = 8
SW_KV = 2
DH = 64
IN_DIM = 2824
C_Z = 1536
C_BA = 2048
C_Q = 2056
C_K = 2568
C_V = 2696
TT = 512
PAST_LEN = 16384
NEG = -30000.0
SKIP = set(os.environ.get('KSKIP', '').split(','))
DNCUT = int(os.environ.get('KDNCUT', '99'))


class Buf:
    __slots__ = ("name", "lw", "rd", "excl")

    def __init__(self, name, excl=False):
        self.name = name
        self.lw = None
        self.rd = []
        self.excl = excl


class Prog:
    ENGS = ("pe", "act", "dve", "pool", "sp")

    def __init__(self, nc, es):
        self.nc = nc
        self.eng = {"pe": nc.tensor, "act": nc.scalar, "dve": nc.vector,
                    "pool": nc.gpsimd, "sp": nc.sync}
        self.ops = []
        self.csem = {e: es.enter_context(nc.semaphore("c_" + e)) for e in self.ENGS}
        self.es = es
        self.last = {}
        self.lastd = {}
        self.pending_dma = []

    def _deps(self, r, w):
        deps = {}

        def add(o, raw):
            deps[o] = deps.get(o, False) or raw

        for b in r:
            if b.lw is not None:
                add(b.lw, True)
            if b.excl:
                for o in b.rd:
                    add(o, False)
        for b in w:
            if b.lw is not None:
                add(b.lw, False)
            for o in b.rd:
                add(o, False)
        return deps

    def _upd(self, oid, r, w):
        for b in w:
            b.lw = oid
            b.rd = []
        for b in r:
            if b in w:
                continue
            if b.excl:
                b.rd = [oid]
            else:
                b.rd.append(oid)

    def op(self, eng, fn, r=(), w=()):
        deps = self._deps(r, w)
        oid = len(self.ops)
        self.ops.append((eng, fn, deps, "c", None))
        self._upd(oid, r, w)
        self.last[eng] = oid
        return oid

    def barrier(self):
        deps = {o: True for o in (set(self.last.values()) | set(self.pending_dma))}
        self.pending_dma = []
        for e in self.ENGS:
            self.ops.append((e, None, dict(deps), "b", None))

    def dma(self, eng, out, in_, r=(), w=(), sem="g"):
        deps = self._deps(r, w)
        oid = len(self.ops)
        self.ops.append((eng, (out, in_), deps, "d", sem))
        self._upd(oid, r, w)
        self.lastd[sem] = oid
        self.pending_dma.append(oid)
        return oid

    DUR = {"pe": 0.12, "act": 0.45, "dve": 0.45, "pool": 0.6, "sp": 0.1}
    LAT = 1.2
    DMA_LAT = 6.0
    WINDOW = int(os.environ.get("KWIN", "1"))

    def schedule(self, ops):
        if self.WINDOW <= 1:
            return ops
        n = len(ops)
        order = []
        seg_start = 0
        i = 0
        segs = []
        while i < n:
            if ops[i][3] == "b":
                j = i
                while j < n and ops[j][3] == "b":
                    j += 1
                segs.append((seg_start, i, j))
                seg_start = j
                i = j
            else:
                i += 1
        segs.append((seg_start, n, n))
        fin = [0.0] * n
        for (a, b, c) in segs:
            self._sched_segment(ops, a, b, order, fin)
            order.extend(range(b, c))
            if c > b:
                t = max([fin[x] for x in order[-400:]] + [0.0])
                for x in range(b, c):
                    fin[x] = t
        remap = {old: new for new, old in enumerate(order)}
        segdeps = {}
        for (a, b, c) in segs:
            lastc = {}
            full = {}
            for x in range(a, b):
                if ops[x][3] == "d":
                    full[remap[x]] = True
                elif ops[x][3] == "c":
                    e_ = ops[x][0]
                    if e_ not in lastc or remap[x] > lastc[e_]:
                        lastc[e_] = remap[x]
            for v in lastc.values():
                full[v] = True
            for x in range(b, c):
                segdeps[x] = full
        out = []
        for old in order:
            eng, fn, deps, kind, dsem = ops[old]
            if kind == "b":
                out.append((eng, fn, segdeps[old], kind, dsem))
            else:
                out.append((eng, fn, {remap[d]: r for d, r in deps.items()}, kind, dsem))
        return out

    def _sched_segment(self, ops, a, b, order, fin):
        if b <= a:
            return
        W = self.WINDOW
        queues = {e: [] for e in self.ENGS}
        for i in range(a, b):
            queues[ops[i][0]].append(i)
        head = {e: 0 for e in self.ENGS}
        done = set()
        placed_before = a
        efree = {e: max([fin[x] for x in order[-50:]] + [0.0]) for e in self.ENGS}
        remaining = b - a
        taken = {e: set() for e in self.ENGS}
        pos_in_q = {i: k for k, i in enumerate(queues["pe"])}
        glue = None
        while remaining > 0:
            best = None
            for e in self.ENGS:
                q = queues[e]
                h = head[e]
                cnt = 0
                k = h
                Wm = W
                if e == "pe" and glue is not None:
                    k = pos_in_q[glue]
                    Wm = 1
                while k < len(q) and cnt < Wm:
                    i = q[k]
                    k += 1
                    if i in taken[e]:
                        continue
                    cnt += 1
                    ok = True
                    rdy = efree[e]
                    for d in ops[i][2]:
                        if d >= a and d not in done:
                            ok = False
                            break
                        if ops[d][0] == e and ops[d][3] != "d":
                            t = fin[d]
                        else:
                            t = fin[d] + self.LAT
                        if t > rdy:
                            rdy = t
                    if not ok:
                        continue
                    key = (rdy, i)
                    if best is None or key < best[0]:
                        best = (key, e, i)
                    if rdy <= efree[e]:
                        break
            if best is None and glue is not None:
                glue = None
                continue
            assert best is not None, "scheduler stuck"
            (rdy, _), e, i = best
            dur = self.DUR[e]
            fin[i] = rdy + (self.DMA_LAT if ops[i][3] == "d" else dur)
            efree[e] = rdy + dur
            order.append(i)
            done.add(i)
            taken[e].add(i)
            if e == "pe":
                glue = None
                kq = pos_in_q[i] + 1
                qpe = queues["pe"]
                if kq < len(qpe) and i in ops[qpe[kq]][2] and qpe[kq] not in taken["pe"]:
                    glue = qpe[kq]
            q = queues[e]
            while head[e] < len(q) and q[head[e]] in taken[e]:
                head[e] += 1
            remaining -= 1

    NPOOL = {"sp": 24, "pool": 12, "act": 4}

    def emit(self):
        ops = self.schedule(self.ops)
        self.ops = ops
        n = len(ops)
        marked = [False] * n
        for i in range(n):
            eng, _, deps, kind, _ = ops[i]
            for d, raw in deps.items():
                de, _, _, dk, _ = ops[d]
                if dk == "d":
                    continue
                if de == eng and eng == "pe":
                    continue
                marked[d] = True
        pools = {q: [self.es.enter_context(self.nc.semaphore("dq_%s%d" % (q, j))) for j in range(k)]
                 for q, k in self.NPOOL.items()}
        rr = {q: 0 for q in pools}
        pcount = {q: [0] * len(pools[q]) for q in pools}
        ev = [None] * n
        slot_of = [None] * n
        ccnt = {e: 0 for e in self.ENGS}
        for i in range(n):
            eng, _, _, kind, _ = ops[i]
            if kind == "d":
                j = rr[eng] % len(pools[eng])
                rr[eng] += 1
                slot_of[i] = (j, pcount[eng][j])
                pcount[eng][j] += 16
                ev[i] = (("d", eng, j), pcount[eng][j])
            elif kind == "c" and marked[i]:
                ccnt[eng] += 1
                ev[i] = (("c", eng), ccnt[eng])
        waited = {e: {} for e in self.ENGS}
        nwait = 0

        def semof(key):
            return self.csem[key[1]] if key[0] == "c" else pools[key[1]][key[2]]

        for i in range(n):
            eng, fn, deps, kind, _ = ops[i]
            E = self.eng[eng]
            need = {}
            for d, raw in deps.items():
                de = ops[d][0]
                if ops[d][3] == "c" and de == eng and eng == "pe":
                    continue
                if ev[d] is None:
                    continue
                key, val = ev[d]
                if need.get(key, 0) < val:
                    need[key] = val
            if kind == "d":
                j, prev = slot_of[i]
                if prev > 0:
                    key = ("d", eng, j)
                    if need.get(key, 0) < prev:
                        need[key] = prev
            for key, val in need.items():
                if waited[eng].get(key, 0) >= val:
                    continue
                waited[eng][key] = val
                E.wait_ge(semof(key), val)
                nwait += 1
            if kind == "b":
                continue
            if kind == "d":
                out, in_ = fn
                E.dma_start(out=out, in_=in_).then_inc(pools[eng][slot_of[i][0]], 16)
            else:
                ins = fn(E)
                if marked[i]:
                    ins.then_inc(self.csem[eng], 1)
        for q in pools:
            for j, c in enumerate(pcount[q]):
                if c > 0 and waited["sp"].get(("d", q, j), 0) < c:
                    self.eng["sp"].wait_ge(pools[q][j], c)
        self.stats = dict(n_ops=n, n_wait=nwait, n_marked=sum(marked))


def bc(ap, dims):
    return bass.AP(ap.tensor, ap.offset, [list(ap.ap[0])] + [list(d) for d in dims])


NAR = 51072
O_X0, O_X1, O_H, O_G, O_R, O_TMP = 33792, 37888, 41984, 44032, 49664, 50176
O_WIN, O_WOUT = 0, 11296
NCF = 1088
NCB = 2304
NBC = 1168


class Builder:
    def __init__(self, seq=SEQ, depth=DEPTH, stop=None):
        self.seq = seq
        self.depth = depth
        self.stop = stop
        self.ntile = seq // TT
        self.nblk = seq // 128
        self._rr = 0
        self.rr_set = list(range(8))

    def view(self, off, n32, dtype=F32, shape=None):
        v = self.AR[:, off:off + n32]
        if dtype == BF16:
            v = v.bitcast(BF16)
        if shape is not None:
            names = " ".join("a%d" % i for i in range(len(shape)))
            kw = {"a%d" % i: shape[i] for i in range(len(shape))}
            v = v.rearrange("p (%s) -> p %s" % (names, names), **kw)
        return v

    def mk(self, name, shape, dtype=F32):
        n = 1
        for x in shape:
            n *= x
        n32 = (n + 1) // 2 if dtype == BF16 else n
        n32 = (n32 + 7) // 8 * 8
        for seg in self.segs:
            if seg[1] - seg[0] >= n32:
                off = seg[0]
                seg[0] += n32
                return self.view(off, n32 if dtype == F32 else n32, dtype, None)[:, 0:n].rearrange(
                    "p (%s) -> p %s" % (" ".join("a%d" % i for i in range(len(shape))),
                                        " ".join("a%d" % i for i in range(len(shape)))),
                    **{"a%d" % i: shape[i] for i in range(len(shape))}), Buf(name)
        raise RuntimeError("arena full allocating " + name)

    def bank(self):
        i = self.rr_set[self._rr % len(self.rr_set)]
        self._rr += 1
        return self.PB[i], self.bPB[i]

    def build(self):
        nc = bass.Bass("TRN2", target_bir_lowering=False)
        self.nc = nc
        L = self.depth
        seq = self.seq
        dt = lambda name, shape, kind, d=F32: nc.dram_tensor(name, list(shape), d, kind=kind).ap()
        I, O = "ExternalInput", "ExternalOutput"
        self.xT = dt("xT", [D, seq], I)
        self.xsT = dt("xsT", [D, NS], I)
        self.csT = dt("csT", [D, 1 + NS], I)
        self.w_ada = dt("w_ada", [L, D, NMOD * D], I)
        self.smallp = dt("smallp", [L, 128, 144], I)
        self.bcp = dt("bcp", [L, 128, NBC], I)
        self.w_gu = [dt("w_ffn1_gu", [L, D, 2 * DFF], I), dt("w_ffn2_gu", [L, D, 2 * DFF], I)]
        self.w_dn = [dt("w_ffn1_down", [L, DFF, D], I), dt("w_ffn2_down", [L, DFF, D], I)]
        self.w_in = dt("w_in", [L, D, IN_DIM], I)
        self.w_out = dt("w_out", [L, D, D], I)
        self.constf = dt("constf", [128, NCF], I)
        self.constb = dt("constb", [128, NCB], I)
        self.ropeT = dt("ropeT", [128, seq // 128, 64], I)
        self.yT = dt("yT", [D, seq], O)
        self.ysT = dt("ysT", [D, NS], O)
        self.o_conv_p = dt("o_conv_p", [L, CONV_CH, 3], O)
        self.o_S_p = dt("o_S_p", [L, DN_H, DK, DV], O)
        self.o_k_p = dt("o_k_p", [L, 128, SW_KV * DH], O)
        self.o_v_p = dt("o_v_p", [L, 128, SW_KV * DH], O)
        self.st_conv = dt("st_conv", [L, NS, 3, CONV_CH], I)
        self.st_S = dt("st_S", [L, NS, DN_H, DK, DV], I)
        self.st_k = dt("st_k", [L, NS, 128, SW_KV * DH], I)
        self.st_v = dt("st_v", [L, NS, 128, SW_KV * DH], I)
        self.o_conv_s = dt("o_conv_s", [L, NS, 3, CONV_CH], O)
        self.o_S_s = dt("o_S_s", [L, NS, DN_H, DK, DV], O)
        self.o_k_s = dt("o_k_s", [L, NS, 128, SW_KV * DH], O)
        self.o_v_s = dt("o_v_s", [L, NS, 128, SW_KV * DH], O)

        with ExitStack() as es:
            self.es = es
            P = Prog(nc, es)
            self.P = P
            sbt = lambda name, shape, d=F32: es.enter_context(nc.sbuf_tensor(name, list(shape), d))
            self.AR = sbt("ARENA", [128, NAR])
            self.W = self.view(0, 33792, BF16)
            self.X = [self.view(O_X0, 4096, F32, [KD, TT]), self.view(O_X1, 4096, F32, [KD, TT])]
            self.H = self.view(O_H, 2048, BF16, [KD, TT])
            self.G = self.view(O_G, 5632, BF16, [NJ, TT])
            self.R = self.view(O_R, 512)
            self.TMP = self.view(O_TMP, 512)
            self.XS = sbt("XS", [128, KD, NS])
            self.HS = sbt("HS", [128, KD, NS], BF16)
            self.GS = sbt("GS", [128, NJ, NS], BF16)
            self.CS = sbt("CS", [128, KD, 1 + NS])
            self.SCS = sbt("SCS", [128, KD, 1 + NS], BF16)
            self.MODS = sbt("MODS", [128, NMOD * KD, 1 + NS])
            self.MA = lambda s_: self.MODS[:, (3 * s_ + 1) * KD:(3 * s_ + 2) * KD, :]
            self.MG = lambda s_: self.MODS[:, (3 * s_ + 2) * KD:(3 * s_ + 3) * KD, :]
            self.SMALL = sbt("SMALL", [128, 144])
            self.ONES = sbt("ONES", [128, 128], BF16)
            self.EPSD = sbt("EPSD", [128, 8])
            self.PB = [es.enter_context(nc.psum_tensor("pb%d" % i, [128, 512], F32)) for i in range(8)]
            self.bW = Buf("W")
            self.bX = [Buf("X0"), Buf("X1")]
            self.bH, self.bG, self.bR, self.bTMP = Buf("H"), Buf("G"), Buf("R"), Buf("TMP")
            self.bXS, self.bHS, self.bGS = Buf("XS"), Buf("HS"), Buf("GS")
            self.bCS, self.bSCS, self.bMODS = Buf("CS"), Buf("SCS"), Buf("MODS")
            self.bSMALL, self.bONES = Buf("SMALL"), Buf("ONES")
            self.bPB = [Buf("pb%d" % i, excl=True) for i in range(8)]
            self.bYT = [Buf("yT%d" % i) for i in range(self.ntile)]

            self.prologue()
            first = True
            for l in range(L):
                self.adaln(l)
                self.ffn(l, 0, src_is_input=first)
                first = False
                if self.stop == "ffn1":
                    break
                P.barrier()
                self.mixer(l)
                P.barrier()
                if self.stop == "mix":
                    break
                self.ffn(l, 2, src_is_input=False)
                P.barrier()
            self.epilogue()
            P.emit()
            self.stats = P.stats
        return nc

    def prologue(self):
        P = self.P
        P.dma("sp", self.XS[:], self.xsT.rearrange("(k p) m -> p k m", p=128), w=[self.bXS], sem="ld")
        P.dma("sp", self.CS[:], self.csT.rearrange("(k p) m -> p k m", p=128), w=[self.bCS], sem="ld")
        P.op("dve", lambda e: e.memset(self.ONES[:], 1.0), w=[self.bONES])
        for i, v in enumerate((D * EPS, DK * EPS, DH * EPS, 1.0, EPS, 0.0, 0.0, 0.0)):
            P.op("dve", lambda e, i=i, v=v: e.memset(self.EPSD[:, i:i + 1], v), w=[self.bONES])
        P.op("act", lambda e: e.activation(out=self.SCS[:], in_=self.CS[:], func=AF.Silu),
             r=[self.bCS], w=[self.bSCS])

    def epilogue(self):
        P = self.P
        P.dma("sp", self.ysT.rearrange("(k p) m -> p k m", p=128), self.XS[:], r=[self.bXS], sem="st")

    def adaln(self, l):
        P = self.P
        P.dma("sp", self.SMALL[:], self.smallp[l], w=[self.bSMALL], sem="ld")
        wv = self.w_ada[l].rearrange("(k p) n -> p k n", p=128)
        NP = 8
        PC = NMOD * D // NP
        CPP = PC // 128
        stage = [self.W[:, 0:KD * PC].rearrange("p (k n) -> p k n", k=KD),
                 self.W[:, KD * PC:2 * KD * PC].rearrange("p (k n) -> p k n", k=KD)]
        bst = [Buf("ada0"), Buf("ada1")]
        for q in range(NP):
            s = q % 2
            deps_w = [bst[s]] + ([self.bW] if q < 2 else [])
            P.dma("pool", stage[s], wv[:, :, q * PC:(q + 1) * PC], w=deps_w, sem="wl")
            pb, bpb = self.bank()
            pv = pb[:, 0:CPP * 17].rearrange("p (c m) -> p c m", c=CPP)
            for c in range(CPP):
                for k in range(KD):
                    P.op("pe", lambda e, s=s, c=c, k=k, pv=pv: e.matmul(
                        pv[:, c, :], stage[s][:, k, c * 128:(c + 1) * 128], self.SCS[:, k, :],
                        start=(k == 0), stop=(k == KD - 1)),
                        r=[bst[s], self.bSCS], w=[bpb])
            bias = bc(self.SMALL[:, q * CPP:q * CPP + 1], [[1, CPP], [0, 17]])
            P.op("dve", lambda e, q=q, pv=pv, bias=bias: e.tensor_tensor(
                out=self.MODS[:, q * CPP:(q + 1) * CPP, :], in0=pv, in1=bias, op=ALU.add),
                r=[bpb, self.bSMALL], w=[self.bMODS])
        self.bW.lw = None
        self.bW.rd = list(bst[0].rd) + list(bst[1].rd)
        for s in range(3):
            gcol = bc(self.SMALL[:, 72 + 8 * s:73 + 8 * s], [[1, KD], [0, 1 + NS]])
            P.op("dve", lambda e, s=s: e.tensor_scalar(
                out=self.MA(s), in0=self.MA(s), scalar1=1.0, scalar2=math.sqrt(D), op0=ALU.add, op1=ALU.mult),
                r=[self.bMODS], w=[self.bMODS])
            P.op("dve", lambda e, s=s, gcol=gcol: e.tensor_tensor(
                out=self.MA(s), in0=self.MA(s), in1=gcol, op=ALU.mult),
                r=[self.bMODS, self.bSMALL], w=[self.bMODS])
            P.op("dve", lambda e, s=s: e.tensor_scalar(
                out=self.MG(s), in0=self.MG(s), scalar1=(1.0 if s == 1 else 0.5), scalar2=None, op0=ALU.mult),
                r=[self.bMODS], w=[self.bMODS])

    def load_ffn_weights(self, l, which):
        P = self.P
        wi = 0 if which == 0 else 1
        gu = self.W[:, 0:KD * 2 * DFF].rearrange("p (k n) -> p k n", k=KD)
        dn = self.W[:, KD * 2 * DFF:KD * 2 * DFF + NJ * D].rearrange("p (j n) -> p j n", j=NJ)
        guv = self.w_gu[wi][l].rearrange("(k p) n -> p k n", p=128)
        dnv = self.w_dn[wi][l].rearrange("(j p) n -> p j n", p=128)
        self.bWgu = [Buf("Wgu%d" % i) for i in range(NJ // 2)]
        self.bWdn = [Buf("Wdn%d" % i) for i in range(NJ // 2)]
        for jb in range(NJ // 2):
            c0 = jb * 256
            for off in (0, DFF):
                P.dma("pool", gu[:, :, off + c0:off + c0 + 256], guv[:, :, off + c0:off + c0 + 256],
                      w=[self.bWgu[jb]] + ([self.bW] if jb == 0 else []), sem="wl")
        for jb in range(NJ // 2):
            P.dma("pool", dn[:, 2 * jb:2 * jb + 2, :], dnv[:, 2 * jb:2 * jb + 2, :], w=[self.bWdn[jb]], sem="wl")
        return gu, dn

    def norm_mod(self, xt, bx, T, s, hs, bh, sample, sqv, bsq):
        P = self.P
        ssq, bssq = self.bank()
        P.op("act", lambda e: e.activation(out=sqv, in_=xt, func=AF.Square), r=[bx], w=[bsq])
        for k in range(KD):
            P.op("pe", lambda e, k=k: e.matmul(ssq[:, 0:T], self.ONES[:], sqv[:, k, :],
                                               start=(k == 0), stop=(k == KD - 1)),
                 r=[self.bONES, bsq], w=[bssq])
        P.op("act", lambda e: e.activation(out=self.R[:, 0:T], in_=ssq[:, 0:T], func=AF.Ln, bias=self.EPSD[:, 0:1]),
             r=[bssq, self.bONES], w=[self.bR])
        P.op("act", lambda e: e.activation(out=self.R[:, 0:T], in_=self.R[:, 0:T], func=AF.Exp, scale=-0.5),
             r=[self.bR], w=[self.bR])
        shift = self.MODS[:, (3 * s) * KD:(3 * s + 1) * KD, :]
        if not sample:
            for k in range(KD):
                P.op("dve", lambda e, k=k: e.tensor_tensor(out=self.TMP[:, 0:T], in0=xt[:, k, :], in1=self.R[:, 0:T],
                                                           op=ALU.mult), r=[bx, self.bR], w=[self.bTMP])
                P.op("act", lambda e, k=k: e.activation(out=hs[:, k, :], in_=self.TMP[:, 0:T], func=AF.Identity,
                                                        scale=self.MA(s)[:, k, 0:1], bias=shift[:, k, 0:1]),
                     r=[self.bTMP, self.bMODS], w=[bh])
        else:
            rb = bc(self.R[:, 0:1], [[0, KD], [1, T]])
            tv = self.TMP[:, 0:KD * T].rearrange("p (k t) -> p k t", k=KD)
            P.op("dve", lambda e: e.tensor_tensor(out=tv, in0=xt, in1=rb, op=ALU.mult), r=[bx, self.bR], w=[self.bTMP])
            P.op("dve", lambda e: e.tensor_tensor(out=tv, in0=tv, in1=self.MA(s)[:, :, 1:1 + NS], op=ALU.mult),
                 r=[self.bTMP, self.bMODS], w=[self.bTMP])
            P.op("dve", lambda e: e.tensor_tensor(out=hs, in0=tv, in1=shift[:, :, 1:1 + NS], op=ALU.add),
                 r=[self.bTMP, self.bMODS], w=[bh])

    def ffn_tile(self, xt, bx, T, s, gu, dn, sample):
        P = self.P
        hs = self.HS[:] if sample else self.H[:, :, 0:T]
        bh = self.bHS if sample else self.bH
        gs = self.GS[:] if sample else self.G[:, :, 0:T]
        bg = self.bGS if sample else self.bG
        self.norm_mod(xt, bx, T, s, hs, bh, sample, gs[:, 0:KD, :], bg)
        for j in range(NJ):
            pa, bpa = self.bank()
            pbk, bpb = self.bank()
            for k in range(KD):
                P.op("pe", lambda e, j=j, k=k, pa=pa: e.matmul(pa[:, 0:T], gu[:, k, j * 128:(j + 1) * 128], hs[:, k, :],
                                                               start=(k == 0), stop=(k == KD - 1)),
                     r=[self.bWgu[j // 2], bh], w=[bpa])
            for k in range(KD):
                P.op("pe", lambda e, j=j, k=k, pbk=pbk: e.matmul(pbk[:, 0:T], gu[:, k, DFF + j * 128:DFF + (j + 1) * 128],
                                                                 hs[:, k, :], start=(k == 0), stop=(k == KD - 1)),
                     r=[self.bWgu[j // 2], bh], w=[bpb])
            P.op("act", lambda e, pa=pa: e.activation(out=self.TMP[:, 0:T], in_=pa[:, 0:T], func=AF.Silu),
                 r=[bpa], w=[self.bTMP])
            P.op("dve", lambda e, j=j, pbk=pbk: e.tensor_tensor(out=gs[:, j, :], in0=self.TMP[:, 0:T], in1=pbk[:, 0:T],
                                                                op=ALU.mult), r=[self.bTMP, bpb], w=[bg])
        for d in range(KD):
            py, bpy = self.bank()
            for j in range(NJ):
                P.op("pe", lambda e, d=d, j=j, py=py: e.matmul(py[:, 0:T], dn[:, j, d * 128:(d + 1) * 128], gs[:, j, :],
                                                               start=(j == 0), stop=(j == NJ - 1)),
                     r=[self.bWdn[j // 2], bg], w=[bpy])
            self.resid(xt, bx, T, s, d, py, bpy, sample)

    def resid(self, xt, bx, T, s, d, py, bpy, sample):
        P = self.P
        if not sample:
            P.op("dve", lambda e: e.scalar_tensor_tensor(
                out=xt[:, d, :], in0=py[:, 0:T], scalar=self.MG(s)[:, d, 0:1], in1=xt[:, d, :],
                op0=ALU.mult, op1=ALU.add), r=[bpy, self.bMODS, bx], w=[bx])
        else:
            P.op("dve", lambda e: e.tensor_tensor(out=self.TMP[:, 0:T], in0=py[:, 0:T],
                                                  in1=self.MG(s)[:, d, 1:1 + NS], op=ALU.mult),
                 r=[bpy, self.bMODS], w=[self.bTMP])
            P.op("dve", lambda e: e.tensor_tensor(out=xt[:, d, :], in0=xt[:, d, :], in1=self.TMP[:, 0:T],
                                                  op=ALU.add), r=[self.bTMP, bx], w=[bx])

    def ffn(self, l, s, src_is_input):
        P = self.P
        gu, dn = self.load_ffn_weights(l, s)
        src = (self.xT if src_is_input else self.yT).rearrange("(k p) t -> p k t", p=128)
        dst = self.yT.rearrange("(k p) t -> p k t", p=128)

        def load(i):
            rd = [] if src_is_input else [self.bYT[i]]
            P.dma("sp", self.X[i % 2][:], src[:, :, i * TT:(i + 1) * TT], r=rd, w=[self.bX[i % 2]], sem="xl")

        load(0)
        for i in range(self.ntile):
            if i + 1 < self.ntile:
                load(i + 1)
            self.ffn_tile(self.X[i % 2][:], self.bX[i % 2], TT, s, gu, dn, sample=False)
            P.dma("sp", dst[:, :, i * TT:(i + 1) * TT], self.X[i % 2][:], r=[self.bX[i % 2]], w=[self.bYT[i]], sem="xs")
        self.ffn_tile(self.XS[:], self.bXS, NS, s, gu, dn, sample=True)

    def mixer(self, l):
        P = self.P
        self.segs = [[15392, 33792], [O_X1, O_X1 + 4096], [O_G, O_G + 5632]]
        mk = self.mk
        self.rr_set = [3, 4, 5, 6, 7]
        WIN = self.view(O_WIN, 11296, BF16, [KD, IN_DIM])
        WOUT = self.view(O_WOUT, 4096, BF16, [KD, D])
        bWI, bWO = Buf("WIN"), Buf("WOUT")
        X0, bX0 = self.X[0], self.bX[0]
        H, bH = self.H, self.bH
        R, bR, TMP, bTMP = self.R, self.bR, self.TMP, self.bTMP
        CF, bCF = mk("CF", [NCF])
        CB, bCB = mk("CB", [NCB], BF16)
        BCP, bBCP = mk("BCP", [NBC])
        SC, bSC = mk("SC", [96])
        bSCc, bSCs = Buf("SCconst"), Buf("SCswa")
        segs_sample = [list(x) for x in self.segs]
        XP, bXP = mk("XP", [NCC, 3 + TT])
        QT, bQT = mk("QT", [DN_H, TT], BF16)
        KT, bKT = mk("KT", [DN_H, TT], BF16)
        VT, bVT = mk("VT", [DN_H, TT], BF16)
        CAT, bCAT = mk("CAT", [KD, TT], BF16)
        ROPE, bROPE = mk("ROPE", [4, 64])
        SQB, bSQB = mk("SQB", [TT], BF16)
        GU, bGU = mk("GU", [DN_H, 128])
        ET, bET = mk("ET", [DN_H, 128])
        EDS, bEDS = mk("EDS", [DN_H, 128])
        EGs = [mk("EGROW0", [DN_H, 128], BF16), mk("EGROW1", [DN_H, 128], BF16)]
        ONB, bONB = mk("ONB", [DN_H, 128], BF16)
        SC1, bSC1 = mk("SC1", [64])
        S32, bS32 = mk("S32", [DN_H, 128])
        bfn = ("DT", "DS", "A", "B", "QKT", "A0", "B0", "X2", "Y2", "T", "U", "NWR", "NWU",
               "KBG", "KDD", "VB", "NWT", "QDT", "VN", "SBF", "ON2", "ZG", "ZG1", "DT1", "DS1")
        bft = {}
        for nm in bfn:
            bft[nm] = mk(nm, [DN_H, 128], BF16)
        QS, bQS = mk("QS", [SW_H, DH])
        KS, bKS = mk("KS", [SW_KV, DH])
        KR, bKR = mk("KR", [SW_KV, DH])
        VR, bVR = mk("VR", [SW_KV, DH])
        QF, bQF = mk("QF", [4, 2, DH], BF16)
        KF, bKF = mk("KF", [SW_KV, DH], BF16)
        QTS, bQTS = mk("QTS", [4, 128], BF16)
        KTS, bKTS = mk("KTS", [2, 128], BF16)
        VE, bVE = mk("VE", [2, SW_KV, DH + 1], BF16)
        PT, bPT = {}, {}
        for hk in range(2):
            for kb in range(2):
                PT[hk, kb], bPT[hk, kb] = mk("PT%d%d" % (hk, kb), [4, 128], BF16)
        OSB, bOSB = mk("OSB", [SW_H, DH], BF16)

        UT = CF[:, 0:128]
        ONESF = CF[:, 128:256]
        NEGMT = bc(CF[:, 256:257], [[0, DN_H], [1, 128]])
        NEGMS = bc(CF[:, 384:385], [[0, DN_H], [1, 128]])
        h4 = lambda v: bc(v[:, 0:1], [[0, DN_H], [1, 128]])
        MD8 = h4(CB[:, 1152:1280])
        MLm = [h4(CB[:, 1280 + li * 128:1408 + li * 128]) for li in range(4)]
        MUm = [h4(CB[:, 1792 + li * 128:1920 + li * 128]) for li in range(4)]
        IDB4 = h4(CB[:, 0:128])
        IDB = CB[:, 0:128]
        MASKC = CB[:, 128:640]
        MASKP = CB[:, 640:1152]
        ALOG, DTB, SINK = BCP[:, 0:4], BCP[:, 4:8], BCP[:, 8:16]
        NORMW4, QNW8, KNW2 = BCP[:, 16:528], BCP[:, 528:1040], BCP[:, 1040:1168]
        NEGA, ESINK = SC[:, 64:68], SC[:, 68:76]
        sc = lambda i, n=4: SC[:, i * 4:i * 4 + n]
        SCV = [[sc(i) for i in range(16)], [SC1[:, i * 4:i * 4 + 4] for i in range(16)]]
        bSCp = [bSC, bSC1]
        DTs = [bft["DT"], bft["DT1"]]
        DSs = [bft["DS"], bft["DS1"]]
        SSQ10, RS10, DEN8 = SC[:, 76:86], SC[:, 86:96], SC[:, 96 - 8:96]
        DEN8, bDEN = mk("DEN8", [16])

        winv = self.w_in[l].rearrange("(k p) n -> p k n", p=128)
        woutv = self.w_out[l].rearrange("(k p) n -> p k n", p=128)
        for k in range(KD):
            P.dma("pool", WIN[:, k, :], winv[:, k, :], w=[bWI], sem="wl")
        P.dma("pool", WOUT[:, 0:4, :], woutv[:, 0:4, :], w=[bWO], sem="wl")
        P.dma("pool", WOUT[:, 4:8, :], woutv[:, 4:8, :], w=[bWO], sem="wl")
        P.dma("pool", CB, self.constb, w=[bCB], sem="wl")
        P.dma("sp", CF, self.constf, w=[bCF], sem="ld")
        P.dma("sp", BCP, self.bcp[l], w=[bBCP], sem="ld")
        P.op("act", lambda e: e.activation(out=NEGA, in_=ALOG, func=AF.Exp), r=[bBCP], w=[bSCc])
        P.op("dve", lambda e: e.tensor_scalar(out=NEGA, in0=NEGA, scalar1=-1.0, scalar2=None, op0=ALU.mult),
             r=[bSCc], w=[bSCc])
        P.op("act", lambda e: e.activation(out=ESINK, in_=SINK, func=AF.Exp), r=[bBCP], w=[bSCc])
        P.op("dve", lambda e: e.tensor_scalar(out=KNW2, in0=KNW2, scalar1=float(math.sqrt(DH)), scalar2=None,
                                              op0=ALU.mult), r=[bBCP], w=[bBCP])
        NORMWB, bNWB = mk("NORMWB", [DN_H * DV], BF16)
        P.op("act", lambda e: e.activation(out=NORMWB, in_=NORMW4, func=AF.Copy), r=[bBCP], w=[bNWB])
        P.op("dve", lambda e: e.memset(S32, 0.0), w=[bS32])
        P.op("dve", lambda e: e.memset(bft["SBF"][0], 0.0), w=[bft["SBF"][1]])
        P.op("dve", lambda e: e.memset(XP[:, :, 0:3], 0.0), w=[bXP])
        P.op("dve", lambda e: e.memset(VE[:, :, :, DH:DH + 1], 1.0), w=[bVE])

        src = self.yT.rearrange("(k p) t -> p k t", p=128)
        def tile(i):
            P.dma("sp", X0, src[:, :, i * TT:(i + 1) * TT], r=[self.bYT[i]], w=[bX0], sem="xl")
            P.dma("sp", ROPE, self.ropeT[:, i * 4:(i + 1) * 4, :], w=[bROPE], sem="xl")
            self.norm_mod(X0, bX0, TT, 1, H, bH, False, CAT, bCAT)
            for c in range(NCC):
                pp, bpp = self.bank()
                for k in range(KD):
                    P.op("pe", lambda e, c=c, k=k, pp=pp: e.matmul(pp[:, :], WIN[:, k, c * 128:(c + 1) * 128], H[:, k, :],
                                                                   start=(k == 0), stop=(k == KD - 1)),
                         r=[bWI, bH], w=[bpp])
                P.op("act", lambda e, c=c, pp=pp: e.activation(out=XP[:, c, 3:3 + TT], in_=pp[:, :], func=AF.Copy),
                     r=[bpp], w=[bXP])
                cw = lambda j, c=c: self.SMALL[:, 96 + c * 4 + j:97 + c * 4 + j]
                acc, bacc = (TMP, bTMP) if c % 2 == 0 else (R, bR)
                ceng = "dve"
                P.op(ceng, lambda e, c=c, cw=cw, acc=acc: e.tensor_scalar(out=acc, in0=XP[:, c, 0:TT], scalar1=cw(0),
                                                                           scalar2=None, op0=ALU.mult),
                     r=[bXP, self.bSMALL], w=[bacc])
                for j in range(1, 4):
                    P.op(ceng, lambda e, c=c, j=j, cw=cw, acc=acc: e.scalar_tensor_tensor(
                        out=acc, in0=XP[:, c, j:j + TT], scalar=cw(j), in1=acc, op0=ALU.mult, op1=ALU.add),
                        r=[bXP, self.bSMALL, bacc], w=[bacc])
                dst, bdst = ((QT, bQT), (KT, bKT), (VT, bVT))[c // 4]
                P.op("act", lambda e, c=c, dst=dst, acc=acc: e.activation(out=dst[:, c % 4, :], in_=acc, func=AF.Silu),
                     r=[bacc], w=[bdst])
            for c in range(8):
                isq = c < 4
                h = c % 4
                dst, bdst = (QT, bQT) if isq else (KT, bKT)
                P.op("act", lambda e, dst=dst, h=h: e.activation(out=SQB, in_=dst[:, h, :], func=AF.Square),
                     r=[bdst], w=[bSQB])
                ps, bps = self.bank()
                P.op("pe", lambda e, ps=ps: e.matmul(ps[:, :], self.ONES[:], SQB, start=True, stop=True),
                     r=[self.bONES, bSQB], w=[bps])
                P.op("act", lambda e, ps=ps, isq=isq: e.activation(
                    out=TMP, in_=ps[:, :], func=AF.Ln, scale=(float(DK) if isq else 1.0),
                    bias=(self.EPSD[:, 1:2] if isq else self.EPSD[:, 4:5])), r=[bps, self.bONES], w=[bTMP])
                P.op("act", lambda e: e.activation(out=TMP, in_=TMP, func=AF.Exp, scale=-0.5), r=[bTMP], w=[bTMP])
                P.op("dve", lambda e, dst=dst, h=h: e.tensor_tensor(out=dst[:, h, :], in0=dst[:, h, :], in1=TMP, op=ALU.mult),
                     r=[bdst, bTMP], w=[bdst])
            if i == self.ntile - 1 and "outs" not in SKIP:
                P.dma("sp", self.o_conv_p[l].rearrange("(c p) j -> p c j", p=128), XP[:, :, TT:TT + 3],
                      r=[bXP], sem="st")
            P.op("act", lambda e: e.activation(out=XP[:, :, 0:3], in_=XP[:, :, TT:TT + 3], func=AF.Copy),
                 r=[bXP], w=[bXP])

            PZ, bPZ = self.PB[0], self.bPB[0]
            PQ, bPQ = self.PB[1], self.bPB[1]
            PKV, bPKV = self.PB[2], self.bPB[2]
            ZGs = [bft["ZG"], bft["ZG1"]]

            def proj(bb):
                tsl_ = slice(bb * 128, bb * 128 + 128)
                for (pt, bpt, c0, ncol, o0) in ((PZ, bPZ, C_Z, 512, 0), (PQ, bPQ, C_Q, 512, 0),
                                                (PKV, bPKV, C_K, 256, 0), (PKV, bPKV, C_BA, 8, 256)):
                    for k in range(KD):
                        P.op("pe", lambda e, pt=pt, c0=c0, ncol=ncol, o0=o0, k=k: e.matmul(
                            pt[:, o0:o0 + ncol], H[:, k, tsl_], WIN[:, k, c0:c0 + ncol],
                            start=(k == 0), stop=(k == KD - 1)), r=[bWI, bH], w=[bpt])
                if "dn" not in SKIP:
                    ZG0, bZG0 = ZGs[(4 * i + bb) % 2]
                    P.op("act", lambda e: e.activation(out=ZG0.rearrange("p h i -> p (h i)"), in_=PZ[:, :], func=AF.Silu),
                         r=[bPZ], w=[bZG0])
                    P.op("pool", lambda e: e.tensor_tensor(out=ZG0.rearrange("p h i -> p (h i)"),
                                                           in0=ZG0.rearrange("p h i -> p (h i)"), in1=NORMWB, op=ALU.mult),
                         r=[bZG0, bNWB], w=[bZG0])

            def dn_pre(bb):
                par_ = (4 * i + bb) % 2
                (BETA, XA_, AXV, E1, L1, G4, GC, NGC, BGC, LNB, EGC, EKD, BEG, EGL, RS4, SSQ4) = SCV[par_]
                bSCx = bSCp[par_]
                DT_, bDT = DTs[par_]
                DS_, bDS = DSs[par_]
                EGROW, bEGROW = EGs[par_]
                PB_ = PKV[:, 256:260]
                PA_ = PKV[:, 260:264]
                P.op("act", lambda e: e.activation(out=BETA, in_=PB_, func=AF.Exp, scale=-1.0), r=[bPKV], w=[bSCx])
                P.op("dve", lambda e: e.tensor_scalar(out=BETA, in0=BETA, scalar1=1.0, scalar2=None, op0=ALU.add),
                     r=[bSCx], w=[bSCx])
                P.op("act", lambda e: e.activation(out=LNB, in_=BETA, func=AF.Ln), r=[bSCx], w=[bSCx])
                P.op("dve", lambda e: e.reciprocal(out=BETA, in_=BETA), r=[bSCx], w=[bSCx])
                P.op("dve", lambda e: e.tensor_tensor(out=XA_, in0=PA_, in1=DTB, op=ALU.add), r=[bPKV, bBCP], w=[bSCx])
                P.op("act", lambda e: e.activation(out=AXV, in_=XA_, func=AF.Abs), r=[bSCx], w=[bSCx])
                P.op("act", lambda e: e.activation(out=E1, in_=AXV, func=AF.Exp, scale=-1.0), r=[bSCx], w=[bSCx])
                P.op("act", lambda e: e.activation(out=L1, in_=E1, func=AF.Ln, bias=self.EPSD[:, 3:4]),
                     r=[bSCx, self.bONES], w=[bSCx])
                P.op("dve", lambda e: e.scalar_tensor_tensor(out=G4, in0=XA_, scalar=0.0, in1=L1, op0=ALU.max,
                                                             op1=ALU.add), r=[bSCx], w=[bSCx])
                P.op("dve", lambda e: e.tensor_tensor(out=G4, in0=G4, in1=NEGA, op=ALU.mult), r=[bSCx, bSCc], w=[bSCx])
                pg, bpg = self.bank()
                P.op("pe", lambda e, pg=pg: e.matmul(pg[:, 0:4], UT, G4, start=True, stop=True), r=[bCF, bSCx], w=[bpg])
                P.op("dve", lambda e, pg=pg: e.tensor_copy(out=GC, in_=pg[:, 0:4]), r=[bpg], w=[bSCx])
                P.op("dve", lambda e: e.tensor_tensor(out=GU, in0=bc(UT, [[0, DN_H], [1, 128]]),
                                                      in1=bc(G4, [[1, DN_H], [0, 128]]), op=ALU.mult),
                     r=[bCF, bSCx], w=[bGU])
                pr, bpr = self.bank()
                prv = pr[:, :].rearrange("p (h i) -> p h i", h=DN_H)
                P.op("pe", lambda e, pr=pr: e.matmul(pr[:, :], ONESF, GU.rearrange("p h i -> p (h i)"),
                                                     start=True, stop=True), r=[bCF, bGU], w=[bpr])
                P.op("dve", lambda e: e.tensor_scalar(out=NGC, in0=GC, scalar1=-1.0, scalar2=None, op0=ALU.mult),
                     r=[bSCx], w=[bSCx])
                P.op("dve", lambda e: e.tensor_tensor(out=BGC, in0=GC, in1=LNB, op=ALU.subtract), r=[bSCx], w=[bSCx])
                P.op("act", lambda e: e.activation(out=EGC, in_=GC, func=AF.Exp), r=[bSCx], w=[bSCx])
                P.op("dve", lambda e: e.tensor_tensor(out=BEG, in0=BETA, in1=EGC, op=ALU.mult), r=[bSCx], w=[bSCx])
                P.op("act", lambda e, prv=prv: e.activation(out=EGL, in_=prv[:, :, 127], func=AF.Exp), r=[bpr], w=[bSCx])
                P.op("dve", lambda e, prv=prv: e.tensor_tensor(out=EKD, in0=prv[:, :, 127], in1=GC, op=ALU.subtract),
                     r=[bpr, bSCx], w=[bSCx])
                P.op("act", lambda e: e.activation(out=EKD, in_=EKD, func=AF.Exp), r=[bSCx], w=[bSCx])
                P.op("act", lambda e, pr=pr: e.activation(out=EGROW.rearrange("p h i -> p (h i)"), in_=pr[:, :],
                                                          func=AF.Exp), r=[bpr], w=[bEGROW])
                P.op("dve", lambda e, prv=prv: e.tensor_tensor(out=ET, in0=prv, in1=NEGMT, op=ALU.add),
                     r=[bpr, bCF], w=[bET])
                P.op("dve", lambda e, prv=prv: e.scalar_tensor_tensor(
                    out=EDS, in0=prv, scalar=-1.0, in1=NEGMS, op0=ALU.mult, op1=ALU.add), r=[bpr, bCF], w=[bEDS])
                for h in range(DN_H):
                    P.op("act", lambda e, h=h: e.activation(out=DT_[:, h, :], in_=ET[:, h, :], func=AF.Exp,
                                                            bias=NGC[:, h:h + 1]), r=[bET, bSCx], w=[bDT])
                    P.op("act", lambda e, h=h: e.activation(out=DS_[:, h, :], in_=EDS[:, h, :], func=AF.Exp,
                                                            bias=BGC[:, h:h + 1]), r=[bEDS, bSCx], w=[bDS])

            def block(b):
                n = 4 * i + b
                t0 = b * 128
                tsl = slice(t0, t0 + 128)
                par = n % 2
                if b == 0:
                    proj(0)
                    if "dn" not in SKIP:
                        dn_pre(0)
                fl = lambda v: v.rearrange("p h i -> p (h i)")
                v3 = lambda v: v.rearrange("p (h i) -> p h i", h=DN_H)
                hb = lambda v: bc(v, [[1, DN_H], [0, 128]])

                def dn_gen():
                    (BETA, XA_, AXV, E1, L1, G4, GC, NGC, BGC, LNB, EGC, EKD, BEG, EGL, RS4, SSQ4) = SCV[n % 2]
                    bSCx = bSCp[n % 2]
                    DT_, bDT = DTs[n % 2]
                    DS_, bDS = DSs[n % 2]
                    EGROW, bEGROW = EGs[n % 2]
                    pkk, bpkk = self.bank()
                    pkq, bpkq = self.bank()
                    for h in range(DN_H):
                        P.op("pe", lambda e, h=h, pkk=pkk: e.matmul(pkk[:, h * 128:(h + 1) * 128], KT[:, h, tsl], KT[:, h, tsl],
                                                                    start=True, stop=True), r=[bKT], w=[bpkk])
                    for h in range(DN_H):
                        P.op("pe", lambda e, h=h, pkq=pkq: e.matmul(pkq[:, h * 128:(h + 1) * 128], KT[:, h, tsl], QT[:, h, tsl],
                                                                    start=True, stop=True), r=[bKT, bQT], w=[bpkq])
                    A_, bA = bft["A"]
                    B_, bB = bft["B"]
                    QKT, bQKT = bft["QKT"]
                    fl = lambda v: v.rearrange("p h i -> p (h i)")
                    P.op("dve", lambda e, pkk=pkk: e.tensor_tensor(out=fl(A_), in0=pkk[:, :], in1=fl(DS_), op=ALU.mult),
                         r=[bpkk, bDS], w=[bA])
                    P.op("dve", lambda e, pkq=pkq: e.tensor_tensor(out=fl(QKT), in0=pkq[:, :], in1=fl(DT_), op=ALU.mult),
                         r=[bpkq, bDT], w=[bQKT])
                    yield
                    ptb, bptb = self.bank()
                    ptbv = ptb[:, 0:256].bitcast(BF16)
                    for h in range(DN_H):
                        P.op("pe", lambda e, h=h, ptbv=ptbv: e.transpose(ptbv[:, h * 128:(h + 1) * 128], A_[:, h, :], IDB),
                             r=[bA, bCB], w=[bptb])
                    P.op("act", lambda e, ptbv=ptbv: e.activation(out=fl(B_), in_=ptbv, func=AF.Copy), r=[bptb], w=[bB])
                    if DNCUT <= 3:
                        return
                    ptk, bptk = self.bank()
                    ptkv = ptk[:, 0:256].bitcast(BF16)
                    ptv, bptv = self.bank()
                    ptvv = ptv[:, 0:256].bitcast(BF16)
                    for h in range(DN_H):
                        P.op("pe", lambda e, h=h, ptkv=ptkv: e.transpose(ptkv[:, h * 128:(h + 1) * 128], KT[:, h, tsl], IDB),
                             r=[bKT, bCB], w=[bptk])
                    for h in range(DN_H):
                        P.op("pe", lambda e, h=h, ptvv=ptvv: e.transpose(ptvv[:, h * 128:(h + 1) * 128], VT[:, h, tsl], IDB),
                             r=[bVT, bCB], w=[bptv])
                    KBG, bKBG = bft["KBG"]
                    KDD, bKDD = bft["KDD"]
                    VB, bVB = bft["VB"]
                    hb = lambda v: bc(v, [[1, DN_H], [0, 128]])
                    v3 = lambda v: v.rearrange("p (h i) -> p h i", h=DN_H)
                    P.op("dve", lambda e, ptkv=ptkv: e.tensor_tensor(out=KBG, in0=v3(ptkv), in1=hb(BEG), op=ALU.mult),
                         r=[bptk, bSCx], w=[bKBG])
                    P.op("dve", lambda e, ptkv=ptkv: e.tensor_tensor(out=KDD, in0=v3(ptkv), in1=hb(EKD), op=ALU.mult),
                         r=[bptk, bSCx], w=[bKDD])
                    P.op("dve", lambda e, ptvv=ptvv: e.tensor_tensor(out=VB, in0=v3(ptvv), in1=hb(BETA), op=ALU.mult),
                         r=[bptv, bSCx], w=[bVB])
                    QDT, bQDT = bft["QDT"]
                    P.op("dve", lambda e: e.tensor_tensor(out=QDT, in0=QT[:, :, tsl], in1=EGROW, op=ALU.mult),
                         r=[bQT, bEGROW], w=[bQDT])
                    yield
                    A0, bA0 = bft["A0"]
                    B0, bB0 = bft["B0"]
                    X2, bX2 = bft["X2"]
                    Y2, bY2 = bft["Y2"]
                    T_, bT_ = bft["T"]
                    U_, bU_ = bft["U"]
                    NWR, bNWR = bft["NWR"]
                    NWU, bNWU = bft["NWU"]
                    P.op("pool", lambda e: e.tensor_tensor(out=A0, in0=A_, in1=MD8, op=ALU.mult), r=[bA, bCB], w=[bA0])
                    P.op("pool", lambda e: e.tensor_tensor(out=B0, in0=B_, in1=MD8, op=ALU.mult), r=[bB, bCB], w=[bB0])
                    P.op("dve", lambda e: e.scalar_tensor_tensor(out=T_, in0=A0, scalar=-1.0, in1=IDB4, op0=ALU.mult,
                                                                 op1=ALU.add), r=[bA0, bCB], w=[bT_])
                    P.op("dve", lambda e: e.scalar_tensor_tensor(out=U_, in0=B0, scalar=-1.0, in1=IDB4, op0=ALU.mult,
                                                                 op1=ALU.add), r=[bB0, bCB], w=[bU_])

                    def mm4(lhs, blhs, rhs, brhs, pre=None):
                        pm, bpm = self.bank()
                        for h in range(DN_H):
                            if pre is not None:
                                P.op("pe", lambda e, h=h, pm=pm: e.matmul(pm[:, h * 128:(h + 1) * 128], IDB, pre[0][:, h, :],
                                                                          start=True, stop=False), r=[bCB, pre[1]], w=[bpm])
                            P.op("pe", lambda e, h=h, pm=pm: e.matmul(pm[:, h * 128:(h + 1) * 128], lhs[:, h, :], rhs[:, h, :],
                                                                      start=(pre is None), stop=True), r=[blhs, brhs], w=[bpm])
                        return pm, bpm

                    def evac(eng, pm, bpm, dst, bdst, neg=False):
                        if eng == "act":
                            P.op("act", lambda e: e.activation(out=fl(dst), in_=pm[:, :], func=AF.Identity,
                                                               scale=(-1.0 if neg else 1.0)), r=[bpm], w=[bdst])
                        else:
                            P.op("dve", lambda e: e.tensor_scalar(out=fl(dst), in0=pm[:, :], scalar1=(-1.0 if neg else 1.0),
                                                                  scalar2=None, op0=ALU.mult), r=[bpm], w=[bdst])
                    yield
                    px_ = mm4(B0, bB0, A0, bA0)
                    py2 = mm4(A0, bA0, B0, bB0)
                    evac("act", *px_, X2, bX2)
                    evac("dve", *py2, Y2, bY2)
                    yield
                    pt_ = mm4(Y2, bY2, T_, bT_, pre=(T_, bT_))
                    pu_ = mm4(X2, bX2, U_, bU_, pre=(U_, bU_))
                    evac("act", *pt_, T_, bT_)
                    evac("dve", *pu_, U_, bU_)
                    yield
                    px_ = mm4(Y2, bY2, X2, bX2)
                    py2 = mm4(X2, bX2, Y2, bY2)
                    evac("act", *px_, A0, bA0)
                    evac("dve", *py2, B0, bB0)
                    yield
                    pt_ = mm4(B0, bB0, T_, bT_, pre=(T_, bT_))
                    pu_ = mm4(A0, bA0, U_, bU_, pre=(U_, bU_))
                    evac("act", *pt_, T_, bT_)
                    evac("dve", *pu_, U_, bU_)
                    yield
                    def evacm(pm, bpm, dst, bdst, mask):
                        P.op("dve", lambda e: e.scalar_tensor_tensor(out=dst, in0=v3(pm[:, :]), scalar=-1.0, in1=mask,
                                                                     op0=ALU.mult, op1=ALU.mult), r=[bpm, bCB], w=[bdst])
                    for li in range(4):
                        last = li == 3
                        pwu = mm4(A_, bA, U_, bU_)
                        if not last:
                            pwr = mm4(B_, bB, T_, bT_)
                            evacm(*pwr, NWR, bNWR, MLm[li])
                        evacm(*pwu, NWU, bNWU, MUm[li])
                        yield
                        pu_ = mm4(T_, bT_, NWU, bNWU, pre=(U_, bU_))
                        if not last:
                            pt_ = mm4(U_, bU_, NWR, bNWR, pre=(T_, bT_))
                        evac("dve", *pu_, U_, bU_)
                        if not last:
                            evac("act", *pt_, T_, bT_)
                        yield
                    PBF, bPBF = U_, bU_
                    if DNCUT <= 4:
                        return
                    yield
                    pw, bpw = self.bank()
                    for h in range(DN_H):
                        P.op("pe", lambda e, h=h, pw=pw: e.matmul(pw[:, h * 128:(h + 1) * 128], KBG[:, h, :], PBF[:, h, :],
                                                                  start=True, stop=True), r=[bKBG, bPBF], w=[bpw])
                    NWT, bNWT = bft["NWT"]
                    P.op("act", lambda e, pw=pw: e.activation(out=fl(NWT), in_=pw[:, :], func=AF.Identity, scale=-1.0),
                         r=[bpw], w=[bNWT])
                    SBF, bSBF = bft["SBF"]
                    VN, bVN = bft["VN"]
                    pvn, bpvn = self.bank()
                    for h in range(DN_H):
                        P.op("pe", lambda e, h=h, pvn=pvn: e.matmul(pvn[:, h * 128:(h + 1) * 128], PBF[:, h, :], VB[:, h, :],
                                                                    start=True, stop=False), r=[bPBF, bVB], w=[bpvn])
                        P.op("pe", lambda e, h=h, pvn=pvn: e.matmul(pvn[:, h * 128:(h + 1) * 128], NWT[:, h, :], SBF[:, h, :],
                                                                    start=False, stop=True), r=[bNWT, bSBF], w=[bpvn])
                    P.op("act", lambda e, pvn=pvn: e.activation(out=fl(VN), in_=pvn[:, :], func=AF.Copy), r=[bpvn], w=[bVN])
                    yield
                    po, bpo = self.bank()
                    for h in range(DN_H):
                        P.op("pe", lambda e, h=h, po=po: e.matmul(po[:, h * 128:(h + 1) * 128], QDT[:, h, :], SBF[:, h, :],
                                                                  start=True, stop=False), r=[bQDT, bSBF], w=[bpo])
                        P.op("pe", lambda e, h=h, po=po: e.matmul(po[:, h * 128:(h + 1) * 128], QKT[:, h, :], VN[:, h, :],
                                                                  start=False, stop=True), r=[bQKT, bVN], w=[bpo])
                    psu, bpsu = self.bank()
                    for h in range(DN_H):
                        P.op("pe", lambda e, h=h, psu=psu: e.matmul(psu[:, h * 128:(h + 1) * 128], KDD[:, h, :], VN[:, h, :],
                                                                    start=True, stop=True), r=[bKDD, bVN], w=[bpsu])
                    P.op("dve", lambda e: e.tensor_tensor(out=S32, in0=S32, in1=hb(EGL), op=ALU.mult),
                         r=[bS32, bSCx], w=[bS32])
                    P.op("dve", lambda e, psu=psu: e.tensor_tensor(out=fl(S32), in0=fl(S32), in1=psu[:, :], op=ALU.add),
                         r=[bS32, bpsu], w=[bS32])
                    P.op("act", lambda e: e.activation(out=fl(SBF), in_=fl(S32), func=AF.Copy), r=[bS32], w=[bSBF])
                    if DNCUT <= 6:
                        return
                    yield
                    ZG, bZG = ZGs[n % 2]
                    P.op("act", lambda e, po=po: e.activation(out=SQB, in_=po[:, :], func=AF.Square), r=[bpo], w=[bSQB])
                    P.op("dve", lambda e: e.tensor_reduce(out=SSQ4, in_=v3(SQB), axis=AX.X, op=ALU.add), r=[bSQB], w=[bSCx])
                    P.op("act", lambda e: e.activation(out=RS4, in_=SSQ4, func=AF.Ln, scale=1.0 / DV,
                                                       bias=self.EPSD[:, 4:5]), r=[bSCx, self.bONES], w=[bSCx])
                    P.op("act", lambda e: e.activation(out=RS4, in_=RS4, func=AF.Exp, scale=-0.5), r=[bSCx], w=[bSCx])
                    P.op("dve", lambda e, po=po: e.tensor_tensor(out=ONB, in0=v3(po[:, :]), in1=hb(RS4), op=ALU.mult),
                         r=[bpo, bSCx], w=[bONB])
                    ON2, bON2 = bft["ON2"]
                    P.op("dve", lambda e: e.tensor_tensor(out=ON2, in0=ONB, in1=ZG, op=ALU.mult), r=[bONB, bZG], w=[bON2])
                    yield
                    pto, bpto = self.bank()
                    ptov = pto[:, 0:256].bitcast(BF16)
                    for h in range(DN_H):
                        P.op("pe", lambda e, h=h, ptov=ptov: e.transpose(ptov[:, h * 128:(h + 1) * 128], ON2[:, h, :], IDB),
                             r=[bON2, bCB], w=[bpto])
                    P.op("act", lambda e, ptov=ptov: e.activation(out=CAT[:, 0:4, tsl], in_=v3(ptov), func=AF.Copy),
                         r=[bpto], w=[bCAT])

                def swa_gen():
                    v8 = lambda v: v.rearrange("p (h d) -> p h d", h=SW_H)
                    v2 = lambda v: v.rearrange("p (h d) -> p h d", h=SW_KV)
                    P.op("act", lambda e, PQ=PQ: e.activation(out=TMP, in_=PQ[:, :], func=AF.Square), r=[bPQ], w=[bTMP])
                    P.op("dve", lambda e: e.tensor_reduce(out=SSQ10[:, 0:8], in_=v8(TMP), axis=AX.X, op=ALU.add),
                         r=[bTMP], w=[bSCs])
                    P.op("act", lambda e, PKV=PKV: e.activation(out=R[:, 0:128], in_=PKV[:, 0:128], func=AF.Square),
                         r=[bPKV], w=[bR])
                    P.op("dve", lambda e: e.tensor_reduce(out=SSQ10[:, 8:10], in_=v2(R[:, 0:128]), axis=AX.X, op=ALU.add),
                         r=[bR], w=[bSCs])
                    P.op("act", lambda e: e.activation(out=RS10, in_=SSQ10, func=AF.Ln, bias=self.EPSD[:, 2:3]),
                         r=[bSCs, self.bONES], w=[bSCs])
                    P.op("act", lambda e: e.activation(out=RS10, in_=RS10, func=AF.Exp, scale=-0.5), r=[bSCs], w=[bSCs])
                    yield
                    P.op("dve", lambda e, PQ=PQ: e.tensor_tensor(out=QS, in0=v8(PQ[:, :]), in1=bc(RS10[:, 0:8], [[1, 8], [0, DH]]),
                                                                 op=ALU.mult), r=[bPQ, bSCs], w=[bQS])
                    P.op("dve", lambda e: e.tensor_tensor(out=QS, in0=QS, in1=v8(QNW8), op=ALU.mult), r=[bQS, bBCP], w=[bQS])
                    P.op("dve", lambda e, PKV=PKV: e.tensor_tensor(out=KS, in0=v2(PKV[:, 0:128]),
                                                                   in1=bc(RS10[:, 8:10], [[1, 2], [0, DH]]), op=ALU.mult),
                         r=[bPKV, bSCs], w=[bKS])
                    P.op("dve", lambda e: e.tensor_tensor(out=KS, in0=KS, in1=v2(KNW2), op=ALU.mult), r=[bKS, bBCP], w=[bKS])
                    yield
                    def rope(x1, x2, cosb, sinb, t1, t2, d1, d2, bsrc, bdst_):
                        P.op("dve", lambda e: e.tensor_tensor(out=t1, in0=x1, in1=cosb, op=ALU.mult),
                             r=[bsrc, bROPE], w=[bTMP])
                        P.op("dve", lambda e: e.tensor_tensor(out=t2, in0=x2, in1=sinb, op=ALU.mult),
                             r=[bsrc, bROPE], w=[bTMP])
                        P.op("dve", lambda e: e.tensor_tensor(out=d1, in0=t1, in1=t2, op=ALU.subtract),
                             r=[bTMP], w=[bdst_])
                        P.op("dve", lambda e: e.tensor_tensor(out=t1, in0=x2, in1=cosb, op=ALU.mult),
                             r=[bsrc, bROPE, bTMP], w=[bTMP])
                        P.op("dve", lambda e: e.tensor_tensor(out=t2, in0=x1, in1=sinb, op=ALU.mult),
                             r=[bsrc, bROPE, bTMP], w=[bTMP])
                        P.op("dve", lambda e: e.tensor_tensor(out=d2, in0=t1, in1=t2, op=ALU.add),
                             r=[bTMP], w=[bdst_])
                    q4 = QS.rearrange("p (a b) d -> p a b d", a=2)
                    rope(q4[:, :, :, 0:32], q4[:, :, :, 32:64],
                         bc(ROPE[:, b, 0:1], [[0, 2], [0, 4], [1, 32]]), bc(ROPE[:, b, 32:33], [[0, 2], [0, 4], [1, 32]]),
                         TMP[:, 0:256].rearrange("p (a b d) -> p a b d", a=2, b=4),
                         TMP[:, 256:512].rearrange("p (a b d) -> p a b d", a=2, b=4),
                         bc(QF[:, 0, 0, 0:1], [[DH, 2], [2 * DH, 4], [1, 32]]),
                         bc(QF[:, 0, 0, 32:33], [[DH, 2], [2 * DH, 4], [1, 32]]), bQS, bQF)
                    rope(KS[:, :, 0:32], KS[:, :, 32:64],
                         bc(ROPE[:, b, 0:1], [[0, 2], [1, 32]]), bc(ROPE[:, b, 32:33], [[0, 2], [1, 32]]),
                         TMP[:, 0:64].rearrange("p (h d) -> p h d", h=2), TMP[:, 256:320].rearrange("p (h d) -> p h d", h=2),
                         KR[:, :, 0:32], KR[:, :, 32:64], bKS, bKR)
                    yield
                    P.op("act", lambda e: e.activation(out=KF, in_=KR, func=AF.Copy), r=[bKR], w=[bKF])
                    P.op("act", lambda e, PKV=PKV: e.activation(out=VE[:, par, :, 0:DH], in_=v2(PKV[:, 128:256]), func=AF.Copy),
                         r=[bPKV], w=[bVE])
                    if n == self.nblk - 1 and "outs" not in SKIP:
                        P.op("act", lambda e, PKV=PKV: e.activation(out=VR, in_=v2(PKV[:, 128:256]), func=AF.Copy),
                             r=[bPKV], w=[bVR])
                        P.dma("sp", self.o_k_p[l], KR.rearrange("p h d -> p (h d)"), r=[bKR], sem="st")
                        P.dma("sp", self.o_v_p[l], VR.rearrange("p h d -> p (h d)"), r=[bVR], sem="st")
                    yield
                    ptq, bptq = self.bank()
                    ptqv = ptq[:, 0:320].bitcast(BF16)
                    for g in range(4):
                        qin = QF[:, g, :, :].rearrange("p a d -> p (a d)")
                        P.op("pe", lambda e, g=g, qin=qin, ptqv=ptqv: e.transpose(ptqv[:, g * 128:(g + 1) * 128], qin, IDB),
                             r=[bQF, bCB], w=[bptq])
                    P.op("pe", lambda e, ptqv=ptqv: e.transpose(ptqv[:, 512:640], KF.rearrange("p h d -> p (h d)"), IDB),
                         r=[bKF, bCB], w=[bptq])
                    P.op("act", lambda e, ptqv=ptqv: e.activation(out=QTS.rearrange("p g t -> p (g t)"), in_=ptqv[:, 0:512],
                                                                  func=AF.Copy), r=[bptq], w=[bQTS])
                    P.op("dve", lambda e, ptqv=ptqv: e.tensor_copy(out=KTS[:, par, :], in_=ptqv[:, 512:640]),
                         r=[bptq], w=[bKTS])
                    yield
                    kbs = ((1, par, MASKC),) if n == 0 else ((0, 1 - par, MASKP), (1, par, MASKC))
                    pos_, bpos_ = [], []
                    for hk in range(SW_KV):
                        psl = slice(hk * DH, (hk + 1) * DH)
                        for (kb, kpar, msk) in kbs:
                            pscr, bpscr = self.bank()
                            P.op("pe", lambda e, pscr=pscr, psl=psl, kpar=kpar: e.matmul(
                                pscr[:, :], KTS[psl, kpar, :], QTS[psl, :, :].rearrange("p g t -> p (g t)"),
                                start=True, stop=False), r=[bKTS, bQTS], w=[bpscr])
                            P.op("pe", lambda e, pscr=pscr, msk=msk: e.matmul(pscr[:, :], IDB, msk, start=False, stop=True),
                                 r=[bCB], w=[bpscr])
                            P.op("act", lambda e, pscr=pscr, hk=hk, kb=kb: e.activation(
                                out=PT[hk, kb].rearrange("p g t -> p (g t)"), in_=pscr[:, :], func=AF.Exp),
                                r=[bpscr], w=[bPT[hk, kb]])
                        yield
                        posb, bposb = self.bank()
                        pov = posb[:, 0:4 * (DH + 1)].rearrange("p (g d) -> p g d", g=4)
                        for g in range(4):
                            for ii, (kb, kpar, msk) in enumerate(kbs):
                                P.op("pe", lambda e, pov=pov, g=g, hk=hk, kb=kb, kpar=kpar, ii=ii: e.matmul(
                                    pov[:, g, :], PT[hk, kb][:, g, :], VE[:, kpar, hk, :],
                                    start=(ii == 0), stop=(ii == len(kbs) - 1)),
                                    r=[bPT[hk, kb], bVE], w=[bposb])
                        P.op("dve", lambda e, pov=pov, hk=hk: e.tensor_tensor(
                            out=DEN8[:, hk * 4:(hk + 1) * 4], in0=pov[:, :, DH], in1=ESINK[:, hk * 4:(hk + 1) * 4], op=ALU.add),
                            r=[bposb, bSCc], w=[bDEN])
                        P.op("dve", lambda e, hk=hk: e.reciprocal(out=DEN8[:, 8 + hk * 4:8 + (hk + 1) * 4],
                                                                  in_=DEN8[:, hk * 4:(hk + 1) * 4]), r=[bDEN], w=[bDEN])
                        P.op("dve", lambda e, pov=pov, hk=hk: e.tensor_tensor(
                            out=OSB[:, hk * 4:(hk + 1) * 4, :], in0=pov[:, :, 0:DH],
                            in1=bc(DEN8[:, 8 + hk * 4:9 + hk * 4], [[1, 4], [0, DH]]), op=ALU.mult),
                            r=[bposb, bDEN], w=[bOSB])
                    yield
                    ptos, bptos = self.bank()
                    ptosv = ptos[:, 0:256].bitcast(BF16)
                    osf = OSB.rearrange("p h d -> p (h d)")
                    for c in range(4):
                        P.op("pe", lambda e, c=c, ptosv=ptosv: e.transpose(ptosv[:, c * 128:(c + 1) * 128],
                                                                           osf[:, c * 128:(c + 1) * 128], IDB),
                             r=[bOSB, bCB], w=[bptos])
                    P.op("dve", lambda e, ptosv=ptosv: e.tensor_copy(out=CAT[:, 4:8, tsl], in_=v3(ptosv)),
                         r=[bptos], w=[bCAT])
                    yield
                    if b < 3:
                        proj(b + 1)
                        if "dn" not in SKIP:
                            dn_pre(b + 1)

                gens = []
                if "dn" not in SKIP:
                    gens.append(dn_gen())
                if "swa" not in SKIP:
                    gens.append(swa_gen())
                if os.environ.get("KSEQ", "0") == "1":
                    for g_ in gens:
                        for _ in g_:
                            pass
                else:
                    DNR = int(os.environ.get("KDNR", "2"))
                    step = 0
                    while gens:
                        for gi, g_ in enumerate(list(gens)):
                            reps = DNR if (gi == 0 and len(gens) > 1) else 1
                            for _ in range(reps):
                                try:
                                    next(g_)
                                except StopIteration:
                                    if g_ in gens:
                                        gens.remove(g_)
                                    break
            for b in range(4):
                block(b)
            for d in range(KD):
                py, bpy = self.bank()
                for c in range(KD):
                    P.op("pe", lambda e, d=d, c=c, py=py: e.matmul(py[:, :], WOUT[:, c, d * 128:(d + 1) * 128], CAT[:, c, :],
                                                                   start=(c == 0), stop=(c == KD - 1)),
                         r=[bWO, bCAT], w=[bpy])
                self.resid(X0, bX0, TT, 1, d, py, bpy, False)
            P.dma("sp", src[:, :, i * TT:(i + 1) * TT], X0, r=[bX0], w=[self.bYT[i]], sem="xs")
        for i in range(self.ntile):
            tile(i)
        if "outs" not in SKIP:
            P.dma("sp", self.o_S_p[l].rearrange("h k v -> k h v"), S32, r=[bS32], sem="st")
        P.barrier()
        if "samp" not in SKIP:
            self.sample_mixer(l, WIN, bWI, WOUT, bWO, CF, bCF, CB, bCB, BCP, bBCP, SC, bSC, segs_sample)
        self.rr_set = list(range(8))


    def sample_mixer(self, l, WIN, bWI, WOUT, bWO, CF, bCF, CB, bCB, BCP, bBCP, SC, bSC, segs):
        P = self.P
        self.segs = [list(x) for x in segs]
        mk = self.mk
        self.rr_set = [4, 5, 6, 7]
        PB, bPB = self.PB, self.bPB
        HS, bHS, XS, bXS = self.HS, self.bHS, self.XS, self.bXS
        p16 = slice(0, NS)
        IDB = CB[:, 0:128]
        ONESF = CF[:, 128:256]
        IDF = CF[:, 512:640]
        SEL = CF[:, 640:896]
        ROPS = CF[:, 896:960]
        ZEROF = CF[:, 960:1088]
        DTB, SINK = BCP[:, 4:8], BCP[:, 8:16]
        NORMW4, QNW8, KNW2 = BCP[:, 16:528], BCP[:, 528:1040], BCP[:, 1040:1168]
        NEGA, ESINK = SC[:, 64:68], SC[:, 68:76]
        SELB, bSELB = mk("SELB", [256], BF16)
        CSTK, bCSTK = mk("CSTK", [CONV_CH])
        CSTT, bCSTT = mk("CSTT", [NCC, 48])
        PCT, bPCT = mk("PCT", [NCC, NS])
        ACC, bACC = mk("ACC", [NCC, NS])
        T12, bT12 = mk("T12", [NCC, NS])
        PCTOK, bPCTOK = mk("PCTOK", [CONV_CH])
        SQS, bSQS = mk("SQS", [8, NS], BF16)
        RN, bRN = mk("RN", [8, NS])
        QKN, bQKN = mk("QKN", [8, NS])
        KM, bKM = mk("KM", [DN_H, NS, NS])
        QM, bQM = mk("QM", [DN_H, NS, NS])
        VTOK, bVTOK = mk("VTOK", [DN_H, DV])
        KTOK, bKTOK = mk("KTOK", [DN_H, DK])
        VNS, bVNS = mk("VNS", [DN_H, DV])
        T1, bT1 = mk("T1", [DN_H, DV])
        KMS = [mk("KMS%d" % i, [DN_H, DK]) for i in range(2)]
        S0 = [mk("S0%d" % i, [DN_H, DV]) for i in range(2)]
        SN = [mk("SN%d" % i, [DN_H, DV]) for i in range(2)]
        AD, bAD = mk("AD", [NS, DN_H])
        AB, bAB = mk("AB", [NS * DN_H])
        SCS_, bSCS_ = mk("SCS", [96])
        ZGS, bZGS = mk("ZGS", [DN_H, DV])
        ONS, bONS = mk("ONS", [DN_H, DV])
        ON2S, bON2S = mk("ON2S", [DN_H, DV], BF16)
        QSS, bQSS = mk("QSS", [SW_H, DH])
        KSS, bKSS = mk("KSS", [SW_KV, DH])
        KRS, bKRS = mk("KRS", [SW_KV, DH])
        VRS, bVRS = mk("VRS", [SW_KV, DH])
        QRS, bQRS = mk("QRS", [SW_H, DH])
        QFS, bQFS = mk("QFS", [4, 2, DH], BF16)
        QTSS, bQTSS = mk("QTSS", [4, NS], BF16)
        KC, bKC = mk("KC", [NS, 128], BF16)
        VCE, bVCE = mk("VCE", [NS, SW_KV, DH + 1], BF16)
        KCT = [mk("KCT%d" % i, [128], BF16) for i in range(2)]
        PTS, bPTS = mk("PTS", [NS, SW_H], BF16)
        PTM = [mk("PTM%d" % i, [SW_H, NS], BF16) for i in range(2)]
        PROD, bPROD = mk("PROD", [SW_H, DH])
        OSS, bOSS = mk("OSS", [SW_H, DH])
        OSBS, bOSBS = mk("OSBS", [SW_H, DH], BF16)
        CATS, bCATS = mk("CATS", [KD, NS], BF16)
        TMP, bTMP = self.TMP, self.bTMP
        sc = lambda i, n=4: SCS_[:, i * 4:i * 4 + n]
        BETA, XA_, AXV, E1, L1, G4, AEX, LNB, RS4, SSQ4 = [sc(i) for i in range(10)]
        SSQ10, RS10, SN8, PN8, DEN8, RDEN8 = (SCS_[:, 40:50], SCS_[:, 50:60], SCS_[:, 60:68], SCS_[:, 68:76],
                                              SCS_[:, 76:84], SCS_[:, 84:92])
        fl = lambda v: v.rearrange("p h i -> p (h i)")
        v3 = lambda v: v.rearrange("p (h i) -> p h i", h=DN_H)
        v8 = lambda v: v.rearrange("p (h d) -> p h d", h=SW_H)
        v2 = lambda v: v.rearrange("p (h d) -> p h d", h=SW_KV)
        hb = lambda v: bc(v, [[1, DN_H], [0, 128]])

        P.dma("sp", CSTK[0:48, :], self.st_conv[l].rearrange("s j c -> (s j) c"), w=[bCSTK])
        for q4 in range(0, NS, 4):
            P.dma("pool", KC[:, q4:q4 + 4, :], self.st_k[l][q4:q4 + 4].rearrange("s k c -> k s c"), w=[bKC])
        P.op("dve", lambda e: e.memset(VCE[:, :, :, DH:DH + 1], 1.0), w=[bVCE])
        for q4 in range(0, NS, 4):
            for hk in range(SW_KV):
                P.dma("pool", VCE[:, q4:q4 + 4, hk, 0:DH],
                      self.st_v[l][q4:q4 + 4, :, hk * DH:(hk + 1) * DH].rearrange("s k d -> k s d"), w=[bVCE])
        P.op("act", lambda e: e.activation(out=SELB, in_=SEL, func=AF.Copy), r=[bCF], w=[bSELB])
        P.dma("sp", self.o_k_s[l][:, 0:127, :], self.st_k[l][:, 1:128, :])
        P.dma("sp", self.o_v_s[l][:, 0:127, :], self.st_v[l][:, 1:128, :])
        P.dma("sp", self.o_conv_s[l][:, 0:2, :], self.st_conv[l][:, 1:3, :])

        self.norm_mod(XS[:], bXS, NS, 1, HS[:], bHS, True, self.GS[:, 0:KD, :], self.bGS)

        pc, bpc = PB[0], bPB[0]
        pcv = pc[:, 0:NCC * NS].rearrange("p (c s) -> p c s", c=NCC)
        for c in range(NCC):
            for k in range(KD):
                P.op("pe", lambda e, c=c, k=k: e.matmul(pcv[:, c, :], WIN[:, k, c * 128:(c + 1) * 128], HS[:, k, :],
                                                        start=(k == 0), stop=(k == KD - 1)), r=[bWI, bHS], w=[bpc])
        P.op("act", lambda e: e.activation(out=PCT, in_=pcv, func=AF.Copy), r=[bpc], w=[bPCT])
        for half in range(2):
            pt, bpt = self.bank()
            for cc in range(6):
                c = half * 6 + cc
                P.op("pe", lambda e, c=c, cc=cc, pt=pt: e.transpose(pt[:, cc * 48:(cc + 1) * 48],
                                                                    CSTK[0:48, c * 128:(c + 1) * 128], IDF[0:48, 0:48]),
                     r=[bCSTK, bCF], w=[bpt])
            P.op("dve", lambda e, half=half, pt=pt: e.tensor_copy(out=fl(CSTT[:, half * 6:(half + 1) * 6, :]),
                                                                  in_=pt[:, 0:6 * 48]), r=[bpt], w=[bCSTT])
        cwv = lambda j: bc(self.SMALL[:, 96 + j:97 + j], [[4, NCC], [0, NS]])
        stv = lambda j: bc(CSTT[:, 0, j:j + 1], [[48, NCC], [3, NS]])
        P.op("dve", lambda e: e.tensor_tensor(out=ACC, in0=PCT, in1=cwv(3), op=ALU.mult), r=[bPCT, self.bSMALL], w=[bACC])
        for j in range(3):
            P.op("dve", lambda e, j=j: e.tensor_tensor(out=T12, in0=stv(j), in1=cwv(j), op=ALU.mult),
                 r=[bCSTT, self.bSMALL], w=[bT12])
            P.op("dve", lambda e: e.tensor_tensor(out=ACC, in0=ACC, in1=T12, op=ALU.add), r=[bT12, bACC], w=[bACC])
        P.op("act", lambda e: e.activation(out=ACC, in_=ACC, func=AF.Silu), r=[bACC], w=[bACC])
        P.op("act", lambda e: e.activation(out=SQS, in_=ACC[:, 0:8, :], func=AF.Square), r=[bACC], w=[bSQS])
        pn, bpn = self.bank()
        P.op("pe", lambda e: e.matmul(pn[:, 0:8 * NS], self.ONES[:], SQS.rearrange("p c s -> p (c s)"),
                                      start=True, stop=True), r=[self.bONES, bSQS], w=[bpn])
        P.op("act", lambda e: e.activation(out=fl(RN)[:, 0:4 * NS], in_=pn[:, 0:4 * NS], func=AF.Ln, scale=float(DK),
                                           bias=self.EPSD[:, 1:2]), r=[bpn, self.bONES], w=[bRN])
        P.op("act", lambda e: e.activation(out=fl(RN)[:, 4 * NS:8 * NS], in_=pn[:, 4 * NS:8 * NS], func=AF.Ln,
                                           bias=self.EPSD[:, 4:5]), r=[bpn, self.bONES], w=[bRN])
        P.op("act", lambda e: e.activation(out=RN, in_=RN, func=AF.Exp, scale=-0.5), r=[bRN], w=[bRN])
        P.op("dve", lambda e: e.tensor_tensor(out=QKN, in0=ACC[:, 0:8, :], in1=RN, op=ALU.mult), r=[bACC, bRN], w=[bQKN])
        selb = bc(SEL[:, 0:1], [[0, DN_H], [NS, NS], [1, NS]])
        P.op("dve", lambda e: e.tensor_tensor(out=KM, in0=bc(QKN[:, 4, 0:1], [[NS, DN_H], [1, NS], [0, NS]]), in1=selb,
                                              op=ALU.mult), r=[bQKN, bCF], w=[bKM])
        P.op("dve", lambda e: e.tensor_tensor(out=QM, in0=bc(QKN[:, 0, 0:1], [[NS, DN_H], [1, NS], [0, NS]]), in1=selb,
                                              op=ALU.mult), r=[bQKN, bCF], w=[bQM])
        ptv, bptv = self.bank()
        ptk, bptk = self.bank()
        for h in range(DN_H):
            P.op("pe", lambda e, h=h: e.transpose(ptv[p16, h * 128:(h + 1) * 128], ACC[:, 8 + h, :], IDF),
                 r=[bACC, bCF], w=[bptv])
            P.op("pe", lambda e, h=h: e.transpose(ptk[p16, h * 128:(h + 1) * 128], QKN[:, 4 + h, :], IDF),
                 r=[bQKN, bCF], w=[bptk])
        P.op("act", lambda e: e.activation(out=fl(VTOK)[p16], in_=ptv[p16, :], func=AF.Copy), r=[bptv], w=[bVTOK])
        P.op("dve", lambda e: e.tensor_copy(out=fl(KTOK)[p16], in_=ptk[p16, :]), r=[bptk], w=[bKTOK])

        def tokproj(pt_, bpt_, c0, ncol, o0):
            for k in range(KD):
                P.op("pe", lambda e, k=k: e.matmul(pt_[p16, o0:o0 + ncol], HS[:, k, :], WIN[:, k, c0:c0 + ncol],
                                                   start=(k == 0), stop=(k == KD - 1)), r=[bWI, bHS], w=[bpt_])
        for cg in range(3):
            pp, bpp = self.bank()
            tokproj(pp, bpp, cg * 512, 512, 0)
            P.op("act", lambda e, pp=pp, cg=cg: e.activation(out=PCTOK[p16, cg * 512:(cg + 1) * 512], in_=pp[p16, :],
                                                             func=AF.Copy), r=[bpp], w=[bPCTOK])
        P.dma("sp", self.o_conv_s[l][:, 2, :], PCTOK[p16, :], r=[bPCTOK])
        PZ, bPZ = self.bank()
        tokproj(PZ, bPZ, C_Z, 512, 0)
        P.op("act", lambda e: e.activation(out=fl(ONS)[p16], in_=PZ[p16, :], func=AF.Silu), r=[bPZ], w=[bONS])
        P.op("dve", lambda e: e.tensor_tensor(out=fl(ZGS)[p16], in0=fl(ONS)[p16], in1=NORMW4[p16], op=ALU.mult),
             r=[bONS, bBCP], w=[bZGS])
        PQ, bPQ = PB[2], bPB[2]
        PKV, bPKV = PB[3], bPB[3]
        tokproj(PQ, bPQ, C_Q, 512, 0)
        tokproj(PKV, bPKV, C_K, 256, 0)
        tokproj(PKV, bPKV, C_BA, 8, 256)
        P.op("act", lambda e: e.activation(out=BETA[p16], in_=PKV[p16, 256:260], func=AF.Exp, scale=-1.0),
             r=[bPKV], w=[bSCS_])
        P.op("dve", lambda e: e.tensor_scalar(out=BETA[p16], in0=BETA[p16], scalar1=1.0, scalar2=None, op0=ALU.add),
             r=[bSCS_], w=[bSCS_])
        P.op("dve", lambda e: e.reciprocal(out=BETA[p16], in_=BETA[p16]), r=[bSCS_], w=[bSCS_])
        P.op("dve", lambda e: e.tensor_tensor(out=XA_[p16], in0=PKV[p16, 260:264], in1=DTB[p16], op=ALU.add),
             r=[bPKV, bBCP], w=[bSCS_])
        P.op("act", lambda e: e.activation(out=AXV[p16], in_=XA_[p16], func=AF.Abs), r=[bSCS_], w=[bSCS_])
        P.op("act", lambda e: e.activation(out=E1[p16], in_=AXV[p16], func=AF.Exp, scale=-1.0), r=[bSCS_], w=[bSCS_])
        P.op("act", lambda e: e.activation(out=L1[p16], in_=E1[p16], func=AF.Ln, bias=self.EPSD[p16, 3:4]),
             r=[bSCS_, self.bONES], w=[bSCS_])
        P.op("dve", lambda e: e.scalar_tensor_tensor(out=G4[p16], in0=XA_[p16], scalar=0.0, in1=L1[p16], op0=ALU.max,
                                                     op1=ALU.add), r=[bSCS_], w=[bSCS_])
        P.op("dve", lambda e: e.tensor_tensor(out=G4[p16], in0=G4[p16], in1=NEGA[p16], op=ALU.mult), r=[bSCS_, bSC], w=[bSCS_])
        P.op("act", lambda e: e.activation(out=AEX[p16], in_=G4[p16], func=AF.Exp), r=[bSCS_], w=[bSCS_])
        P.op("dve", lambda e: e.tensor_tensor(out=AD[p16], in0=bc(AEX[p16], [[0, NS], [1, DN_H]]),
                                              in1=bc(IDF[p16, 0:1], [[1, NS], [0, DN_H]]), op=ALU.mult),
             r=[bSCS_, bCF], w=[bAD])
        pab, bpab = self.bank()
        P.op("pe", lambda e: e.matmul(pab[:, 0:NS * DN_H], ONESF[p16, :], AD[p16].rearrange("p s h -> p (s h)"),
                                      start=True, stop=True), r=[bCF, bAD], w=[bpab])
        P.op("dve", lambda e: e.tensor_copy(out=AB, in_=pab[:, 0:NS * DN_H]), r=[bpab], w=[bAB])

        P.op("act", lambda e: e.activation(out=TMP[p16], in_=PQ[p16, :], func=AF.Square), r=[bPQ], w=[bTMP])
        P.op("dve", lambda e: e.tensor_reduce(out=SSQ10[p16, 0:8], in_=v8(TMP[p16]), axis=AX.X, op=ALU.add),
             r=[bTMP], w=[bSCS_])
        P.op("act", lambda e: e.activation(out=self.R[p16, 0:128], in_=PKV[p16, 0:128], func=AF.Square),
             r=[bPKV], w=[self.bR])
        P.op("dve", lambda e: e.tensor_reduce(out=SSQ10[p16, 8:10], in_=v2(self.R[p16, 0:128]), axis=AX.X, op=ALU.add),
             r=[self.bR], w=[bSCS_])
        P.op("act", lambda e: e.activation(out=RS10[p16], in_=SSQ10[p16], func=AF.Ln, bias=self.EPSD[p16, 2:3]),
             r=[bSCS_, self.bONES], w=[bSCS_])
        P.op("act", lambda e: e.activation(out=RS10[p16], in_=RS10[p16], func=AF.Exp, scale=-0.5), r=[bSCS_], w=[bSCS_])
        P.op("dve", lambda e: e.tensor_tensor(out=QSS[p16], in0=v8(PQ[p16, :]), in1=bc(RS10[p16, 0:8], [[1, 8], [0, DH]]),
                                              op=ALU.mult), r=[bPQ, bSCS_], w=[bQSS])
        P.op("dve", lambda e: e.tensor_tensor(out=QSS[p16], in0=QSS[p16], in1=v8(QNW8[p16]), op=ALU.mult),
             r=[bQSS, bBCP], w=[bQSS])
        P.op("dve", lambda e: e.tensor_tensor(out=KSS[p16], in0=v2(PKV[p16, 0:128]),
                                              in1=bc(RS10[p16, 8:10], [[1, 2], [0, DH]]), op=ALU.mult),
             r=[bPKV, bSCS_], w=[bKSS])
        P.op("dve", lambda e: e.tensor_tensor(out=KSS[p16], in0=KSS[p16], in1=v2(KNW2[p16]), op=ALU.mult),
             r=[bKSS, bBCP], w=[bKSS])
        P.op("act", lambda e: e.activation(out=VRS[p16], in_=v2(PKV[p16, 128:256]), func=AF.Copy), r=[bPKV], w=[bVRS])

        def rope(x1, x2, nh, d1, d2, bsrc, bdst_):
            cosb = bc(ROPS[p16, 0:1], [[0, nh], [1, 32]])
            sinb = bc(ROPS[p16, 32:33], [[0, nh], [1, 32]])
            t1 = TMP[p16, 0:nh * 32].rearrange("p (h d) -> p h d", h=nh)
            t2 = TMP[p16, 256:256 + nh * 32].rearrange("p (h d) -> p h d", h=nh)
            for (xa, xb, dd, op_) in ((x1, x2, d1, ALU.subtract), (x2, x1, d2, ALU.add)):
                P.op("dve", lambda e, xa=xa: e.tensor_tensor(out=t1, in0=xa, in1=cosb, op=ALU.mult),
                     r=[bsrc, bCF, bTMP], w=[bTMP])
                P.op("dve", lambda e, xb=xb: e.tensor_tensor(out=t2, in0=xb, in1=sinb, op=ALU.mult),
                     r=[bsrc, bCF, bTMP], w=[bTMP])
                P.op("dve", lambda e, dd=dd, op_=op_: e.tensor_tensor(out=dd, in0=t1, in1=t2, op=op_),
                     r=[bTMP], w=[bdst_])
        rope(QSS[p16, :, 0:32], QSS[p16, :, 32:64], SW_H, QRS[p16, :, 0:32], QRS[p16, :, 32:64], bQSS, bQRS)
        rope(KSS[p16, :, 0:32], KSS[p16, :, 32:64], SW_KV, KRS[p16, :, 0:32], KRS[p16, :, 32:64], bKSS, bKRS)
        P.dma("sp", self.o_k_s[l][:, 127, :], KRS[p16].rearrange("p h d -> p (h d)"), r=[bKRS])
        P.dma("sp", self.o_v_s[l][:, 127, :], VRS[p16].rearrange("p h d -> p (h d)"), r=[bVRS])
        P.op("act", lambda e: e.activation(out=bc(QFS[p16, 0, 0, 0:1], [[DH, 2], [2 * DH, 4], [1, DH]]),
                                           in_=QRS[p16].rearrange("p (a b) d -> p a b d", a=2), func=AF.Copy),
             r=[bQRS], w=[bQFS])
        ptq, bptq = self.bank()
        ptqv = ptq[:, 0:32].bitcast(BF16)
        for g in range(4):
            P.op("pe", lambda e, g=g: e.transpose(ptqv[:, g * NS:(g + 1) * NS],
                                                  QFS[p16, g, :, :].rearrange("p a d -> p (a d)"), IDB[p16, p16]),
                 r=[bQFS, bCB], w=[bptq])
        P.op("act", lambda e: e.activation(out=QTSS.rearrange("p g s -> p (g s)"), in_=ptqv[:, 0:4 * NS], func=AF.Copy),
             r=[bptq], w=[bQTSS])
        P.op("dve", lambda e: e.tensor_tensor(out=PROD[p16].rearrange("p (a b) d -> p a b d", a=2),
                                              in0=QRS[p16].rearrange("p (a b) d -> p a b d", a=2),
                                              in1=bc(KRS[p16, 0, 0:1], [[DH, 2], [0, 4], [1, DH]]), op=ALU.mult),
             r=[bQRS, bKRS], w=[bPROD])
        P.op("dve", lambda e: e.tensor_reduce(out=SN8[p16], in_=PROD[p16], axis=AX.X, op=ALU.add), r=[bPROD], w=[bSCS_])
        P.op("act", lambda e: e.activation(out=PN8[p16], in_=SN8[p16], func=AF.Exp), r=[bSCS_], w=[bSCS_])

        pks, bpks = PB[0], bPB[0]
        pos_, bpos_ = PB[1], bPB[1]
        for (pz_, bpz_) in ((pks, bpks), (pos_, bpos_)):
            P.op("pe", lambda e, pz_=pz_: e.matmul(pz_[p16, :], ZEROF[:, 0:NS], CF[:, 0:512], start=True, stop=False,
                                                   skip_group_check=True), r=[bCF], w=[bpz_])
        stS = self.st_S[l]
        for s_ in range(NS):
            S0b, bS0b = S0[s_ % 2]
            P.dma("sp", S0b, stS[s_].rearrange("h k v -> k h v"), w=[bS0b])
            for h in range(DN_H):
                P.op("pe", lambda e, h=h, s_=s_, S0b=S0b: e.matmul(
                    pks[p16, h * 128:(h + 1) * 128], KM[:, h, s_, :], S0b[:, h, :], start=False, stop=(s_ == NS - 1),
                    skip_group_check=True), r=[bKM, bS0b], w=[bpks])
        P.op("dve", lambda e: e.tensor_tensor(out=T1[p16], in0=v3(pks[p16, :]), in1=hb(AEX[p16]), op=ALU.mult),
             r=[bpks, bSCS_], w=[bT1])
        P.op("dve", lambda e: e.tensor_tensor(out=T1[p16], in0=VTOK[p16], in1=T1[p16], op=ALU.subtract),
             r=[bVTOK, bT1], w=[bT1])
        P.op("dve", lambda e: e.tensor_tensor(out=VNS[p16], in0=T1[p16], in1=hb(BETA[p16]), op=ALU.mult),
             r=[bT1, bSCS_], w=[bVNS])
        for s_ in range(NS):
            S0b, bS0b = S0[s_ % 2]
            SNb, bSNb = SN[s_ % 2]
            KMSb, bKMSb = KMS[s_ % 2]
            P.dma("sp", S0b, stS[s_].rearrange("h k v -> k h v"), w=[bS0b])
            P.op("dve", lambda e, s_=s_, KMSb=KMSb: e.tensor_scalar(out=fl(KMSb)[p16], in0=fl(KTOK)[p16],
                                                                    scalar1=IDF[p16, s_:s_ + 1], scalar2=None, op0=ALU.mult),
                 r=[bKTOK, bCF], w=[bKMSb])
            pu, bpu = self.bank()
            for h in range(DN_H):
                P.op("pe", lambda e, h=h, pu=pu, KMSb=KMSb: e.matmul(pu[:, h * 128:(h + 1) * 128], KMSb[p16, h, :],
                                                                    VNS[p16, h, :], start=True, stop=True),
                     r=[bKMSb, bVNS], w=[bpu])
            P.op("dve", lambda e, s_=s_, S0b=S0b, SNb=SNb: e.tensor_tensor(
                out=SNb, in0=S0b, in1=bc(AB[:, s_ * DN_H:s_ * DN_H + 1], [[1, DN_H], [0, DV]]), op=ALU.mult),
                r=[bS0b, bAB], w=[bSNb])
            P.op("dve", lambda e, pu=pu, SNb=SNb: e.tensor_tensor(out=fl(SNb), in0=fl(SNb), in1=pu[:, :], op=ALU.add),
                 r=[bSNb, bpu], w=[bSNb])
            P.dma("sp", self.o_S_s[l][s_].rearrange("h k v -> k h v"), SNb, r=[bSNb])
            for h in range(DN_H):
                P.op("pe", lambda e, h=h, s_=s_, SNb=SNb: e.matmul(
                    pos_[p16, h * 128:(h + 1) * 128], QM[:, h, s_, :], SNb[:, h, :], start=False, stop=(s_ == NS - 1),
                    skip_group_check=True), r=[bQM, bSNb], w=[bpos_])
        P.op("act", lambda e: e.activation(out=fl(T1)[p16], in_=pos_[p16, :], func=AF.Square), r=[bpos_], w=[bT1])
        P.op("dve", lambda e: e.tensor_reduce(out=SSQ4[p16], in_=T1[p16], axis=AX.X, op=ALU.add), r=[bT1], w=[bSCS_])
        P.op("act", lambda e: e.activation(out=RS4[p16], in_=SSQ4[p16], func=AF.Ln, scale=1.0 / DV,
                                           bias=self.EPSD[p16, 4:5]), r=[bSCS_, self.bONES], w=[bSCS_])
        P.op("act", lambda e: e.activation(out=RS4[p16], in_=RS4[p16], func=AF.Exp, scale=-0.5), r=[bSCS_], w=[bSCS_])
        P.op("dve", lambda e: e.tensor_tensor(out=ONS[p16], in0=v3(pos_[p16, :]), in1=hb(RS4[p16]), op=ALU.mult),
             r=[bpos_, bSCS_], w=[bONS])
        P.op("dve", lambda e: e.tensor_tensor(out=ON2S[p16], in0=ONS[p16], in1=ZGS[p16], op=ALU.mult),
             r=[bONS, bZGS], w=[bON2S])
        pto, bpto = self.bank()
        ptov = pto[:, 0:32].bitcast(BF16)
        for h in range(DN_H):
            P.op("pe", lambda e, h=h: e.transpose(ptov[:, h * NS:(h + 1) * NS], ON2S[p16, h, :], IDB[p16, p16]),
                 r=[bON2S, bCB], w=[bpto])
        P.op("act", lambda e: e.activation(out=CATS[:, 0:4, :], in_=ptov[:, 0:4 * NS].rearrange("p (h s) -> p h s", h=4),
                                           func=AF.Copy), r=[bpto], w=[bCATS])

        psc, bpsc = PB[2], bPB[2]
        for s_ in range(NS):
            KCTb, bKCTb = KCT[s_ % 2]
            ptc, bptc = self.bank()
            ptcv = ptc[:, 0:64].bitcast(BF16)
            P.op("pe", lambda e, s_=s_, ptcv=ptcv: e.transpose(ptcv, KC[:, s_, :], IDB), r=[bKC, bCB], w=[bptc])
            P.op("act", lambda e, ptcv=ptcv, KCTb=KCTb: e.activation(out=KCTb, in_=ptcv, func=AF.Copy),
                 r=[bptc], w=[bKCTb])
            for hk in range(SW_KV):
                psl = slice(hk * DH, (hk + 1) * DH)
                P.op("pe", lambda e, s_=s_, hk=hk, psl=psl, KCTb=KCTb: e.matmul(
                    psc[:, s_ * 8 + hk * 4:s_ * 8 + hk * 4 + 4], KCTb[psl, :], QTSS[psl, :, s_], start=True, stop=True),
                    r=[bKCTb, bQTSS], w=[bpsc])
        P.op("act", lambda e: e.activation(out=PTS.rearrange("p s h -> p (s h)"), in_=psc[:, 0:NS * SW_H], func=AF.Exp),
             r=[bpsc], w=[bPTS])
        pov_ = [(PB[0], bPB[0]), (PB[1], bPB[1])]
        for (pz_, bpz_) in pov_:
            P.op("pe", lambda e, pz_=pz_: e.matmul(pz_[p16, :], ZEROF[:, 0:NS], CF[:, 0:512], start=True, stop=False,
                                                   skip_group_check=True), r=[bCF], w=[bpz_])
        for s_ in range(NS):
            PTMb, bPTMb = PTM[s_ % 2]
            P.op("dve", lambda e, s_=s_, PTMb=PTMb: e.tensor_tensor(
                out=PTMb, in0=bc(PTS[:, s_, 0:1], [[1, SW_H], [0, NS]]),
                in1=bc(SELB[:, s_ * NS:s_ * NS + 1], [[0, SW_H], [1, NS]]), op=ALU.mult),
                r=[bPTS, bSELB], w=[bPTMb])
            for hk in range(SW_KV):
                po_, bpo_ = pov_[hk]
                for g in range(4):
                    P.op("pe", lambda e, s_=s_, hk=hk, g=g, po_=po_, PTMb=PTMb: e.matmul(
                        po_[p16, g * (DH + 1):(g + 1) * (DH + 1)], PTMb[:, hk * 4 + g, :], VCE[:, s_, hk, :],
                        start=False, stop=(s_ == NS - 1), skip_group_check=True), r=[bPTMb, bVCE], w=[bpo_])
        for hk in range(SW_KV):
            po_, bpo_ = pov_[hk]
            pov = po_[p16, 0:4 * (DH + 1)].rearrange("p (g d) -> p g d", g=4)
            hs_ = slice(hk * 4, (hk + 1) * 4)
            P.op("dve", lambda e, hk=hk, hs_=hs_: e.tensor_tensor(
                out=PROD[p16, hs_, :], in0=bc(VRS[p16, hk, 0:1], [[0, 4], [1, DH]]),
                in1=bc(PN8[p16, hk * 4:hk * 4 + 1], [[1, 4], [0, DH]]), op=ALU.mult), r=[bVRS, bSCS_], w=[bPROD])
            P.op("dve", lambda e, pov=pov, hs_=hs_: e.tensor_tensor(out=OSS[p16, hs_, :], in0=pov[:, :, 0:DH],
                                                                   in1=PROD[p16, hs_, :], op=ALU.add),
                 r=[bpo_, bPROD], w=[bOSS])
            P.op("dve", lambda e, pov=pov, hs_=hs_: e.tensor_tensor(out=DEN8[p16, hs_], in0=pov[:, :, DH],
                                                                   in1=PN8[p16, hs_], op=ALU.add),
                 r=[bpo_, bSCS_], w=[bSCS_])
        P.op("dve", lambda e: e.tensor_tensor(out=DEN8[p16], in0=DEN8[p16], in1=ESINK[p16], op=ALU.add),
             r=[bSCS_, bSC], w=[bSCS_])
        P.op("dve", lambda e: e.reciprocal(out=RDEN8[p16], in_=DEN8[p16]), r=[bSCS_], w=[bSCS_])
        P.op("dve", lambda e: e.tensor_tensor(out=OSBS[p16], in0=OSS[p16], in1=bc(RDEN8[p16, 0:1], [[1, 8], [0, DH]]),
                                              op=ALU.mult), r=[bOSS, bSCS_], w=[bOSBS])
        ptos, bptos = self.bank()
        ptosv = ptos[:, 0:32].bitcast(BF16)
        osf = OSBS.rearrange("p h d -> p (h d)")
        for c in range(4):
            P.op("pe", lambda e, c=c: e.transpose(ptosv[:, c * NS:(c + 1) * NS], osf[p16, c * 128:(c + 1) * 128],
                                                  IDB[p16, p16]), r=[bOSBS, bCB], w=[bptos])
        P.op("act", lambda e: e.activation(out=CATS[:, 4:8, :], in_=ptosv[:, 0:4 * NS].rearrange("p (h s) -> p h s", h=4),
                                           func=AF.Copy), r=[bptos], w=[bCATS])
        for d in range(KD):
            py, bpy = self.bank()
            for c in range(KD):
                P.op("pe", lambda e, d=d, c=c, py=py: e.matmul(py[:, 0:NS], WOUT[:, c, d * 128:(d + 1) * 128], CATS[:, c, :],
                                                               start=(c == 0), stop=(c == KD - 1)),
                     r=[bWO, bCATS], w=[bpy])
            self.resid(XS[:], bXS, NS, 1, d, py, bpy, True)


def make_constf():
    c = np.zeros((128, NCF), np.float32)
    j = np.arange(128)[:, None]
    i = np.arange(128)[None, :]
    c[:, 0:128] = (j <= i)
    c[:, 128:256] = 1.0
    c[:, 256:384] = np.where(i >= j, 0.0, NEG)
    c[:, 384:512] = np.where(i < j, 0.0, NEG)
    c[:, 512:640] = np.eye(128)
    c[:, 640:896] = np.eye(NS, dtype=np.float32).reshape(1, NS * NS)
    half = DH // 2
    inv = np.power(np.float32(10000.0), -np.arange(half, dtype=np.float32) * np.float32(2.0) / np.float32(DH)).astype(np.float32)
    ang = (np.float32(PAST_LEN) * inv).astype(np.float32)
    c[:, 896:928] = np.cos(ang)[None, :]
    c[:, 928:960] = np.sin(ang)[None, :]
    return c.astype(np.float32)


def make_constb():
    c = np.zeros((128, NCB), np.float32)
    j = np.arange(128)[:, None]
    i = np.arange(128)[None, :]
    c[:, 0:128] = np.eye(128)
    c[:, 128:640] = np.tile(np.where(j <= i, 0.0, NEG), (1, 4))
    c[:, 640:1152] = np.tile(np.where(j >= i, 0.0, NEG), (1, 4))
    c[:, 1152:1280] = (j // 8 == i // 8)
    for li, m in enumerate((8, 16, 32, 64)):
        ml = (j // (2 * m) == i // (2 * m)) & ((j // m) % 2 == 1) & ((i // m) % 2 == 0)
        c[:, 1280 + li * 128:1408 + li * 128] = ml
        c[:, 1792 + li * 128:1920 + li * 128] = ml.T
    return c.astype(np.float32)


def make_rope(seq):
    half = DH // 2
    inv = np.power(np.float32(10000.0), -np.arange(half, dtype=np.float32) * np.float32(2.0) / np.float32(DH)).astype(np.float32)
    pos = np.arange(seq, dtype=np.float32)
    ang = (pos[:, None] * inv[None, :]).astype(np.float32)
    t = np.concatenate([np.cos(ang), np.sin(ang)], -1).astype(np.float32)
    return np.ascontiguousarray(t.reshape(seq // 128, 128, 64).transpose(1, 0, 2))


def make_bcp(inp, L):
    b = np.zeros((L, 128, NBC), np.float32)
    for l in range(L):
        row = np.concatenate([
            np.asarray(inp["dn_A_log"][l]), np.asarray(inp["dn_dt_bias"][l]), np.asarray(inp["swa_sinks"][l]),
            np.tile(np.asarray(inp["dn_norm_w"][l]), 4), np.tile(np.asarray(inp["swa_q_norm"][l]), 8),
            np.tile(np.asarray(inp["swa_k_norm"][l]), 2)]).astype(np.float32)
        b[l] = row[None, :]
    return b


def make_smallp(inp, L):
    sp = np.zeros((L, 128, 144), np.float32)
    for l in range(L):
        sp[l, :, 0:72] = np.asarray(inp["b_ada"][l]).reshape(72, 128).T
        for s, nm in enumerate(("g_ffn1", "g_mix", "g_ffn2")):
            sp[l, :, 72 + 8 * s:80 + 8 * s] = np.asarray(inp[nm][l]).reshape(8, 128).T
        sp[l, :, 96:144] = np.asarray(inp["dn_conv_w"][l]).T.reshape(12, 128, 4).transpose(1, 0, 2).reshape(128, 48)
    return sp


_CACHE = {}


def get_nc(seq, depth, stop):
    key = (seq, depth, stop)
    if key not in _CACHE:
        b = Builder(seq, depth, stop)
        nc = b.build()
        _CACHE[key] = (nc, b)
    return _CACHE[key]


def make_in_maps(inputs, seq, depth):
    f = lambda a: np.ascontiguousarray(np.asarray(a, dtype=np.float32))
    shared = {
        "w_ada": f(inputs["w_ada"][:depth]),
        "smallp": make_smallp(inputs, depth),
        "w_ffn1_gu": f(inputs["w_ffn1_gu"][:depth]), "w_ffn2_gu": f(inputs["w_ffn2_gu"][:depth]),
        "w_ffn1_down": f(inputs["w_ffn1_down"][:depth]), "w_ffn2_down": f(inputs["w_ffn2_down"][:depth]),
        "w_in": f(inputs["w_in"][:depth]), "w_out": f(inputs["w_out"][:depth]),
        "constf": make_constf(), "constb": make_constb(), "ropeT": make_rope(seq),
        "bcp": make_bcp(inputs, depth),
    }
    xp = np.asarray(inputs["x_prompt"], np.float32)
    xs = np.asarray(inputs["x_sample"], np.float32)
    cp = np.asarray(inputs["c_prompt"], np.float32)
    cs = np.asarray(inputs["c_sample"], np.float32)
    in_maps = []
    for c in range(NCORES):
        m = dict(shared)
        m["xT"] = np.ascontiguousarray(xp[c, :seq].T)
        m["xsT"] = np.ascontiguousarray(xs[c * NS:(c + 1) * NS, 0].T)
        m["csT"] = np.ascontiguousarray(np.concatenate([cp[c:c + 1], cs[c * NS:(c + 1) * NS]], 0).T)
        sl = slice(c * NS, (c + 1) * NS)
        m["st_conv"] = f(inputs["state_dn_conv"][:depth, sl])
        m["st_S"] = f(inputs["state_dn_S"][:depth, sl])
        m["st_k"] = f(inputs["cache_swa_k"][:depth, sl]).reshape(depth, NS, 128, SW_KV * DH)
        m["st_v"] = f(inputs["cache_swa_v"][:depth, sl]).reshape(depth, NS, 128, SW_KV * DH)
        in_maps.append(m)
    return in_maps


def run(inputs, seq=SEQ, depth=DEPTH, stop=None, trace=False):
    nc, b = get_nc(seq, depth, stop)
    in_maps = make_in_maps(inputs, seq, depth)
    res = run_bass_kernel_spmd(nc, in_maps, core_ids=list(range(NCORES)), trace=trace)
    return res, b


def assemble(r, depth):
    g = lambda c, n: np.asarray(r[c][n])
    yp = np.stack([g(c, "yT").T for c in range(NCORES)], 0)
    ys = np.concatenate([g(c, "ysT").T for c in range(NCORES)], 0)[:, None, :]
    conv_p = np.stack([g(c, "o_conv_p").transpose(0, 2, 1) for c in range(NCORES)], 1)
    S_p = np.stack([g(c, "o_S_p") for c in range(NCORES)], 1)
    k_p = np.stack([g(c, "o_k_p").reshape(depth, 128, SW_KV, DH) for c in range(NCORES)], 1)
    v_p = np.stack([g(c, "o_v_p").reshape(depth, 128, SW_KV, DH) for c in range(NCORES)], 1)
    conv_s = np.concatenate([g(c, "o_conv_s") for c in range(NCORES)], 1)
    S_s = np.concatenate([g(c, "o_S_s") for c in range(NCORES)], 1)
    k_s = np.concatenate([g(c, "o_k_s").reshape(depth, NS, 128, SW_KV, DH) for c in range(NCORES)], 1)
    v_s = np.concatenate([g(c, "o_v_s").reshape(depth, NS, 128, SW_KV, DH) for c in range(NCORES)], 1)
    outs = (yp, ys, conv_p, S_p, k_p, v_p, conv_s, S_s, k_s, v_s)
    return tuple(np.ascontiguousarray(o, dtype=np.float32) for o in outs)


def kernel(**inputs):
    res, b = run(inputs)
    return assemble(res.results, DEPTH)
```

```python
import math
import os
from contextlib import ExitStack

import numpy as np
import ml_dtypes

import concourse.bass as bass
import concourse.mybir as mybir
from concourse.bass_utils import run_bass_kernel_spmd

F32 = mybir.dt.float32
BF16 = mybir.dt.bfloat16
AF = mybir.ActivationFunctionType
ALU = mybir.AluOpType
AX = mybir.AxisListType

NCORES = 8
D = 1024
KD = D // 128
SEQ = 4096
NS = 16
DEPTH = 2
DFF = 2816
NJ = DFF // 128
NMOD = 9
EPS = 1e-6
DN_H = 4
DK = 128
DV = 128
CONV_CH = 1536
NCC = CONV_CH // 128
SW_H = 8
SW_KV = 2
DH = 64
IN_DIM = 2824
C_Z = 1536
C_BA = 2048
C_Q = 2056
C_K = 2568
C_V = 2696
TT = 512
PAST_LEN = 16384
NEG = -30000.0
SKIP = set(os.environ.get('KSKIP', '').split(','))
DNCUT = int(os.environ.get('KDNCUT', '99'))


class Buf:
    __slots__ = ("name", "lw", "rd", "excl")

    def __init__(self, name, excl=False):
        self.name = name
        self.lw = None
        self.rd = []
        self.excl = excl


class Prog:
    ENGS = ("pe", "act", "dve", "pool", "sp")

    def __init__(self, nc, es):
        self.nc = nc
        self.eng = {"pe": nc.tensor, "act": nc.scalar, "dve": nc.vector,
                    "pool": nc.gpsimd, "sp": nc.sync}
        self.ops = []
        self.csem = {e: es.enter_context(nc.semaphore("c_" + e)) for e in self.ENGS}
        self.es = es
        self.last = {}
        self.lastd = {}
        self.pending_dma = []

    def _deps(self, r, w):
        deps = {}

        def add(o, raw):
            deps[o] = deps.get(o, False) or raw

        for b in r:
            if b.lw is not None:
                add(b.lw, True)
            if b.excl:
                for o in b.rd:
                    add(o, False)
        for b in w:
            if b.lw is not None:
                add(b.lw, False)
            for o in b.rd:
                add(o, False)
        return deps

    def _upd(self, oid, r, w):
        for b in w:
            b.lw = oid
            b.rd = []
        for b in r:
            if b in w:
                continue
            if b.excl:
                b.rd = [oid]
            else:
                b.rd.append(oid)

    def op(self, eng, fn, r=(), w=()):
        deps = self._deps(r, w)
        oid = len(self.ops)
        self.ops.append((eng, fn, deps, "c", None))
        self._upd(oid, r, w)
        self.last[eng] = oid
        return oid

    def barrier(self):
        deps = {o: True for o in (set(self.last.values()) | set(self.pending_dma))}
        self.pending_dma = []
        for e in self.ENGS:
            self.ops.append((e, None, dict(deps), "b", None))

    def dma(self, eng, out, in_, r=(), w=(), sem="g"):
        deps = self._deps(r, w)
        oid = len(self.ops)
        self.ops.append((eng, (out, in_), deps, "d", sem))
        self._upd(oid, r, w)
        self.lastd[sem] = oid
        self.pending_dma.append(oid)
        return oid

    DUR = {"pe": 0.12, "act": 0.45, "dve": 0.45, "pool": 0.6, "sp": 0.1}
    LAT = 1.2
    DMA_LAT = 6.0
    WINDOW = int(os.environ.get("KWIN", "1"))

    def schedule(self, ops):
        if self.WINDOW <= 1:
            return ops
        n = len(ops)
        order = []
        seg_start = 0
        i = 0
        segs = []
        while i < n:
            if ops[i][3] == "b":
                j = i
                while j < n and ops[j][3] == "b":
                    j += 1
                segs.append((seg_start, i, j))
                seg_start = j
                i = j
            else:
                i += 1
        segs.append((seg_start, n, n))
        fin = [0.0] * n
        for (a, b, c) in segs:
            self._sched_segment(ops, a, b, order, fin)
            order.extend(range(b, c))
            if c > b:
                t = max([fin[x] for x in order[-400:]] + [0.0])
                for x in range(b, c):
                    fin[x] = t
        remap = {old: new for new, old in enumerate(order)}
        segdeps = {}
        for (a, b, c) in segs:
            lastc = {}
            full = {}
            for x in range(a, b):
                if ops[x][3] == "d":
                    full[remap[x]] = True
                elif ops[x][3] == "c":
                    e_ = ops[x][0]
                    if e_ not in lastc or remap[x] > lastc[e_]:
                        lastc[e_] = remap[x]
            for v in lastc.values():
                full[v] = True
            for x in range(b, c):
                segdeps[x] = full
        out = []
        for old in order:
            eng, fn, deps, kind, dsem = ops[old]
            if kind == "b":
                out.append((eng, fn, segdeps[old], kind, dsem))
            else:
                out.append((eng, fn, {remap[d]: r for d, r in deps.items()}, kind, dsem))
        return out

    def _sched_segment(self, ops, a, b, order, fin):
        if b <= a:
            return
        W = self.WINDOW
        queues = {e: [] for e in self.ENGS}
        for i in range(a, b):
            queues[ops[i][0]].append(i)
        head = {e: 0 for e in self.ENGS}
        done = set()
        placed_before = a
        efree = {e: max([fin[x] for x in order[-50:]] + [0.0]) for e in self.ENGS}
        remaining = b - a
        taken = {e: set() for e in self.ENGS}
        pos_in_q = {i: k for k, i in enumerate(queues["pe"])}
        glue = None
        while remaining > 0:
            best = None
            for e in self.ENGS:
                q = queues[e]
                h = head[e]
                cnt = 0
                k = h
                Wm = W
                if e == "pe" and glue is not None:
                    k = pos_in_q[glue]
                    Wm = 1
                while k < len(q) and cnt < Wm:
                    i = q[k]
                    k += 1
                    if i in taken[e]:
                        continue
                    cnt += 1
                    ok = True
                    rdy = efree[e]
                    for d in ops[i][2]:
                        if d >= a and d not in done:
                            ok = False
                            break
                        if ops[d][0] == e and ops[d][3] != "d":
                            t = fin[d]
                        else:
                            t = fin[d] + self.LAT
                        if t > rdy:
                            rdy = t
                    if not ok:
                        continue
                    key = (rdy, i)
                    if best is None or key < best[0]:
                        best = (key, e, i)
                    if rdy <= efree[e]:
                        break
            if best is None and glue is not None:
                glue = None
                continue
            assert best is not None, "scheduler stuck"
            (rdy, _), e, i = best
            dur = self.DUR[e]
            fin[i] = rdy + (self.DMA_LAT if ops[i][3] == "d" else dur)
            efree[e] = rdy + dur
            order.append(i)
            done.add(i)
            taken[e].add(i)
            if e == "pe":
                glue = None
                kq = pos_in_q[i] + 1
                qpe = queues["pe"]
                if kq < len(qpe) and i in ops[qpe[kq]][2] and qpe[kq] not in taken["pe"]:
                    glue = qpe[kq]
            q = queues[e]
            while head[e] < len(q) and q[head[e]] in taken[e]:
                head[e] += 1
            remaining -= 1

    NPOOL = {"sp": 24, "pool": 12, "act": 4}

    def emit(self):
        ops = self.schedule(self.ops)
        self.ops = ops
        n = len(ops)
        marked = [False] * n
        for i in range(n):
            eng, _, deps, kind, _ = ops[i]
            for d, raw in deps.items():
                de, _, _, dk, _ = ops[d]
                if dk == "d":
                    continue
                if de == eng and eng == "pe":
                    continue
                marked[d] = True
        pools = {q: [self.es.enter_context(self.nc.semaphore("dq_%s%d" % (q, j))) for j in range(k)]
                 for q, k in self.NPOOL.items()}
        rr = {q: 0 for q in pools}
        pcount = {q: [0] * len(pools[q]) for q in pools}
        ev = [None] * n
        slot_of = [None] * n
        ccnt = {e: 0 for e in self.ENGS}
        for i in range(n):
            eng, _, _, kind, _ = ops[i]
            if kind == "d":
                j = rr[eng] % len(pools[eng])
                rr[eng] += 1
                slot_of[i] = (j, pcount[eng][j])
                pcount[eng][j] += 16
                ev[i] = (("d", eng, j), pcount[eng][j])
            elif kind == "c" and marked[i]:
                ccnt[eng] += 1
                ev[i] = (("c", eng), ccnt[eng])
        waited = {e: {} for e in self.ENGS}
        nwait = 0

        def semof(key):
            return self.csem[key[1]] if key[0] == "c" else pools[key[1]][key[2]]

        for i in range(n):
            eng, fn, deps, kind, _ = ops[i]
            E = self.eng[eng]
            need = {}
            for d, raw in deps.items():
                de = ops[d][0]
                if ops[d][3] == "c" and de == eng and eng == "pe":
                    continue
                if ev[d] is None:
                    continue
                key, val = ev[d]
                if need.get(key, 0) < val:
                    need[key] = val
            if kind == "d":
                j, prev = slot_of[i]
                if prev > 0:
                    key = ("d", eng, j)
                    if need.get(key, 0) < prev:
                        need[key] = prev
            for key, val in need.items():
                if waited[eng].get(key, 0) >= val:
                    continue
                waited[eng][key] = val
                E.wait_ge(semof(key), val)
                nwait += 1
            if kind == "b":
                continue
            if kind == "d":
                out, in_ = fn
                E.dma_start(out=out, in_=in_).then_inc(pools[eng][slot_of[i][0]], 16)
            else:
                ins = fn(E)
                if marked[i]:
                    ins.then_inc(self.csem[eng], 1)
        for q in pools:
            for j, c in enumerate(pcount[q]):
                if c > 0 and waited["sp"].get(("d", q, j), 0) < c:
                    self.eng["sp"].wait_ge(pools[q][j], c)
        self.stats = dict(n_ops=n, n_wait=nwait, n_marked=sum(marked))


def bc(ap, dims):
    return bass.AP(ap.tensor, ap.offset, [list(ap.ap[0])] + [list(d) for d in dims])


NAR = 51072
O_X0, O_X1, O_H, O_G, O_R, O_TMP = 33792, 37888, 41984, 44032, 49664, 50176
O_WIN, O_WOUT = 0, 11296
NCF = 1088
NCB = 2304
NBC = 1168


class Builder:
    def __init__(self, seq=SEQ, depth=DEPTH, stop=None):
        self.seq = seq
        self.depth = depth
        self.stop = stop
        self.ntile = seq // TT
        self.nblk = seq // 128
        self._rr = 0
        self.rr_set = list(range(8))

    def view(self, off, n32, dtype=F32, shape=None):
        v = self.AR[:, off:off + n32]
        if dtype == BF16:
            v = v.bitcast(BF16)
        if shape is not None:
            names = " ".join("a%d" % i for i in range(len(shape)))
            kw = {"a%d" % i: shape[i] for i in range(len(shape))}
            v = v.rearrange("p (%s) -> p %s" % (names, names), **kw)
        return v

    def mk(self, name, shape, dtype=F32):
        n = 1
        for x in shape:
            n *= x
        n32 = (n + 1) // 2 if dtype == BF16 else n
        n32 = (n32 + 7) // 8 * 8
        for seg in self.segs:
            if seg[1] - seg[0] >= n32:
                off = seg[0]
                seg[0] += n32
                return self.view(off, n32 if dtype == F32 else n32, dtype, None)[:, 0:n].rearrange(
                    "p (%s) -> p %s" % (" ".join("a%d" % i for i in range(len(shape))),
                                        " ".join("a%d" % i for i in range(len(shape)))),
                    **{"a%d" % i: shape[i] for i in range(len(shape))}), Buf(name)
        raise RuntimeError("arena full allocating " + name)

    def bank(self):
        i = self.rr_set[self._rr % len(self.rr_set)]
        self._rr += 1
        return self.PB[i], self.bPB[i]

    def build(self):
        nc = bass.Bass("TRN2", target_bir_lowering=False)
        self.nc = nc
        L = self.depth
        seq = self.seq
        dt = lambda name, shape, kind, d=F32: nc.dram_tensor(name, list(shape), d, kind=kind).ap()
        I, O = "ExternalInput", "ExternalOutput"
        self.xT = dt("xT", [D, seq], I)
        self.xsT = dt("xsT", [D, NS], I)
        self.csT = dt("csT", [D, 1 + NS], I)
        self.w_ada = dt("w_ada", [L, D, NMOD * D], I)
        self.smallp = dt("smallp", [L, 128, 144], I)
        self.bcp = dt("bcp", [L, 128, NBC], I)
        self.w_gu = [dt("w_ffn1_gu", [L, D, 2 * DFF], I), dt("w_ffn2_gu", [L, D, 2 * DFF], I)]
        self.w_dn = [dt("w_ffn1_down", [L, DFF, D], I), dt("w_ffn2_down", [L, DFF, D], I)]
        self.w_in = dt("w_in", [L, D, IN_DIM], I)
        self.w_out = dt("w_out", [L, D, D], I)
        self.constf = dt("constf", [128, NCF], I)
        self.constb = dt("constb", [128, NCB], I)
        self.ropeT = dt("ropeT", [128, seq // 128, 64], I)
        self.yT = dt("yT", [D, seq], O)
        self.ysT = dt("ysT", [D, NS], O)
        self.o_conv_p = dt("o_conv_p", [L, CONV_CH, 3], O)
        self.o_S_p = dt("o_S_p", [L, DN_H, DK, DV], O)
        self.o_k_p = dt("o_k_p", [L, 128, SW_KV * DH], O)
        self.o_v_p = dt("o_v_p", [L, 128, SW_KV * DH], O)
        self.st_conv = dt("st_conv", [L, NS, 3, CONV_CH], I)
        self.st_S = dt("st_S", [L, NS, DN_H, DK, DV], I)
        self.st_k = dt("st_k", [L, NS, 128, SW_KV * DH], I)
        self.st_v = dt("st_v", [L, NS, 128, SW_KV * DH], I)
        self.o_conv_s = dt("o_conv_s", [L, NS, 3, CONV_CH], O)
        self.o_S_s = dt("o_S_s", [L, NS, DN_H, DK, DV], O)
        self.o_k_s = dt("o_k_s", [L, NS, 128, SW_KV * DH], O)
        self.o_v_s = dt("o_v_s", [L, NS, 128, SW_KV * DH], O)

        with ExitStack() as es:
            self.es = es
            P = Prog(nc, es)
            self.P = P
            sbt = lambda name, shape, d=F32: es.enter_context(nc.sbuf_tensor(name, list(shape), d))
            self.AR = sbt("ARENA", [128, NAR])
            self.W = self.view(0, 33792, BF16)
            self.X = [self.view(O_X0, 4096, F32, [KD, TT]), self.view(O_X1, 4096, F32, [KD, TT])]
            self.H = self.view(O_H, 2048, BF16, [KD, TT])
            self.G = self.view(O_G, 5632, BF16, [NJ, TT])
            self.R = self.view(O_R, 512)
            self.TMP = self.view(O_TMP, 512)
            self.XS = sbt("XS", [128, KD, NS])
            self.HS = sbt("HS", [128, KD, NS], BF16)
            self.GS = sbt("GS", [128, NJ, NS], BF16)
            self.CS = sbt("CS", [128, KD, 1 + NS])
            self.SCS = sbt("SCS", [128, KD, 1 + NS], BF16)
            self.MODS = sbt("MODS", [128, NMOD * KD, 1 + NS])
            self.MA = lambda s_: self.MODS[:, (3 * s_ + 1) * KD:(3 * s_ + 2) * KD, :]
            self.MG = lambda s_: self.MODS[:, (3 * s_ + 2) * KD:(3 * s_ + 3) * KD, :]
            self.SMALL = sbt("SMALL", [128, 144])
            self.ONES = sbt("ONES", [128, 128], BF16)
            self.EPSD = sbt("EPSD", [128, 8])
            self.PB = [es.enter_context(nc.psum_tensor("pb%d" % i, [128, 512], F32)) for i in range(8)]
            self.bW = Buf("W")
            self.bX = [Buf("X0"), Buf("X1")]
            self.bH, self.bG, self.bR, self.bTMP = Buf("H"), Buf("G"), Buf("R"), Buf("TMP")
            self.bXS, self.bHS, self.bGS = Buf("XS"), Buf("HS"), Buf("GS")
            self.bCS, self.bSCS, self.bMODS = Buf("CS"), Buf("SCS"), Buf("MODS")
            self.bSMALL, self.bONES = Buf("SMALL"), Buf("ONES")
            self.bPB = [Buf("pb%d" % i, excl=True) for i in range(8)]
            self.bYT = [Buf("yT%d" % i) for i in range(self.ntile)]

            self.prologue()
            first = True
            for l in range(L):
                self.adaln(l)
                self.ffn(l, 0, src_is_input=first)
                first = False
                if self.stop == "ffn1":
                    break
                P.barrier()
                self.mixer(l)
                P.barrier()
                if self.stop == "mix":
                    break
                self.ffn(l, 2, src_is_input=False)
                P.barrier()
            self.epilogue()
            P.emit()
            self.stats = P.stats
        return nc

    def prologue(self):
        P = self.P
        P.dma("sp", self.XS[:], self.xsT.rearrange("(k p) m -> p k m", p=128), w=[self.bXS], sem="ld")
        P.dma("sp", self.CS[:], self.csT.rearrange("(k p) m -> p k m", p=128), w=[self.bCS], sem="ld")
        P.op("dve", lambda e: e.memset(self.ONES[:], 1.0), w=[self.bONES])
        for i, v in enumerate((D * EPS, DK * EPS, DH * EPS, 1.0, EPS, 0.0, 0.0, 0.0)):
            P.op("dve", lambda e, i=i, v=v: e.memset(self.EPSD[:, i:i + 1], v), w=[self.bONES])
        P.op("act", lambda e: e.activation(out=self.SCS[:], in_=self.CS[:], func=AF.Silu),
             r=[self.bCS], w=[self.bSCS])

    def epilogue(self):
        P = self.P
        P.dma("sp", self.ysT.rearrange("(k p) m -> p k m", p=128), self.XS[:], r=[self.bXS], sem="st")

    def adaln(self, l):
        P = self.P
        P.dma("sp", self.SMALL[:], self.smallp[l], w=[self.bSMALL], sem="ld")
        wv = self.w_ada[l].rearrange("(k p) n -> p k n", p=128)
        NP = 8
        PC = NMOD * D // NP
        CPP = PC // 128
        stage = [self.W[:, 0:KD * PC].rearrange("p (k n) -> p k n", k=KD),
                 self.W[:, KD * PC:2 * KD * PC].rearrange("p (k n) -> p k n", k=KD)]
        bst = [Buf("ada0"), Buf("ada1")]
        for q in range(NP):
            s = q % 2
            deps_w = [bst[s]] + ([self.bW] if q < 2 else [])
            P.dma("pool", stage[s], wv[:, :, q * PC:(q + 1) * PC], w=deps_w, sem="wl")
            pb, bpb = self.bank()
            pv = pb[:, 0:CPP * 17].rearrange("p (c m) -> p c m", c=CPP)
            for c in range(CPP):
                for k in range(KD):
                    P.op("pe", lambda e, s=s, c=c, k=k, pv=pv: e.matmul(
                        pv[:, c, :], stage[s][:, k, c * 128:(c + 1) * 128], self.SCS[:, k, :],
                        start=(k == 0), stop=(k == KD - 1)),
                        r=[bst[s], self.bSCS], w=[bpb])
            bias = bc(self.SMALL[:, q * CPP:q * CPP + 1], [[1, CPP], [0, 17]])
            P.op("dve", lambda e, q=q, pv=pv, bias=bias: e.tensor_tensor(
                out=self.MODS[:, q * CPP:(q + 1) * CPP, :], in0=pv, in1=bias, op=ALU.add),
                r=[bpb, self.bSMALL], w=[self.bMODS])
        self.bW.lw = None
        self.bW.rd = list(bst[0].rd) + list(bst[1].rd)
        for s in range(3):
            gcol = bc(self.SMALL[:, 72 + 8 * s:73 + 8 * s], [[1, KD], [0, 1 + NS]])
            P.op("dve", lambda e, s=s: e.tensor_scalar(
                out=self.MA(s), in0=self.MA(s), scalar1=1.0, scalar2=math.sqrt(D), op0=ALU.add, op1=ALU.mult),
                r=[self.bMODS], w=[self.bMODS])
            P.op("dve", lambda e, s=s, gcol=gcol: e.tensor_tensor(
                out=self.MA(s), in0=self.MA(s), in1=gcol, op=ALU.mult),
                r=[self.bMODS, self.bSMALL], w=[self.bMODS])
            P.op("dve", lambda e, s=s: e.tensor_scalar(
                out=self.MG(s), in0=self.MG(s), scalar1=(1.0 if s == 1 else 0.5), scalar2=None, op0=ALU.mult),
                r=[self.bMODS], w=[self.bMODS])

    def load_ffn_weights(self, l, which):
        P = self.P
        wi = 0 if which == 0 else 1
        gu = self.W[:, 0:KD * 2 * DFF].rearrange("p (k n) -> p k n", k=KD)
        dn = self.W[:, KD * 2 * DFF:KD * 2 * DFF + NJ * D].rearrange("p (j n) -> p j n", j=NJ)
        guv = self.w_gu[wi][l].rearrange("(k p) n -> p k n", p=128)
        dnv = self.w_dn[wi][l].rearrange("(j p) n -> p j n", p=128)
        self.bWgu = [Buf("Wgu%d" % i) for i in range(NJ // 2)]
        self.bWdn = [Buf("Wdn%d" % i) for i in range(NJ // 2)]
        for jb in range(NJ // 2):
            c0 = jb * 256
            for off in (0, DFF):
                P.dma("pool", gu[:, :, off + c0:off + c0 + 256], guv[:, :, off + c0:off + c0 + 256],
                      w=[self.bWgu[jb]] + ([self.bW] if jb == 0 else []), sem="wl")
        for jb in range(NJ // 2):
            P.dma("pool", dn[:, 2 * jb:2 * jb + 2, :], dnv[:, 2 * jb:2 * jb + 2, :], w=[self.bWdn[jb]], sem="wl")
        return gu, dn

    def norm_mod(self, xt, bx, T, s, hs, bh, sample, sqv, bsq):
        P = self.P
        ssq, bssq = self.bank()
        P.op("act", lambda e: e.activation(out=sqv, in_=xt, func=AF.Square), r=[bx], w=[bsq])
        for k in range(KD):
            P.op("pe", lambda e, k=k: e.matmul(ssq[:, 0:T], self.ONES[:], sqv[:, k, :],
                                               start=(k == 0), stop=(k == KD - 1)),
                 r=[self.bONES, bsq], w=[bssq])
        P.op("act", lambda e: e.activation(out=self.R[:, 0:T], in_=ssq[:, 0:T], func=AF.Ln, bias=self.EPSD[:, 0:1]),
             r=[bssq, self.bONES], w=[self.bR])
        P.op("act", lambda e: e.activation(out=self.R[:, 0:T], in_=self.R[:, 0:T], func=AF.Exp, scale=-0.5),
             r=[self.bR], w=[self.bR])
        shift = self.MODS[:, (3 * s) * KD:(3 * s + 1) * KD, :]
        if not sample:
            for k in range(KD):
                P.op("dve", lambda e, k=k: e.tensor_tensor(out=self.TMP[:, 0:T], in0=xt[:, k, :], in1=self.R[:, 0:T],
                                                           op=ALU.mult), r=[bx, self.bR], w=[self.bTMP])
                P.op("act", lambda e, k=k: e.activation(out=hs[:, k, :], in_=self.TMP[:, 0:T], func=AF.Identity,
                                                        scale=self.MA(s)[:, k, 0:1], bias=shift[:, k, 0:1]),
                     r=[self.bTMP, self.bMODS], w=[bh])
        else:
            rb = bc(self.R[:, 0:1], [[0, KD], [1, T]])
            tv = self.TMP[:, 0:KD * T].rearrange("p (k t) -> p k t", k=KD)
            P.op("dve", lambda e: e.tensor_tensor(out=tv, in0=xt, in1=rb, op=ALU.mult), r=[bx, self.bR], w=[self.bTMP])
            P.op("dve", lambda e: e.tensor_tensor(out=tv, in0=tv, in1=self.MA(s)[:, :, 1:1 + NS], op=ALU.mult),
                 r=[self.bTMP, self.bMODS], w=[self.bTMP])
            P.op("dve", lambda e: e.tensor_tensor(out=hs, in0=tv, in1=shift[:, :, 1:1 + NS], op=ALU.add),
                 r=[self.bTMP, self.bMODS], w=[bh])

    def ffn_tile(self, xt, bx, T, s, gu, dn, sample):
        P = self.P
        hs = self.HS[:] if sample else self.H[:, :, 0:T]
        bh = self.bHS if sample else self.bH
        gs = self.GS[:] if sample else self.G[:, :, 0:T]
        bg = self.bGS if sample else self.bG
        self.norm_mod(xt, bx, T, s, hs, bh, sample, gs[:, 0:KD, :], bg)
        for j in range(NJ):
            pa, bpa = self.bank()
            pbk, bpb = self.bank()
            for k in range(KD):
                P.op("pe", lambda e, j=j, k=k, pa=pa: e.matmul(pa[:, 0:T], gu[:, k, j * 128:(j + 1) * 128], hs[:, k, :],
                                                               start=(k == 0), stop=(k == KD - 1)),
                     r=[self.bWgu[j // 2], bh], w=[bpa])
            for k in range(KD):
                P.op("pe", lambda e, j=j, k=k, pbk=pbk: e.matmul(pbk[:, 0:T], gu[:, k, DFF + j * 128:DFF + (j + 1) * 128],
                                                                 hs[:, k, :], start=(k == 0), stop=(k == KD - 1)),
                     r=[self.bWgu[j // 2], bh], w=[bpb])
            P.op("act", lambda e, pa=pa: e.activation(out=self.TMP[:, 0:T], in_=pa[:, 0:T], func=AF.Silu),
                 r=[bpa], w=[self.bTMP])
            P.op("dve", lambda e, j=j, pbk=pbk: e.tensor_tensor(out=gs[:, j, :], in0=self.TMP[:, 0:T], in1=pbk[:, 0:T],
                                                                op=ALU.mult), r=[self.bTMP, bpb], w=[bg])
        for d in range(KD):
            py, bpy = self.bank()
            for j in range(NJ):
                P.op("pe", lambda e, d=d, j=j, py=py: e.matmul(py[:, 0:T], dn[:, j, d * 128:(d + 1) * 128], gs[:, j, :],
                                                               start=(j == 0), stop=(j == NJ - 1)),
                     r=[self.bWdn[j // 2], bg], w=[bpy])
            self.resid(xt, bx, T, s, d, py, bpy, sample)

    def resid(self, xt, bx, T, s, d, py, bpy, sample):
        P = self.P
        if not sample:
            P.op("dve", lambda e: e.scalar_tensor_tensor(
                out=xt[:, d, :], in0=py[:, 0:T], scalar=self.MG(s)[:, d, 0:1], in1=xt[:, d, :],
                op0=ALU.mult, op1=ALU.add), r=[bpy, self.bMODS, bx], w=[bx])
        else:
            P.op("dve", lambda e: e.tensor_tensor(out=self.TMP[:, 0:T], in0=py[:, 0:T],
                                                  in1=self.MG(s)[:, d, 1:1 + NS], op=ALU.mult),
                 r=[bpy, self.bMODS], w=[self.bTMP])
            P.op("dve", lambda e: e.tensor_tensor(out=xt[:, d, :], in0=xt[:, d, :], in1=self.TMP[:, 0:T],
                                                  op=ALU.add), r=[self.bTMP, bx], w=[bx])

    def ffn(self, l, s, src_is_input):
        P = self.P
        gu, dn = self.load_ffn_weights(l, s)
        src = (self.xT if src_is_input else self.yT).rearrange("(k p) t -> p k t", p=128)
        dst = self.yT.rearrange("(k p) t -> p k t", p=128)

        def load(i):
            rd = [] if src_is_input else [self.bYT[i]]
            P.dma("sp", self.X[i % 2][:], src[:, :, i * TT:(i + 1) * TT], r=rd, w=[self.bX[i % 2]], sem="xl")

        load(0)
        for i in range(self.ntile):
            if i + 1 < self.ntile:
                load(i + 1)
            self.ffn_tile(self.X[i % 2][:], self.bX[i % 2], TT, s, gu, dn, sample=False)
            P.dma("sp", dst[:, :, i * TT:(i + 1) * TT], self.X[i % 2][:], r=[self.bX[i % 2]], w=[self.bYT[i]], sem="xs")
        self.ffn_tile(self.XS[:], self.bXS, NS, s, gu, dn, sample=True)

    def mixer(self, l):
        P = self.P
        self.segs = [[15392, 33792], [O_X1, O_X1 + 4096], [O_G, O_G + 5632]]
        mk = self.mk
        self.rr_set = [3, 4, 5, 6, 7]
        WIN = self.view(O_WIN, 11296, BF16, [KD, IN_DIM])
        WOUT = self.view(O_WOUT, 4096, BF16, [KD, D])
        bWI, bWO = Buf("WIN"), Buf("WOUT")
        X0, bX0 = self.X[0], self.bX[0]
        H, bH = self.H, self.bH
        R, bR, TMP, bTMP = self.R, self.bR, self.TMP, self.bTMP
        CF, bCF = mk("CF", [NCF])
        CB, bCB = mk("CB", [NCB], BF16)
        BCP, bBCP = mk("BCP", [NBC])
        SC, bSC = mk("SC", [96])
        bSCc, bSCs = Buf("SCconst"), Buf("SCswa")
        segs_sample = [list(x) for x in self.segs]
        XP, bXP = mk("XP", [NCC, 3 + TT])
        QT, bQT = mk("QT", [DN_H, TT], BF16)
        KT, bKT = mk("KT", [DN_H, TT], BF16)
        VT, bVT = mk("VT", [DN_H, TT], BF16)
        CAT, bCAT = mk("CAT", [KD, TT], BF16)
        ROPE, bROPE = mk("ROPE", [4, 64])
        SQB, bSQB = mk("SQB", [TT], BF16)
        GU, bGU = mk("GU", [DN_H, 128])
        ET, bET = mk("ET", [DN_H, 128])
        EDS, bEDS = mk("EDS", [DN_H, 128])
        EGs = [mk("EGROW0", [DN_H, 128], BF16), mk("EGROW1", [DN_H, 128], BF16)]
        ONB, bONB = mk("ONB", [DN_H, 128], BF16)
        SC1, bSC1 = mk("SC1", [64])
        S32, bS32 = mk("S32", [DN_H, 128])
        bfn = ("DT", "DS", "A", "B", "QKT", "A0", "B0", "X2", "Y2", "T", "U", "NWR", "NWU",
               "KBG", "KDD", "VB", "NWT", "QDT", "VN", "SBF", "ON2", "ZG", "ZG1", "DT1", "DS1")
        bft = {}
        for nm in bfn:
            bft[nm] = mk(nm, [DN_H, 128], BF16)
        QS, bQS = mk("QS", [SW_H, DH])
        KS, bKS = mk("KS", [SW_KV, DH])
        KR, bKR = mk("KR", [SW_KV, DH])
        VR, bVR = mk("VR", [SW_KV, DH])
        QF, bQF = mk("QF", [4, 2, DH], BF16)
        KF, bKF = mk("KF", [SW_KV, DH], BF16)
        QTS, bQTS = mk("QTS", [4, 128], BF16)
        KTS, bKTS = mk("KTS", [2, 128], BF16)
        VE, bVE = mk("VE", [2, SW_KV, DH + 1], BF16)
        PT, bPT = {}, {}
        for hk in range(2):
            for kb in range(2):
                PT[hk, kb], bPT[hk, kb] = mk("PT%d%d" % (hk, kb), [4, 128], BF16)
        OSB, bOSB = mk("OSB", [SW_H, DH], BF16)

        UT = CF[:, 0:128]
        ONESF = CF[:, 128:256]
        NEGMT = bc(CF[:, 256:257], [[0, DN_H], [1, 128]])
        NEGMS = bc(CF[:, 384:385], [[0, DN_H], [1, 128]])
        h4 = lambda v: bc(v[:, 0:1], [[0, DN_H], [1, 128]])
        MD8 = h4(CB[:, 1152:1280])
        MLm = [h4(CB[:, 1280 + li * 128:1408 + li * 128]) for li in range(4)]
        MUm = [h4(CB[:, 1792 + li * 128:1920 + li * 128]) for li in range(4)]
        IDB4 = h4(CB[:, 0:128])
        IDB = CB[:, 0:128]
        MASKC = CB[:, 128:640]
        MASKP = CB[:, 640:1152]
        ALOG, DTB, SINK = BCP[:, 0:4], BCP[:, 4:8], BCP[:, 8:16]
        NORMW4, QNW8, KNW2 = BCP[:, 16:528], BCP[:, 528:1040], BCP[:, 1040:1168]
        NEGA, ESINK = SC[:, 64:68], SC[:, 68:76]
        sc = lambda i, n=4: SC[:, i * 4:i * 4 + n]
        SCV = [[sc(i) for i in range(16)], [SC1[:, i * 4:i * 4 + 4] for i in range(16)]]
        bSCp = [bSC, bSC1]
        DTs = [bft["DT"], bft["DT1"]]
        DSs = [bft["DS"], bft["DS1"]]
        SSQ10, RS10, DEN8 = SC[:, 76:86], SC[:, 86:96], SC[:, 96 - 8:96]
        DEN8, bDEN = mk("DEN8", [16])

        winv = self.w_in[l].rearrange("(k p) n -> p k n", p=128)
        woutv = self.w_out[l].rearrange("(k p) n -> p k n", p=128)
        for k in range(KD):
            P.dma("pool", WIN[:, k, :], winv[:, k, :], w=[bWI], sem="wl")
        P.dma("pool", WOUT[:, 0:4, :], woutv[:, 0:4, :], w=[bWO], sem="wl")
        P.dma("pool", WOUT[:, 4:8, :], woutv[:, 4:8, :], w=[bWO], sem="wl")
        P.dma("pool", CB, self.constb, w=[bCB], sem="wl")
        P.dma("sp", CF, self.constf, w=[bCF], sem="ld")
        P.dma("sp", BCP, self.bcp[l], w=[bBCP], sem="ld")
        P.op("act", lambda e: e.activation(out=NEGA, in_=ALOG, func=AF.Exp), r=[bBCP], w=[bSCc])
        P.op("dve", lambda e: e.tensor_scalar(out=NEGA, in0=NEGA, scalar1=-1.0, scalar2=None, op0=ALU.mult),
             r=[bSCc], w=[bSCc])
        P.op("act", lambda e: e.activation(out=ESINK, in_=SINK, func=AF.Exp), r=[bBCP], w=[bSCc])
        P.op("dve", lambda e: e.tensor_scalar(out=KNW2, in0=KNW2, scalar1=float(math.sqrt(DH)), scalar2=None,
                                              op0=ALU.mult), r=[bBCP], w=[bBCP])
        NORMWB, bNWB = mk("NORMWB", [DN_H * DV], BF16)
        P.op("act", lambda e: e.activation(out=NORMWB, in_=NORMW4, func=AF.Copy), r=[bBCP], w=[bNWB])
        P.op("dve", lambda e: e.memset(S32, 0.0), w=[bS32])
        P.op("dve", lambda e: e.memset(bft["SBF"][0], 0.0), w=[bft["SBF"][1]])
        P.op("dve", lambda e: e.memset(XP[:, :, 0:3], 0.0), w=[bXP])
        P.op("dve", lambda e: e.memset(VE[:, :, :, DH:DH + 1], 1.0), w=[bVE])

        src = self.yT.rearrange("(k p) t -> p k t", p=128)
        def tile(i):
            P.dma("sp", X0, src[:, :, i * TT:(i + 1) * TT], r=[self.bYT[i]], w=[bX0], sem="xl")
            P.dma("sp", ROPE, self.ropeT[:, i * 4:(i + 1) * 4, :], w=[bROPE], sem="xl")
            self.norm_mod(X0, bX0, TT, 1, H, bH, False, CAT, bCAT)
            for c in range(NCC):
                pp, bpp = self.bank()
                for k in range(KD):
                    P.op("pe", lambda e, c=c, k=k, pp=pp: e.matmul(pp[:, :], WIN[:, k, c * 128:(c + 1) * 128], H[:, k, :],
                                                                   start=(k == 0), stop=(k == KD - 1)),
                         r=[bWI, bH], w=[bpp])
                P.op("act", lambda e, c=c, pp=pp: e.activation(out=XP[:, c, 3:3 + TT], in_=pp[:, :], func=AF.Copy),
                     r=[bpp], w=[bXP])
                cw = lambda j, c=c: self.SMALL[:, 96 + c * 4 + j:97 + c * 4 + j]
                acc, bacc = (TMP, bTMP) if c % 2 == 0 else (R, bR)
                ceng = "dve"
                P.op(ceng, lambda e, c=c, cw=cw, acc=acc: e.tensor_scalar(out=acc, in0=XP[:, c, 0:TT], scalar1=cw(0),
                                                                           scalar2=None, op0=ALU.mult),
                     r=[bXP, self.bSMALL], w=[bacc])
                for j in range(1, 4):
                    P.op(ceng, lambda e, c=c, j=j, cw=cw, acc=acc: e.scalar_tensor_tensor(
                        out=acc, in0=XP[:, c, j:j + TT], scalar=cw(j), in1=acc, op0=ALU.mult, op1=ALU.add),
                        r=[bXP, self.bSMALL, bacc], w=[bacc])
                dst, bdst = ((QT, bQT), (KT, bKT), (VT, bVT))[c // 4]
                P.op("act", lambda e, c=c, dst=dst, acc=acc: e.activation(out=dst[:, c % 4, :], in_=acc, func=AF.Silu),
                     r=[bacc], w=[bdst])
            for c in range(8):
                isq = c < 4
                h = c % 4
                dst, bdst = (QT, bQT) if isq else (KT, bKT)
                P.op("act", lambda e, dst=dst, h=h: e.activation(out=SQB, in_=dst[:, h, :], func=AF.Square),
                     r=[bdst], w=[bSQB])
                ps, bps = self.bank()
                P.op("pe", lambda e, ps=ps: e.matmul(ps[:, :], self.ONES[:], SQB, start=True, stop=True),
                     r=[self.bONES, bSQB], w=[bps])
                P.op("act", lambda e, ps=ps, isq=isq: e.activation(
                    out=TMP, in_=ps[:, :], func=AF.Ln, scale=(float(DK) if isq else 1.0),
                    bias=(self.EPSD[:, 1:2] if isq else self.EPSD[:, 4:5])), r=[bps, self.bONES], w=[bTMP])
                P.op("act", lambda e: e.activation(out=TMP, in_=TMP, func=AF.Exp, scale=-0.5), r=[bTMP], w=[bTMP])
                P.op("dve", lambda e, dst=dst, h=h: e.tensor_tensor(out=dst[:, h, :], in0=dst[:, h, :], in1=TMP, op=ALU.mult),
                     r=[bdst, bTMP], w=[bdst])
            if i == self.ntile - 1 and "outs" not in SKIP:
                P.dma("sp", self.o_conv_p[l].rearrange("(c p) j -> p c j", p=128), XP[:, :, TT:TT + 3],
                      r=[bXP], sem="st")
            P.op("act", lambda e: e.activation(out=XP[:, :, 0:3], in_=XP[:, :, TT:TT + 3], func=AF.Copy),
                 r=[bXP], w=[bXP])

            PZ, bPZ = self.PB[0], self.bPB[0]
            PQ, bPQ = self.PB[1], self.bPB[1]
            PKV, bPKV = self.PB[2], self.bPB[2]
            ZGs = [bft["ZG"], bft["ZG1"]]

            def proj(bb):
                tsl_ = slice(bb * 128, bb * 128 + 128)
                for (pt, bpt, c0, ncol, o0) in ((PZ, bPZ, C_Z, 512, 0), (PQ, bPQ, C_Q, 512, 0),
                                                (PKV, bPKV, C_K, 256, 0), (PKV, bPKV, C_BA, 8, 256)):
                    for k in range(KD):
                        P.op("pe", lambda e, pt=pt, c0=c0, ncol=ncol, o0=o0, k=k: e.matmul(
                            pt[:, o0:o0 + ncol], H[:, k, tsl_], WIN[:, k, c0:c0 + ncol],
                            start=(k == 0), stop=(k == KD - 1)), r=[bWI, bH], w=[bpt])
                if "dn" not in SKIP:
                    ZG0, bZG0 = ZGs[(4 * i + bb) % 2]
                    P.op("act", lambda e: e.activation(out=ZG0.rearrange("p h i -> p (h i)"), in_=PZ[:, :], func=AF.Silu),
                         r=[bPZ], w=[bZG0])
                    P.op("pool", lambda e: e.tensor_tensor(out=ZG0.rearrange("p h i -> p (h i)"),
                                                           in0=ZG0.rearrange("p h i -> p (h i)"), in1=NORMWB, op=ALU.mult),
                         r=[bZG0, bNWB], w=[bZG0])

            def dn_pre(bb):
                par_ = (4 * i + bb) % 2
                (BETA, XA_, AXV, E1, L1, G4, GC, NGC, BGC, LNB, EGC, EKD, BEG, EGL, RS4, SSQ4) = SCV[par_]
                bSCx = bSCp[par_]
                DT_, bDT = DTs[par_]
                DS_, bDS = DSs[par_]
                EGROW, bEGROW = EGs[par_]
                PB_ = PKV[:, 256:260]
                PA_ = PKV[:, 260:264]
                P.op("act", lambda e: e.activation(out=BETA, in_=PB_, func=AF.Exp, scale=-1.0), r=[bPKV], w=[bSCx])
                P.op("dve", lambda e: e.tensor_scalar(out=BETA, in0=BETA, scalar1=1.0, scalar2=None, op0=ALU.add),
                     r=[bSCx], w=[bSCx])
                P.op("act", lambda e: e.activation(out=LNB, in_=BETA, func=AF.Ln), r=[bSCx], w=[bSCx])
                P.op("dve", lambda e: e.reciprocal(out=BETA, in_=BETA), r=[bSCx], w=[bSCx])
                P.op("dve", lambda e: e.tensor_tensor(out=XA_, in0=PA_, in1=DTB, op=ALU.add), r=[bPKV, bBCP], w=[bSCx])
                P.op("act", lambda e: e.activation(out=AXV, in_=XA_, func=AF.Abs), r=[bSCx], w=[bSCx])
                P.op("act", lambda e: e.activation(out=E1, in_=AXV, func=AF.Exp, scale=-1.0), r=[bSCx], w=[bSCx])
                P.op("act", lambda e: e.activation(out=L1, in_=E1, func=AF.Ln, bias=self.EPSD[:, 3:4]),
                     r=[bSCx, self.bONES], w=[bSCx])
                P.op("dve", lambda e: e.scalar_tensor_tensor(out=G4, in0=XA_, scalar=0.0, in1=L1, op0=ALU.max,
                                                             op1=ALU.add), r=[bSCx], w=[bSCx])
                P.op("dve", lambda e: e.tensor_tensor(out=G4, in0=G4, in1=NEGA, op=ALU.mult), r=[bSCx, bSCc], w=[bSCx])
                pg, bpg = self.bank()
                P.op("pe", lambda e, pg=pg: e.matmul(pg[:, 0:4], UT, G4, start=True, stop=True), r=[bCF, bSCx], w=[bpg])
                P.op("dve", lambda e, pg=pg: e.tensor_copy(out=GC, in_=pg[:, 0:4]), r=[bpg], w=[bSCx])
                P.op("dve", lambda e: e.tensor_tensor(out=GU, in0=bc(UT, [[0, DN_H], [1, 128]]),
                                                      in1=bc(G4, [[1, DN_H], [0, 128]]), op=ALU.mult),
                     r=[bCF, bSCx], w=[bGU])
                pr, bpr = self.bank()
                prv = pr[:, :].rearrange("p (h i) -> p h i", h=DN_H)
                P.op("pe", lambda e, pr=pr: e.matmul(pr[:, :], ONESF, GU.rearrange("p h i -> p (h i)"),
                                                     start=True, stop=True), r=[bCF, bGU], w=[bpr])
                P.op("dve", lambda e: e.tensor_scalar(out=NGC, in0=GC, scalar1=-1.0, scalar2=None, op0=ALU.mult),
                     r=[bSCx], w=[bSCx])
                P.op("dve", lambda e: e.tensor_tensor(out=BGC, in0=GC, in1=LNB, op=ALU.subtract), r=[bSCx], w=[bSCx])
                P.op("act", lambda e: e.activation(out=EGC, in_=GC, func=AF.Exp), r=[bSCx], w=[bSCx])
                P.op("dve", lambda e: e.tensor_tensor(out=BEG, in0=BETA, in1=EGC, op=ALU.mult), r=[bSCx], w=[bSCx])
                P.op("act", lambda e, prv=prv: e.activation(out=EGL, in_=prv[:, :, 127], func=AF.Exp), r=[bpr], w=[bSCx])
                P.op("dve", lambda e, prv=prv: e.tensor_tensor(out=EKD, in0=prv[:, :, 127], in1=GC, op=ALU.subtract),
                     r=[bpr, bSCx], w=[bSCx])
                P.op("act", lambda e: e.activation(out=EKD, in_=EKD, func=AF.Exp), r=[bSCx], w=[bSCx])
                P.op("act", lambda e, pr=pr: e.activation(out=EGROW.rearrange("p h i -> p (h i)"), in_=pr[:, :],
                                                          func=AF.Exp), r=[bpr], w=[bEGROW])
                P.op("dve", lambda e, prv=prv: e.tensor_tensor(out=ET, in0=prv, in1=NEGMT, op=ALU.add),
                     r=[bpr, bCF], w=[bET])
                P.op("dve", lambda e, prv=prv: e.scalar_tensor_tensor(
                    out=EDS, in0=prv, scalar=-1.0, in1=NEGMS, op0=ALU.mult, op1=ALU.add), r=[bpr, bCF], w=[bEDS])
                for h in range(DN_H):
                    P.op("act", lambda e, h=h: e.activation(out=DT_[:, h, :], in_=ET[:, h, :], func=AF.Exp,
                                                            bias=NGC[:, h:h + 1]), r=[bET, bSCx], w=[bDT])
                    P.op("act", lambda e, h=h: e.activation(out=DS_[:, h, :], in_=EDS[:, h, :], func=AF.Exp,
                                                            bias=BGC[:, h:h + 1]), r=[bEDS, bSCx], w=[bDS])

            def block(b):
                n = 4 * i + b
                t0 = b * 128
                tsl = slice(t0, t0 + 128)
                par = n % 2
                if b == 0:
                    proj(0)
                    if "dn" not in SKIP:
                        dn_pre(0)
                fl = lambda v: v.rearrange("p h i -> p (h i)")
                v3 = lambda v: v.rearrange("p (h i) -> p h i", h=DN_H)
                hb = lambda v: bc(v, [[1, DN_H], [0, 128]])

                def dn_gen():
                    (BETA, XA_, AXV, E1, L1, G4, GC, NGC, BGC, LNB, EGC, EKD, BEG, EGL, RS4, SSQ4) = SCV[n % 2]
                    bSCx = bSCp[n % 2]
                    DT_, bDT = DTs[n % 2]
                    DS_, bDS = DSs[n % 2]
                    EGROW, bEGROW = EGs[n % 2]
                    pkk, bpkk = self.bank()
                    pkq, bpkq = self.bank()
                    for h in range(DN_H):
                        P.op("pe", lambda e, h=h, pkk=pkk: e.matmul(pkk[:, h * 128:(h + 1) * 128], KT[:, h, tsl], KT[:, h, tsl],
                                                                    start=True, stop=True), r=[bKT], w=[bpkk])
                    for h in range(DN_H):
                        P.op("pe", lambda e, h=h, pkq=pkq: e.matmul(pkq[:, h * 128:(h + 1) * 128], KT[:, h, tsl], QT[:, h, tsl],
                                                                    start=True, stop=True), r=[bKT, bQT], w=[bpkq])
                    A_, bA = bft["A"]
                    B_, bB = bft["B"]
                    QKT, bQKT = bft["QKT"]
                    fl = lambda v: v.rearrange("p h i -> p (h i)")
                    P.op("dve", lambda e, pkk=pkk: e.tensor_tensor(out=fl(A_), in0=pkk[:, :], in1=fl(DS_), op=ALU.mult),
                         r=[bpkk, bDS], w=[bA])
                    P.op("dve", lambda e, pkq=pkq: e.tensor_tensor(out=fl(QKT), in0=pkq[:, :], in1=fl(DT_), op=ALU.mult),
                         r=[bpkq, bDT], w=[bQKT])
                    yield
                    ptb, bptb = self.bank()
                    ptbv = ptb[:, 0:256].bitcast(BF16)
                    for h in range(DN_H):
                        P.op("pe", lambda e, h=h, ptbv=ptbv: e.transpose(ptbv[:, h * 128:(h + 1) * 128], A_[:, h, :], IDB),
                             r=[bA, bCB], w=[bptb])
                    P.op("act", lambda e, ptbv=ptbv: e.activation(out=fl(B_), in_=ptbv, func=AF.Copy), r=[bptb], w=[bB])
                    if DNCUT <= 3:
                        return
                    ptk, bptk = self.bank()
                    ptkv = ptk[:, 0:256].bitcast(BF16)
                    ptv, bptv = self.bank()
                    ptvv = ptv[:, 0:256].bitcast(BF16)
                    for h in range(DN_H):
                        P.op("pe", lambda e, h=h, ptkv=ptkv: e.transpose(ptkv[:, h * 128:(h + 1) * 128], KT[:, h, tsl], IDB),
                             r=[bKT, bCB], w=[bptk])
                    for h in range(DN_H):
                        P.op("pe", lambda e, h=h, ptvv=ptvv: e.transpose(ptvv[:, h * 128:(h + 1) * 128], VT[:, h, tsl], IDB),
                             r=[bVT, bCB], w=[bptv])
                    KBG, bKBG = bft["KBG"]
                    KDD, bKDD = bft["KDD"]
                    VB, bVB = bft["VB"]
                    hb = lambda v: bc(v, [[1, DN_H], [0, 128]])
                    v3 = lambda v: v.rearrange("p (h i) -> p h i", h=DN_H)
                    P.op("dve", lambda e, ptkv=ptkv: e.tensor_tensor(out=KBG, in0=v3(ptkv), in1=hb(BEG), op=ALU.mult),
                         r=[bptk, bSCx], w=[bKBG])
                    P.op("dve", lambda e, ptkv=ptkv: e.tensor_tensor(out=KDD, in0=v3(ptkv), in1=hb(EKD), op=ALU.mult),
                         r=[bptk, bSCx], w=[bKDD])
                    P.op("dve", lambda e, ptvv=ptvv: e.tensor_tensor(out=VB, in0=v3(ptvv), in1=hb(BETA), op=ALU.mult),
                         r=[bptv, bSCx], w=[bVB])
                    QDT, bQDT = bft["QDT"]
                    P.op("dve", lambda e: e.tensor_tensor(out=QDT, in0=QT[:, :, tsl], in1=EGROW, op=ALU.mult),
                         r=[bQT, bEGROW], w=[bQDT])
                    yield
                    A0, bA0 = bft["A0"]
                    B0, bB0 = bft["B0"]
                    X2, bX2 = bft["X2"]
                    Y2, bY2 = bft["Y2"]
                    T_, bT_ = bft["T"]
                    U_, bU_ = bft["U"]
                    NWR, bNWR = bft["NWR"]
                    NWU, bNWU = bft["NWU"]
                    P.op("dve", lambda e: e.tensor_tensor(out=A0, in0=A_, in1=MD8, op=ALU.mult), r=[bA, bCB], w=[bA0])
                    P.op("dve", lambda e: e.tensor_tensor(out=B0, in0=B_, in1=MD8, op=ALU.mult), r=[bB, bCB], w=[bB0])
                    P.op("dve", lambda e: e.scalar_tensor_tensor(out=T_, in0=A0, scalar=-1.0, in1=IDB4, op0=ALU.mult,
                                                                 op1=ALU.add), r=[bA0, bCB], w=[bT_])
                    P.op("dve", lambda e: e.scalar_tensor_tensor(out=U_, in0=B0, scalar=-1.0, in1=IDB4, op0=ALU.mult,
                                                                 op1=ALU.add), r=[bB0, bCB], w=[bU_])

                    def mm4(lhs, blhs, rhs, brhs, pre=None):
                        pm, bpm = self.bank()
                        for h in range(DN_H):
                            if pre is not None:
                                P.op("pe", lambda e, h=h, pm=pm: e.matmul(pm[:, h * 128:(h + 1) * 128], IDB, pre[0][:, h, :],
                                                                          start=True, stop=False), r=[bCB, pre[1]], w=[bpm])
                            P.op("pe", lambda e, h=h, pm=pm: e.matmul(pm[:, h * 128:(h + 1) * 128], lhs[:, h, :], rhs[:, h, :],
                                                                      start=(pre is None), stop=True), r=[blhs, brhs], w=[bpm])
                        return pm, bpm

                    def evac(eng, pm, bpm, dst, bdst, neg=False):
                        if eng == "act":
                            P.op("act", lambda e: e.activation(out=fl(dst), in_=pm[:, :], func=AF.Identity,
                                                               scale=(-1.0 if neg else 1.0)), r=[bpm], w=[bdst])
                        else:
                            P.op("dve", lambda e: e.tensor_scalar(out=fl(dst), in0=pm[:, :], scalar1=(-1.0 if neg else 1.0),
                                                                  scalar2=None, op0=ALU.mult), r=[bpm], w=[bdst])
                    yield
                    px_ = mm4(B0, bB0, A0, bA0)
                    py2 = mm4(A0, bA0, B0, bB0)
                    evac("act", *px_, X2, bX2)
                    evac("dve", *py2, Y2, bY2)
                    yield
                    pt_ = mm4(Y2, bY2, T_, bT_, pre=(T_, bT_))
                    pu_ = mm4(X2, bX2, U_, bU_, pre=(U_, bU_))
                    evac("act", *pt_, T_, bT_)
                    evac("dve", *pu_, U_, bU_)
                    yield
                    px_ = mm4(Y2, bY2, X2, bX2)
                    py2 = mm4(X2, bX2, Y2, bY2)
                    evac("act", *px_, A0, bA0)
                    evac("dve", *py2, B0, bB0)
                    yield
                    pt_ = mm4(B0, bB0, T_, bT_, pre=(T_, bT_))
                    pu_ = mm4(A0, bA0, U_, bU_, pre=(U_, bU_))
                    evac("act", *pt_, T_, bT_)
                    evac("dve", *pu_, U_, bU_)
                    yield
                    def evacm(pm, bpm, dst, bdst, mask):
                        P.op("dve", lambda e: e.scalar_tensor_tensor(out=dst, in0=v3(pm[:, :]), scalar=-1.0, in1=mask,
                                                                     op0=ALU.mult, op1=ALU.mult), r=[bpm, bCB], w=[bdst])
                    for li in range(4):
                        last = li == 3
                        pwu = mm4(A_, bA, U_, bU_)
                        if not last:
                            pwr = mm4(B_, bB, T_, bT_)
                            evacm(*pwr, NWR, bNWR, MLm[li])
                        evacm(*pwu, NWU, bNWU, MUm[li])
                        yield
                        pu_ = mm4(T_, bT_, NWU, bNWU, pre=(U_, bU_))
                        if not last:
                            pt_ = mm4(U_, bU_, NWR, bNWR, pre=(T_, bT_))
                        evac("dve", *pu_, U_, bU_)
                        if not last:
                            evac("act", *pt_, T_, bT_)
                        yield
                    PBF, bPBF = U_, bU_
                    if DNCUT <= 4:
                        return
                    yield
                    pw, bpw = self.bank()
                    for h in range(DN_H):
                        P.op("pe", lambda e, h=h, pw=pw: e.matmul(pw[:, h * 128:(h + 1) * 128], KBG[:, h, :], PBF[:, h, :],
                                                                  start=True, stop=True), r=[bKBG, bPBF], w=[bpw])
                    NWT, bNWT = bft["NWT"]
                    P.op("act", lambda e, pw=pw: e.activation(out=fl(NWT), in_=pw[:, :], func=AF.Identity, scale=-1.0),
                         r=[bpw], w=[bNWT])
                    SBF, bSBF = bft["SBF"]
                    VN, bVN = bft["VN"]
                    pvn, bpvn = self.bank()
                    for h in range(DN_H):
                        P.op("pe", lambda e, h=h, pvn=pvn: e.matmul(pvn[:, h * 128:(h + 1) * 128], PBF[:, h, :], VB[:, h, :],
                                                                    start=True, stop=False), r=[bPBF, bVB], w=[bpvn])
                        P.op("pe", lambda e, h=h, pvn=pvn: e.matmul(pvn[:, h * 128:(h + 1) * 128], NWT[:, h, :], SBF[:, h, :],
                                                                    start=False, stop=True), r=[bNWT, bSBF], w=[bpvn])
                    P.op("act", lambda e, pvn=pvn: e.activation(out=fl(VN), in_=pvn[:, :], func=AF.Copy), r=[bpvn], w=[bVN])
                    yield
                    po, bpo = self.bank()
                    for h in range(DN_H):
                        P.op("pe", lambda e, h=h, po=po: e.matmul(po[:, h * 128:(h + 1) * 128], QDT[:, h, :], SBF[:, h, :],
                                                                  start=True, stop=False), r=[bQDT, bSBF], w=[bpo])
                        P.op("pe", lambda e, h=h, po=po: e.matmul(po[:, h * 128:(h + 1) * 128], QKT[:, h, :], VN[:, h, :],
                                                                  start=False, stop=True), r=[bQKT, bVN], w=[bpo])
                    psu, bpsu = self.bank()
                    for h in range(DN_H):
                        P.op("pe", lambda e, h=h, psu=psu: e.matmul(psu[:, h * 128:(h + 1) * 128], KDD[:, h, :], VN[:, h, :],
                                                                    start=True, stop=True), r=[bKDD, bVN], w=[bpsu])
                    P.op("dve", lambda e: e.tensor_tensor(out=S32, in0=S32, in1=hb(EGL), op=ALU.mult),
                         r=[bS32, bSCx], w=[bS32])
                    P.op("dve", lambda e, psu=psu: e.tensor_tensor(out=fl(S32), in0=fl(S32), in1=psu[:, :], op=ALU.add),
                         r=[bS32, bpsu], w=[bS32])
                    P.op("act", lambda e: e.activation(out=fl(SBF), in_=fl(S32), func=AF.Copy), r=[bS32], w=[bSBF])
                    if DNCUT <= 6:
                        return
                    yield
                    ZG, bZG = ZGs[n % 2]
                    P.op("act", lambda e, po=po: e.activation(out=SQB, in_=po[:, :], func=AF.Square), r=[bpo], w=[bSQB])
                    P.op("dve", lambda e: e.tensor_reduce(out=SSQ4, in_=v3(SQB), axis=AX.X, op=ALU.add), r=[bSQB], w=[bSCx])
                    P.op("act", lambda e: e.activation(out=RS4, in_=SSQ4, func=AF.Ln, scale=1.0 / DV,
                                                       bias=self.EPSD[:, 4:5]), r=[bSCx, self.bONES], w=[bSCx])
                    P.op("act", lambda e: e.activation(out=RS4, in_=RS4, func=AF.Exp, scale=-0.5), r=[bSCx], w=[bSCx])
                    P.op("dve", lambda e, po=po: e.tensor_tensor(out=ONB, in0=v3(po[:, :]), in1=hb(RS4), op=ALU.mult),
                         r=[bpo, bSCx], w=[bONB])
                    ON2, bON2 = bft["ON2"]
                    P.op("dve", lambda e: e.tensor_tensor(out=ON2, in0=ONB, in1=ZG, op=ALU.mult), r=[bONB, bZG], w=[bON2])
                    yield
                    pto, bpto = self.bank()
                    ptov = pto[:, 0:256].bitcast(BF16)
                    for h in range(DN_H):
                        P.op("pe", lambda e, h=h, ptov=ptov: e.transpose(ptov[:, h * 128:(h + 1) * 128], ON2[:, h, :], IDB),
                             r=[bON2, bCB], w=[bpto])
                    P.op("act", lambda e, ptov=ptov: e.activation(out=CAT[:, 0:4, tsl], in_=v3(ptov), func=AF.Copy),
                         r=[bpto], w=[bCAT])

                def swa_gen():
                    v8 = lambda v: v.rearrange("p (h d) -> p h d", h=SW_H)
                    v2 = lambda v: v.rearrange("p (h d) -> p h d", h=SW_KV)
                    P.op("act", lambda e, PQ=PQ: e.activation(out=TMP, in_=PQ[:, :], func=AF.Square), r=[bPQ], w=[bTMP])
                    P.op("dve", lambda e: e.tensor_reduce(out=SSQ10[:, 0:8], in_=v8(TMP), axis=AX.X, op=ALU.add),
                         r=[bTMP], w=[bSCs])
                    P.op("act", lambda e, PKV=PKV: e.activation(out=R[:, 0:128], in_=PKV[:, 0:128], func=AF.Square),
                         r=[bPKV], w=[bR])
                    P.op("dve", lambda e: e.tensor_reduce(out=SSQ10[:, 8:10], in_=v2(R[:, 0:128]), axis=AX.X, op=ALU.add),
                         r=[bR], w=[bSCs])
                    P.op("act", lambda e: e.activation(out=RS10, in_=SSQ10, func=AF.Ln, bias=self.EPSD[:, 2:3]),
                         r=[bSCs, self.bONES], w=[bSCs])
                    P.op("act", lambda e: e.activation(out=RS10, in_=RS10, func=AF.Exp, scale=-0.5), r=[bSCs], w=[bSCs])
                    yield
                    P.op("dve", lambda e, PQ=PQ: e.tensor_tensor(out=QS, in0=v8(PQ[:, :]), in1=bc(RS10[:, 0:8], [[1, 8], [0, DH]]),
                                                                 op=ALU.mult), r=[bPQ, bSCs], w=[bQS])
                    P.op("dve", lambda e: e.tensor_tensor(out=QS, in0=QS, in1=v8(QNW8), op=ALU.mult), r=[bQS, bBCP], w=[bQS])
                    P.op("dve", lambda e, PKV=PKV: e.tensor_tensor(out=KS, in0=v2(PKV[:, 0:128]),
                                                                   in1=bc(RS10[:, 8:10], [[1, 2], [0, DH]]), op=ALU.mult),
                         r=[bPKV, bSCs], w=[bKS])
                    P.op("dve", lambda e: e.tensor_tensor(out=KS, in0=KS, in1=v2(KNW2), op=ALU.mult), r=[bKS, bBCP], w=[bKS])
                    yield
                    def rope(x1, x2, cosb, sinb, t1, t2, d1, d2, bsrc, bdst_):
                        P.op("pool", lambda e: e.tensor_tensor(out=t1, in0=x1, in1=cosb, op=ALU.mult),
                             r=[bsrc, bROPE], w=[bTMP])
                        P.op("pool", lambda e: e.tensor_tensor(out=t2, in0=x2, in1=sinb, op=ALU.mult),
                             r=[bsrc, bROPE], w=[bTMP])
                        P.op("pool", lambda e: e.tensor_tensor(out=d1, in0=t1, in1=t2, op=ALU.subtract),
                             r=[bTMP], w=[bdst_])
                        P.op("pool", lambda e: e.tensor_tensor(out=t1, in0=x2, in1=cosb, op=ALU.mult),
                             r=[bsrc, bROPE, bTMP], w=[bTMP])
                        P.op("pool", lambda e: e.tensor_tensor(out=t2, in0=x1, in1=sinb, op=ALU.mult),
                             r=[bsrc, bROPE, bTMP], w=[bTMP])
                        P.op("pool", lambda e: e.tensor_tensor(out=d2, in0=t1, in1=t2, op=ALU.add),
                             r=[bTMP], w=[bdst_])
                    q4 = QS.rearrange("p (a b) d -> p a b d", a=2)
                    rope(q4[:, :, :, 0:32], q4[:, :, :, 32:64],
                         bc(ROPE[:, b, 0:1], [[0, 2], [0, 4], [1, 32]]), bc(ROPE[:, b, 32:33], [[0, 2], [0, 4], [1, 32]]),
                         TMP[:, 0:256].rearrange("p (a b d) -> p a b d", a=2, b=4),
                         TMP[:, 256:512].rearrange("p (a b d) -> p a b d", a=2, b=4),
                         bc(QF[:, 0, 0, 0:1], [[DH, 2], [2 * DH, 4], [1, 32]]),
                         bc(QF[:, 0, 0, 32:33], [[DH, 2], [2 * DH, 4], [1, 32]]), bQS, bQF)
                    rope(KS[:, :, 0:32], KS[:, :, 32:64],
                         bc(ROPE[:, b, 0:1], [[0, 2], [1, 32]]), bc(ROPE[:, b, 32:33], [[0, 2], [1, 32]]),
                         TMP[:, 0:64].rearrange("p (h d) -> p h d", h=2), TMP[:, 256:320].rearrange("p (h d) -> p h d", h=2),
                         KR[:, :, 0:32], KR[:, :, 32:64], bKS, bKR)
                    yield
                    P.op("act", lambda e: e.activation(out=KF, in_=KR, func=AF.Copy), r=[bKR], w=[bKF])
                    P.op("act", lambda e, PKV=PKV: e.activation(out=VE[:, par, :, 0:DH], in_=v2(PKV[:, 128:256]), func=AF.Copy),
                         r=[bPKV], w=[bVE])
                    if n == self.nblk - 1 and "outs" not in SKIP:
                        P.op("act", lambda e, PKV=PKV: e.activation(out=VR, in_=v2(PKV[:, 128:256]), func=AF.Copy),
                             r=[bPKV], w=[bVR])
                        P.dma("sp", self.o_k_p[l], KR.rearrange("p h d -> p (h d)"), r=[bKR], sem="st")
                        P.dma("sp", self.o_v_p[l], VR.rearrange("p h d -> p (h d)"), r=[bVR], sem="st")
                    yield
                    ptq, bptq = self.bank()
                    ptqv = ptq[:, 0:320].bitcast(BF16)
                    for g in range(4):
                        qin = QF[:, g, :, :].rearrange("p a d -> p (a d)")
                        P.op("pe", lambda e, g=g, qin=qin, ptqv=ptqv: e.transpose(ptqv[:, g * 128:(g + 1) * 128], qin, IDB),
                             r=[bQF, bCB], w=[bptq])
                    P.op("pe", lambda e, ptqv=ptqv: e.transpose(ptqv[:, 512:640], KF.rearrange("p h d -> p (h d)"), IDB),
                         r=[bKF, bCB], w=[bptq])
                    P.op("act", lambda e, ptqv=ptqv: e.activation(out=QTS.rearrange("p g t -> p (g t)"), in_=ptqv[:, 0:512],
                                                                  func=AF.Copy), r=[bptq], w=[bQTS])
                    P.op("dve", lambda e, ptqv=ptqv: e.tensor_copy(out=KTS[:, par, :], in_=ptqv[:, 512:640]),
                         r=[bptq], w=[bKTS])
                    yield
                    kbs = ((1, par, MASKC),) if n == 0 else ((0, 1 - par, MASKP), (1, par, MASKC))
                    pos_, bpos_ = [], []
                    for hk in range(SW_KV):
                        psl = slice(hk * DH, (hk + 1) * DH)
                        for (kb, kpar, msk) in kbs:
                            pscr, bpscr = self.bank()
                            P.op("pe", lambda e, pscr=pscr, psl=psl, kpar=kpar: e.matmul(
                                pscr[:, :], KTS[psl, kpar, :], QTS[psl, :, :].rearrange("p g t -> p (g t)"),
                                start=True, stop=False), r=[bKTS, bQTS], w=[bpscr])
                            P.op("pe", lambda e, pscr=pscr, msk=msk: e.matmul(pscr[:, :], IDB, msk, start=False, stop=True),
                                 r=[bCB], w=[bpscr])
                            P.op("act", lambda e, pscr=pscr, hk=hk, kb=kb: e.activation(
                                out=PT[hk, kb].rearrange("p g t -> p (g t)"), in_=pscr[:, :], func=AF.Exp),
                                r=[bpscr], w=[bPT[hk, kb]])
                        yield
                        posb, bposb = self.bank()
                        pov = posb[:, 0:4 * (DH + 1)].rearrange("p (g d) -> p g d", g=4)
                        for g in range(4):
                            for ii, (kb, kpar, msk) in enumerate(kbs):
                                P.op("pe", lambda e, pov=pov, g=g, hk=hk, kb=kb, kpar=kpar, ii=ii: e.matmul(
                                    pov[:, g, :], PT[hk, kb][:, g, :], VE[:, kpar, hk, :],
                                    start=(ii == 0), stop=(ii == len(kbs) - 1)),
                                    r=[bPT[hk, kb], bVE], w=[bposb])
                        P.op("dve", lambda e, pov=pov, hk=hk: e.tensor_tensor(
                            out=DEN8[:, hk * 4:(hk + 1) * 4], in0=pov[:, :, DH], in1=ESINK[:, hk * 4:(hk + 1) * 4], op=ALU.add),
                            r=[bposb, bSCc], w=[bDEN])
                        P.op("dve", lambda e, hk=hk: e.reciprocal(out=DEN8[:, 8 + hk * 4:8 + (hk + 1) * 4],
                                                                  in_=DEN8[:, hk * 4:(hk + 1) * 4]), r=[bDEN], w=[bDEN])
                        P.op("dve", lambda e, pov=pov, hk=hk: e.tensor_tensor(
                            out=OSB[:, hk * 4:(hk + 1) * 4, :], in0=pov[:, :, 0:DH],
                            in1=bc(DEN8[:, 8 + hk * 4:9 + hk * 4], [[1, 4], [0, DH]]), op=ALU.mult),
                            r=[bposb, bDEN], w=[bOSB])
                    yield
                    ptos, bptos = self.bank()
                    ptosv = ptos[:, 0:256].bitcast(BF16)
                    osf = OSB.rearrange("p h d -> p (h d)")
                    for c in range(4):
                        P.op("pe", lambda e, c=c, ptosv=ptosv: e.transpose(ptosv[:, c * 128:(c + 1) * 128],
                                                                           osf[:, c * 128:(c + 1) * 128], IDB),
                             r=[bOSB, bCB], w=[bptos])
                    P.op("dve", lambda e, ptosv=ptosv: e.tensor_copy(out=CAT[:, 4:8, tsl], in_=v3(ptosv)),
                         r=[bptos], w=[bCAT])
                    yield
                    if b < 3:
                        proj(b + 1)
                        if "dn" not in SKIP:
                            dn_pre(b + 1)

                gens = []
                if "dn" not in SKIP:
                    gens.append(dn_gen())
                if "swa" not in SKIP:
                    gens.append(swa_gen())
                if os.environ.get("KSEQ", "0") == "1":
                    for g_ in gens:
                        for _ in g_:
                            pass
                else:
                    DNR = int(os.environ.get("KDNR", "2"))
                    step = 0
                    while gens:
                        for gi, g_ in enumerate(list(gens)):
                            reps = DNR if (gi == 0 and len(gens) > 1) else 1
                            for _ in range(reps):
                                try:
                                    next(g_)
                                except StopIteration:
                                    if g_ in gens:
                                        gens.remove(g_)
                                    break
            for b in range(4):
                block(b)
            for d in range(KD):
                py, bpy = self.bank()
                for c in range(KD):
                    P.op("pe", lambda e, d=d, c=c, py=py: e.matmul(py[:, :], WOUT[:, c, d * 128:(d + 1) * 128], CAT[:, c, :],
                                                                   start=(c == 0), stop=(c == KD - 1)),
                         r=[bWO, bCAT], w=[bpy])
                self.resid(X0, bX0, TT, 1, d, py, bpy, False)
            P.dma("sp", src[:, :, i * TT:(i + 1) * TT], X0, r=[bX0], w=[self.bYT[i]], sem="xs")
        for i in range(self.ntile):
            tile(i)
        if "outs" not in SKIP:
            P.dma("sp", self.o_S_p[l].rearrange("h k v -> k h v"), S32, r=[bS32], sem="st")
        P.barrier()
        if "samp" not in SKIP:
            self.sample_mixer(l, WIN, bWI, WOUT, bWO, CF, bCF, CB, bCB, BCP, bBCP, SC, bSC, segs_sample)
        self.rr_set = list(range(8))


    def sample_mixer(self, l, WIN, bWI, WOUT, bWO, CF, bCF, CB, bCB, BCP, bBCP, SC, bSC, segs):
        P = self.P
        self.segs = [list(x) for x in segs]
        mk = self.mk
        self.rr_set = [4, 5, 6, 7]
        PB, bPB = self.PB, self.bPB
        HS, bHS, XS, bXS = self.HS, self.bHS, self.XS, self.bXS
        p16 = slice(0, NS)
        IDB = CB[:, 0:128]
        ONESF = CF[:, 128:256]
        IDF = CF[:, 512:640]
        SEL = CF[:, 640:896]
        ROPS = CF[:, 896:960]
        ZEROF = CF[:, 960:1088]
        DTB, SINK = BCP[:, 4:8], BCP[:, 8:16]
        NORMW4, QNW8, KNW2 = BCP[:, 16:528], BCP[:, 528:1040], BCP[:, 1040:1168]
        NEGA, ESINK = SC[:, 64:68], SC[:, 68:76]
        SELB, bSELB = mk("SELB", [256], BF16)
        CSTK, bCSTK = mk("CSTK", [CONV_CH])
        CSTT, bCSTT = mk("CSTT", [NCC, 48])
        PCT, bPCT = mk("PCT", [NCC, NS])
        ACC, bACC = mk("ACC", [NCC, NS])
        T12, bT12 = mk("T12", [NCC, NS])
        PCTOK, bPCTOK = mk("PCTOK", [CONV_CH])
        SQS, bSQS = mk("SQS", [8, NS], BF16)
        RN, bRN = mk("RN", [8, NS])
        QKN, bQKN = mk("QKN", [8, NS])
        KM, bKM = mk("KM", [DN_H, NS, NS])
        QM, bQM = mk("QM", [DN_H, NS, NS])
        VTOK, bVTOK = mk("VTOK", [DN_H, DV])
        KTOK, bKTOK = mk("KTOK", [DN_H, DK])
        VNS, bVNS = mk("VNS", [DN_H, DV])
        T1, bT1 = mk("T1", [DN_H, DV])
        KMS = [mk("KMS%d" % i, [DN_H, DK]) for i in range(2)]
        S0 = [mk("S0%d" % i, [DN_H, DV]) for i in range(2)]
        SN = [mk("SN%d" % i, [DN_H, DV]) for i in range(2)]
        AD, bAD = mk("AD", [NS, DN_H])
        AB, bAB = mk("AB", [NS * DN_H])
        SCS_, bSCS_ = mk("SCS", [96])
        ZGS, bZGS = mk("ZGS", [DN_H, DV])
        ONS, bONS = mk("ONS", [DN_H, DV])
        ON2S, bON2S = mk("ON2S", [DN_H, DV], BF16)
        QSS, bQSS = mk("QSS", [SW_H, DH])
        KSS, bKSS = mk("KSS", [SW_KV, DH])
        KRS, bKRS = mk("KRS", [SW_KV, DH])
        VRS, bVRS = mk("VRS", [SW_KV, DH])
        QRS, bQRS = mk("QRS", [SW_H, DH])
        QFS, bQFS = mk("QFS", [4, 2, DH], BF16)
        QTSS, bQTSS = mk("QTSS", [4, NS], BF16)
        KC, bKC = mk("KC", [NS, 128], BF16)
        VCE, bVCE = mk("VCE", [NS, SW_KV, DH + 1], BF16)
        KCT = [mk("KCT%d" % i, [128], BF16) for i in range(2)]
        PTS, bPTS = mk("PTS", [NS, SW_H], BF16)
        PTM = [mk("PTM%d" % i, [SW_H, NS], BF16) for i in range(2)]
        PROD, bPROD = mk("PROD", [SW_H, DH])
        OSS, bOSS = mk("OSS", [SW_H, DH])
        OSBS, bOSBS = mk("OSBS", [SW_H, DH], BF16)
        CATS, bCATS = mk("CATS", [KD, NS], BF16)
        TMP, bTMP = self.TMP, self.bTMP
        sc = lambda i, n=4: SCS_[:, i * 4:i * 4 + n]
        BETA, XA_, AXV, E1, L1, G4, AEX, LNB, RS4, SSQ4 = [sc(i) for i in range(10)]
        SSQ10, RS10, SN8, PN8, DEN8, RDEN8 = (SCS_[:, 40:50], SCS_[:, 50:60], SCS_[:, 60:68], SCS_[:, 68:76],
                                              SCS_[:, 76:84], SCS_[:, 84:92])
        fl = lambda v: v.rearrange("p h i -> p (h i)")
        v3 = lambda v: v.rearrange("p (h i) -> p h i", h=DN_H)
        v8 = lambda v: v.rearrange("p (h d) -> p h d", h=SW_H)
        v2 = lambda v: v.rearrange("p (h d) -> p h d", h=SW_KV)
        hb = lambda v: bc(v, [[1, DN_H], [0, 128]])

        P.dma("sp", CSTK[0:48, :], self.st_conv[l].rearrange("s j c -> (s j) c"), w=[bCSTK])
        for q4 in range(0, NS, 4):
            P.dma("pool", KC[:, q4:q4 + 4, :], self.st_k[l][q4:q4 + 4].rearrange("s k c -> k s c"), w=[bKC])
        P.op("dve", lambda e: e.memset(VCE[:, :, :, DH:DH + 1], 1.0), w=[bVCE])
        for q4 in range(0, NS, 4):
            for hk in range(SW_KV):
                P.dma("pool", VCE[:, q4:q4 + 4, hk, 0:DH],
                      self.st_v[l][q4:q4 + 4, :, hk * DH:(hk + 1) * DH].rearrange("s k d -> k s d"), w=[bVCE])
        P.op("act", lambda e: e.activation(out=SELB, in_=SEL, func=AF.Copy), r=[bCF], w=[bSELB])
        P.dma("sp", self.o_k_s[l][:, 0:127, :], self.st_k[l][:, 1:128, :])
        P.dma("sp", self.o_v_s[l][:, 0:127, :], self.st_v[l][:, 1:128, :])
        P.dma("sp", self.o_conv_s[l][:, 0:2, :], self.st_conv[l][:, 1:3, :])

        self.norm_mod(XS[:], bXS, NS, 1, HS[:], bHS, True, self.GS[:, 0:KD, :], self.bGS)

        pc, bpc = PB[0], bPB[0]
        pcv = pc[:, 0:NCC * NS].rearrange("p (c s) -> p c s", c=NCC)
        for c in range(NCC):
            for k in range(KD):
                P.op("pe", lambda e, c=c, k=k: e.matmul(pcv[:, c, :], WIN[:, k, c * 128:(c + 1) * 128], HS[:, k, :],
                                                        start=(k == 0), stop=(k == KD - 1)), r=[bWI, bHS], w=[bpc])
        P.op("act", lambda e: e.activation(out=PCT, in_=pcv, func=AF.Copy), r=[bpc], w=[bPCT])
        for half in range(2):
            pt, bpt = self.bank()
            for cc in range(6):
                c = half * 6 + cc
                P.op("pe", lambda e, c=c, cc=cc, pt=pt: e.transpose(pt[:, cc * 48:(cc + 1) * 48],
                                                                    CSTK[0:48, c * 128:(c + 1) * 128], IDF[0:48, 0:48]),
                     r=[bCSTK, bCF], w=[bpt])
            P.op("dve", lambda e, half=half, pt=pt: e.tensor_copy(out=fl(CSTT[:, half * 6:(half + 1) * 6, :]),
                                                                  in_=pt[:, 0:6 * 48]), r=[bpt], w=[bCSTT])
        cwv = lambda j: bc(self.SMALL[:, 96 + j:97 + j], [[4, NCC], [0, NS]])
        stv = lambda j: bc(CSTT[:, 0, j:j + 1], [[48, NCC], [3, NS]])
        P.op("dve", lambda e: e.tensor_tensor(out=ACC, in0=PCT, in1=cwv(3), op=ALU.mult), r=[bPCT, self.bSMALL], w=[bACC])
        for j in range(3):
            P.op("dve", lambda e, j=j: e.tensor_tensor(out=T12, in0=stv(j), in1=cwv(j), op=ALU.mult),
                 r=[bCSTT, self.bSMALL], w=[bT12])
            P.op("dve", lambda e: e.tensor_tensor(out=ACC, in0=ACC, in1=T12, op=ALU.add), r=[bT12, bACC], w=[bACC])
        P.op("act", lambda e: e.activation(out=ACC, in_=ACC, func=AF.Silu), r=[bACC], w=[bACC])
        P.op("act", lambda e: e.activation(out=SQS, in_=ACC[:, 0:8, :], func=AF.Square), r=[bACC], w=[bSQS])
        pn, bpn = self.bank()
        P.op("pe", lambda e: e.matmul(pn[:, 0:8 * NS], self.ONES[:], SQS.rearrange("p c s -> p (c s)"),
                                      start=True, stop=True), r=[self.bONES, bSQS], w=[bpn])
        P.op("act", lambda e: e.activation(out=fl(RN)[:, 0:4 * NS], in_=pn[:, 0:4 * NS], func=AF.Ln, scale=float(DK),
                                           bias=self.EPSD[:, 1:2]), r=[bpn, self.bONES], w=[bRN])
        P.op("act", lambda e: e.activation(out=fl(RN)[:, 4 * NS:8 * NS], in_=pn[:, 4 * NS:8 * NS], func=AF.Ln,
                                           bias=self.EPSD[:, 4:5]), r=[bpn, self.bONES], w=[bRN])
        P.op("act", lambda e: e.activation(out=RN, in_=RN, func=AF.Exp, scale=-0.5), r=[bRN], w=[bRN])
        P.op("dve", lambda e: e.tensor_tensor(out=QKN, in0=ACC[:, 0:8, :], in1=RN, op=ALU.mult), r=[bACC, bRN], w=[bQKN])
        selb = bc(SEL[:, 0:1], [[0, DN_H], [NS, NS], [1, NS]])
        P.op("dve", lambda e: e.tensor_tensor(out=KM, in0=bc(QKN[:, 4, 0:1], [[NS, DN_H], [1, NS], [0, NS]]), in1=selb,
                                              op=ALU.mult), r=[bQKN, bCF], w=[bKM])
        P.op("dve", lambda e: e.tensor_tensor(out=QM, in0=bc(QKN[:, 0, 0:1], [[NS, DN_H], [1, NS], [0, NS]]), in1=selb,
                                              op=ALU.mult), r=[bQKN, bCF], w=[bQM])
        ptv, bptv = self.bank()
        ptk, bptk = self.bank()
        for h in range(DN_H):
            P.op("pe", lambda e, h=h: e.transpose(ptv[p16, h * 128:(h + 1) * 128], ACC[:, 8 + h, :], IDF),
                 r=[bACC, bCF], w=[bptv])
            P.op("pe", lambda e, h=h: e.transpose(ptk[p16, h * 128:(h + 1) * 128], QKN[:, 4 + h, :], IDF),
                 r=[bQKN, bCF], w=[bptk])
        P.op("act", lambda e: e.activation(out=fl(VTOK)[p16], in_=ptv[p16, :], func=AF.Copy), r=[bptv], w=[bVTOK])
        P.op("dve", lambda e: e.tensor_copy(out=fl(KTOK)[p16], in_=ptk[p16, :]), r=[bptk], w=[bKTOK])

        def tokproj(pt_, bpt_, c0, ncol, o0):
            for k in range(KD):
                P.op("pe", lambda e, k=k: e.matmul(pt_[p16, o0:o0 + ncol], HS[:, k, :], WIN[:, k, c0:c0 + ncol],
                                                   start=(k == 0), stop=(k == KD - 1)), r=[bWI, bHS], w=[bpt_])
        for cg in range(3):
            pp, bpp = self.bank()
            tokproj(pp, bpp, cg * 512, 512, 0)
            P.op("act", lambda e, pp=pp, cg=cg: e.activation(out=PCTOK[p16, cg * 512:(cg + 1) * 512], in_=pp[p16, :],
                                                             func=AF.Copy), r=[bpp], w=[bPCTOK])
        P.dma("sp", self.o_conv_s[l][:, 2, :], PCTOK[p16, :], r=[bPCTOK])
        PZ, bPZ = self.bank()
        tokproj(PZ, bPZ, C_Z, 512, 0)
        P.op("act", lambda e: e.activation(out=fl(ONS)[p16], in_=PZ[p16, :], func=AF.Silu), r=[bPZ], w=[bONS])
        P.op("dve", lambda e: e.tensor_tensor(out=fl(ZGS)[p16], in0=fl(ONS)[p16], in1=NORMW4[p16], op=ALU.mult),
             r=[bONS, bBCP], w=[bZGS])
        PQ, bPQ = PB[2], bPB[2]
        PKV, bPKV = PB[3], bPB[3]
        tokproj(PQ, bPQ, C_Q, 512, 0)
        tokproj(PKV, bPKV, C_K, 256, 0)
        tokproj(PKV, bPKV, C_BA, 8, 256)
        P.op("act", lambda e: e.activation(out=BETA[p16], in_=PKV[p16, 256:260], func=AF.Exp, scale=-1.0),
             r=[bPKV], w=[bSCS_])
        P.op("dve", lambda e: e.tensor_scalar(out=BETA[p16], in0=BETA[p16], scalar1=1.0, scalar2=None, op0=ALU.add),
             r=[bSCS_], w=[bSCS_])
        P.op("dve", lambda e: e.reciprocal(out=BETA[p16], in_=BETA[p16]), r=[bSCS_], w=[bSCS_])
        P.op("dve", lambda e: e.tensor_tensor(out=XA_[p16], in0=PKV[p16, 260:264], in1=DTB[p16], op=ALU.add),
             r=[bPKV, bBCP], w=[bSCS_])
        P.op("act", lambda e: e.activation(out=AXV[p16], in_=XA_[p16], func=AF.Abs), r=[bSCS_], w=[bSCS_])
        P.op("act", lambda e: e.activation(out=E1[p16], in_=AXV[p16], func=AF.Exp, scale=-1.0), r=[bSCS_], w=[bSCS_])
        P.op("act", lambda e: e.activation(out=L1[p16], in_=E1[p16], func=AF.Ln, bias=self.EPSD[p16, 3:4]),
             r=[bSCS_, self.bONES], w=[bSCS_])
        P.op("dve", lambda e: e.scalar_tensor_tensor(out=G4[p16], in0=XA_[p16], scalar=0.0, in1=L1[p16], op0=ALU.max,
                                                     op1=ALU.add), r=[bSCS_], w=[bSCS_])
        P.op("dve", lambda e: e.tensor_tensor(out=G4[p16], in0=G4[p16], in1=NEGA[p16], op=ALU.mult), r=[bSCS_, bSC], w=[bSCS_])
        P.op("act", lambda e: e.activation(out=AEX[p16], in_=G4[p16], func=AF.Exp), r=[bSCS_], w=[bSCS_])
        P.op("dve", lambda e: e.tensor_tensor(out=AD[p16], in0=bc(AEX[p16], [[0, NS], [1, DN_H]]),
                                              in1=bc(IDF[p16, 0:1], [[1, NS], [0, DN_H]]), op=ALU.mult),
             r=[bSCS_, bCF], w=[bAD])
        pab, bpab = self.bank()
        P.op("pe", lambda e: e.matmul(pab[:, 0:NS * DN_H], ONESF[p16, :], AD[p16].rearrange("p s h -> p (s h)"),
                                      start=True, stop=True), r=[bCF, bAD], w=[bpab])
        P.op("dve", lambda e: e.tensor_copy(out=AB, in_=pab[:, 0:NS * DN_H]), r=[bpab], w=[bAB])

        P.op("act", lambda e: e.activation(out=TMP[p16], in_=PQ[p16, :], func=AF.Square), r=[bPQ], w=[bTMP])
        P.op("dve", lambda e: e.tensor_reduce(out=SSQ10[p16, 0:8], in_=v8(TMP[p16]), axis=AX.X, op=ALU.add),
             r=[bTMP], w=[bSCS_])
        P.op("act", lambda e: e.activation(out=self.R[p16, 0:128], in_=PKV[p16, 0:128], func=AF.Square),
             r=[bPKV], w=[self.bR])
        P.op("dve", lambda e: e.tensor_reduce(out=SSQ10[p16, 8:10], in_=v2(self.R[p16, 0:128]), axis=AX.X, op=ALU.add),
             r=[self.bR], w=[bSCS_])
        P.op("act", lambda e: e.activation(out=RS10[p16], in_=SSQ10[p16], func=AF.Ln, bias=self.EPSD[p16, 2:3]),
             r=[bSCS_, self.bONES], w=[bSCS_])
        P.op("act", lambda e: e.activation(out=RS10[p16], in_=RS10[p16], func=AF.Exp, scale=-0.5), r=[bSCS_], w=[bSCS_])
        P.op("dve", lambda e: e.tensor_tensor(out=QSS[p16], in0=v8(PQ[p16, :]), in1=bc(RS10[p16, 0:8], [[1, 8], [0, DH]]),
                                              op=ALU.mult), r=[bPQ, bSCS_], w=[bQSS])
        P.op("dve", lambda e: e.tensor_tensor(out=QSS[p16], in0=QSS[p16], in1=v8(QNW8[p16]), op=ALU.mult),
             r=[bQSS, bBCP], w=[bQSS])
        P.op("dve", lambda e: e.tensor_tensor(out=KSS[p16], in0=v2(PKV[p16, 0:128]),
                                              in1=bc(RS10[p16, 8:10], [[1, 2], [0, DH]]), op=ALU.mult),
             r=[bPKV, bSCS_], w=[bKSS])
        P.op("dve", lambda e: e.tensor_tensor(out=KSS[p16], in0=KSS[p16], in1=v2(KNW2[p16]), op=ALU.mult),
             r=[bKSS, bBCP], w=[bKSS])
        P.op("act", lambda e: e.activation(out=VRS[p16], in_=v2(PKV[p16, 128:256]), func=AF.Copy), r=[bPKV], w=[bVRS])

        def rope(x1, x2, nh, d1, d2, bsrc, bdst_):
            cosb = bc(ROPS[p16, 0:1], [[0, nh], [1, 32]])
            sinb = bc(ROPS[p16, 32:33], [[0, nh], [1, 32]])
            t1 = TMP[p16, 0:nh * 32].rearrange("p (h d) -> p h d", h=nh)
            t2 = TMP[p16, 256:256 + nh * 32].rearrange("p (h d) -> p h d", h=nh)
            for (xa, xb, dd, op_) in ((x1, x2, d1, ALU.subtract), (x2, x1, d2, ALU.add)):
                P.op("dve", lambda e, xa=xa: e.tensor_tensor(out=t1, in0=xa, in1=cosb, op=ALU.mult),
                     r=[bsrc, bCF, bTMP], w=[bTMP])
                P.op("dve", lambda e, xb=xb: e.tensor_tensor(out=t2, in0=xb, in1=sinb, op=ALU.mult),
                     r=[bsrc, bCF, bTMP], w=[bTMP])
                P.op("dve", lambda e, dd=dd, op_=op_: e.tensor_tensor(out=dd, in0=t1, in1=t2, op=op_),
                     r=[bTMP], w=[bdst_])
        rope(QSS[p16, :, 0:32], QSS[p16, :, 32:64], SW_H, QRS[p16, :, 0:32], QRS[p16, :, 32:64], bQSS, bQRS)
        rope(KSS[p16, :, 0:32], KSS[p16, :, 32:64], SW_KV, KRS[p16, :, 0:32], KRS[p16, :, 32:64], bKSS, bKRS)
        P.dma("sp", self.o_k_s[l][:, 127, :], KRS[p16].rearrange("p h d -> p (h d)"), r=[bKRS])
        P.dma("sp", self.o_v_s[l][:, 127, :], VRS[p16].rearrange("p h d -> p (h d)"), r=[bVRS])
        P.op("act", lambda e: e.activation(out=bc(QFS[p16, 0, 0, 0:1], [[DH, 2], [2 * DH, 4], [1, DH]]),
                                           in_=QRS[p16].rearrange("p (a b) d -> p a b d", a=2), func=AF.Copy),
             r=[bQRS], w=[bQFS])
        ptq, bptq = self.bank()
        ptqv = ptq[:, 0:32].bitcast(BF16)
        for g in range(4):
            P.op("pe", lambda e, g=g: e.transpose(ptqv[:, g * NS:(g + 1) * NS],
                                                  QFS[p16, g, :, :].rearrange("p a d -> p (a d)"), IDB[p16, p16]),
                 r=[bQFS, bCB], w=[bptq])
        P.op("act", lambda e: e.activation(out=QTSS.rearrange("p g s -> p (g s)"), in_=ptqv[:, 0:4 * NS], func=AF.Copy),
             r=[bptq], w=[bQTSS])
        P.op("dve", lambda e: e.tensor_tensor(out=PROD[p16].rearrange("p (a b) d -> p a b d", a=2),
                                              in0=QRS[p16].rearrange("p (a b) d -> p a b d", a=2),
                                              in1=bc(KRS[p16, 0, 0:1], [[DH, 2], [0, 4], [1, DH]]), op=ALU.mult),
             r=[bQRS, bKRS], w=[bPROD])
        P.op("dve", lambda e: e.tensor_reduce(out=SN8[p16], in_=PROD[p16], axis=AX.X, op=ALU.add), r=[bPROD], w=[bSCS_])
        P.op("act", lambda e: e.activation(out=PN8[p16], in_=SN8[p16], func=AF.Exp), r=[bSCS_], w=[bSCS_])

        pks, bpks = PB[0], bPB[0]
        pos_, bpos_ = PB[1], bPB[1]
        for (pz_, bpz_) in ((pks, bpks), (pos_, bpos_)):
            P.op("pe", lambda e, pz_=pz_: e.matmul(pz_[p16, :], ZEROF[:, 0:NS], CF[:, 0:512], start=True, stop=False,
                                                   skip_group_check=True), r=[bCF], w=[bpz_])
        stS = self.st_S[l]
        for s_ in range(NS):
            S0b, bS0b = S0[s_ % 2]
            P.dma("sp", S0b, stS[s_].rearrange("h k v -> k h v"), w=[bS0b])
            for h in range(DN_H):
                P.op("pe", lambda e, h=h, s_=s_, S0b=S0b: e.matmul(
                    pks[p16, h * 128:(h + 1) * 128], KM[:, h, s_, :], S0b[:, h, :], start=False, stop=(s_ == NS - 1),
                    skip_group_check=True), r=[bKM, bS0b], w=[bpks])
        P.op("dve", lambda e: e.tensor_tensor(out=T1[p16], in0=v3(pks[p16, :]), in1=hb(AEX[p16]), op=ALU.mult),
             r=[bpks, bSCS_], w=[bT1])
        P.op("dve", lambda e: e.tensor_tensor(out=T1[p16], in0=VTOK[p16], in1=T1[p16], op=ALU.subtract),
             r=[bVTOK, bT1], w=[bT1])
        P.op("dve", lambda e: e.tensor_tensor(out=VNS[p16], in0=T1[p16], in1=hb(BETA[p16]), op=ALU.mult),
             r=[bT1, bSCS_], w=[bVNS])
        for s_ in range(NS):
            S0b, bS0b = S0[s_ % 2]
            SNb, bSNb = SN[s_ % 2]
            KMSb, bKMSb = KMS[s_ % 2]
            P.dma("sp", S0b, stS[s_].rearrange("h k v -> k h v"), w=[bS0b])
            P.op("dve", lambda e, s_=s_, KMSb=KMSb: e.tensor_scalar(out=fl(KMSb)[p16], in0=fl(KTOK)[p16],
                                                                    scalar1=IDF[p16, s_:s_ + 1], scalar2=None, op0=ALU.mult),
                 r=[bKTOK, bCF], w=[bKMSb])
            pu, bpu = self.bank()
            for h in range(DN_H):
                P.op("pe", lambda e, h=h, pu=pu, KMSb=KMSb: e.matmul(pu[:, h * 128:(h + 1) * 128], KMSb[p16, h, :],
                                                                    VNS[p16, h, :], start=True, stop=True),
                     r=[bKMSb, bVNS], w=[bpu])
            P.op("dve", lambda e, s_=s_, S0b=S0b, SNb=SNb: e.tensor_tensor(
                out=SNb, in0=S0b, in1=bc(AB[:, s_ * DN_H:s_ * DN_H + 1], [[1, DN_H], [0, DV]]), op=ALU.mult),
                r=[bS0b, bAB], w=[bSNb])
            P.op("dve", lambda e, pu=pu, SNb=SNb: e.tensor_tensor(out=fl(SNb), in0=fl(SNb), in1=pu[:, :], op=ALU.add),
                 r=[bSNb, bpu], w=[bSNb])
            P.dma("sp", self.o_S_s[l][s_].rearrange("h k v -> k h v"), SNb, r=[bSNb])
            for h in range(DN_H):
                P.op("pe", lambda e, h=h, s_=s_, SNb=SNb: e.matmul(
                    pos_[p16, h * 128:(h + 1) * 128], QM[:, h, s_, :], SNb[:, h, :], start=False, stop=(s_ == NS - 1),
                    skip_group_check=True), r=[bQM, bSNb], w=[bpos_])
        P.op("act", lambda e: e.activation(out=fl(T1)[p16], in_=pos_[p16, :], func=AF.Square), r=[bpos_], w=[bT1])
        P.op("dve", lambda e: e.tensor_reduce(out=SSQ4[p16], in_=T1[p16], axis=AX.X, op=ALU.add), r=[bT1], w=[bSCS_])
        P.op("act", lambda e: e.activation(out=RS4[p16], in_=SSQ4[p16], func=AF.Ln, scale=1.0 / DV,
                                           bias=self.EPSD[p16, 4:5]), r=[bSCS_, self.bONES], w=[bSCS_])
        P.op("act", lambda e: e.activation(out=RS4[p16], in_=RS4[p16], func=AF.Exp, scale=-0.5), r=[bSCS_], w=[bSCS_])
        P.op("dve", lambda e: e.tensor_tensor(out=ONS[p16], in0=v3(pos_[p16, :]), in1=hb(RS4[p16]), op=ALU.mult),
             r=[bpos_, bSCS_], w=[bONS])
        P.op("dve", lambda e: e.tensor_tensor(out=ON2S[p16], in0=ONS[p16], in1=ZGS[p16], op=ALU.mult),
             r=[bONS, bZGS], w=[bON2S])
        pto, bpto = self.bank()
        ptov = pto[:, 0:32].bitcast(BF16)
        for h in range(DN_H):
            P.op("pe", lambda e, h=h: e.transpose(ptov[:, h * NS:(h + 1) * NS], ON2S[p16, h, :], IDB[p16, p16]),
                 r=[bON2S, bCB], w=[bpto])
        P.op("act", lambda e: e.activation(out=CATS[:, 0:4, :], in_=ptov[:, 0:4 * NS].rearrange("p (h s) -> p h s", h=4),
                                           func=AF.Copy), r=[bpto], w=[bCATS])

        psc, bpsc = PB[2], bPB[2]
        for s_ in range(NS):
            KCTb, bKCTb = KCT[s_ % 2]
            ptc, bptc = self.bank()
            ptcv = ptc[:, 0:64].bitcast(BF16)
            P.op("pe", lambda e, s_=s_, ptcv=ptcv: e.transpose(ptcv, KC[:, s_, :], IDB), r=[bKC, bCB], w=[bptc])
            P.op("act", lambda e, ptcv=ptcv, KCTb=KCTb: e.activation(out=KCTb, in_=ptcv, func=AF.Copy),
                 r=[bptc], w=[bKCTb])
            for hk in range(SW_KV):
                psl = slice(hk * DH, (hk + 1) * DH)
                P.op("pe", lambda e, s_=s_, hk=hk, psl=psl, KCTb=KCTb: e.matmul(
                    psc[:, s_ * 8 + hk * 4:s_ * 8 + hk * 4 + 4], KCTb[psl, :], QTSS[psl, :, s_], start=True, stop=True),
                    r=[bKCTb, bQTSS], w=[bpsc])
        P.op("act", lambda e: e.activation(out=PTS.rearrange("p s h -> p (s h)"), in_=psc[:, 0:NS * SW_H], func=AF.Exp),
             r=[bpsc], w=[bPTS])
        pov_ = [(PB[0], bPB[0]), (PB[1], bPB[1])]
        for (pz_, bpz_) in pov_:
            P.op("pe", lambda e, pz_=pz_: e.matmul(pz_[p16, :], ZEROF[:, 0:NS], CF[:, 0:512], start=True, stop=False,
                                                   skip_group_check=True), r=[bCF], w=[bpz_])
        for s_ in range(NS):
            PTMb, bPTMb = PTM[s_ % 2]
            P.op("dve", lambda e, s_=s_, PTMb=PTMb: e.tensor_tensor(
                out=PTMb, in0=bc(PTS[:, s_, 0:1], [[1, SW_H], [0, NS]]),
                in1=bc(SELB[:, s_ * NS:s_ * NS + 1], [[0, SW_H], [1, NS]]), op=ALU.mult),
                r=[bPTS, bSELB], w=[bPTMb])
            for hk in range(SW_KV):
                po_, bpo_ = pov_[hk]
                for g in range(4):
                    P.op("pe", lambda e, s_=s_, hk=hk, g=g, po_=po_, PTMb=PTMb: e.matmul(
                        po_[p16, g * (DH + 1):(g + 1) * (DH + 1)], PTMb[:, hk * 4 + g, :], VCE[:, s_, hk, :],
                        start=False, stop=(s_ == NS - 1), skip_group_check=True), r=[bPTMb, bVCE], w=[bpo_])
        for hk in range(SW_KV):
            po_, bpo_ = pov_[hk]
            pov = po_[p16, 0:4 * (DH + 1)].rearrange("p (g d) -> p g d", g=4)
            hs_ = slice(hk * 4, (hk + 1) * 4)
            P.op("dve", lambda e, hk=hk, hs_=hs_: e.tensor_tensor(
                out=PROD[p16, hs_, :], in0=bc(VRS[p16, hk, 0:1], [[0, 4], [1, DH]]),
                in1=bc(PN8[p16, hk * 4:hk * 4 + 1], [[1, 4], [0, DH]]), op=ALU.mult), r=[bVRS, bSCS_], w=[bPROD])
            P.op("dve", lambda e, pov=pov, hs_=hs_: e.tensor_tensor(out=OSS[p16, hs_, :], in0=pov[:, :, 0:DH],
                                                                   in1=PROD[p16, hs_, :], op=ALU.add),
                 r=[bpo_, bPROD], w=[bOSS])
            P.op("dve", lambda e, pov=pov, hs_=hs_: e.tensor_tensor(out=DEN8[p16, hs_], in0=pov[:, :, DH],
                                                                   in1=PN8[p16, hs_], op=ALU.add),
                 r=[bpo_, bSCS_], w=[bSCS_])
        P.op("dve", lambda e: e.tensor_tensor(out=DEN8[p16], in0=DEN8[p16], in1=ESINK[p16], op=ALU.add),
             r=[bSCS_, bSC], w=[bSCS_])
        P.op("dve", lambda e: e.reciprocal(out=RDEN8[p16], in_=DEN8[p16]), r=[bSCS_], w=[bSCS_])
        P.op("dve", lambda e: e.tensor_tensor(out=OSBS[p16], in0=OSS[p16], in1=bc(RDEN8[p16, 0:1], [[1, 8], [0, DH]]),
                                              op=ALU.mult), r=[bOSS, bSCS_], w=[bOSBS])
        ptos, bptos = self.bank()
        ptosv = ptos[:, 0:32].bitcast(BF16)
        osf = OSBS.rearrange("p h d -> p (h d)")
        for c in range(4):
            P.op("pe", lambda e, c=c: e.transpose(ptosv[:, c * NS:(c + 1) * NS], osf[p16, c * 128:(c + 1) * 128],
                                                  IDB[p16, p16]), r=[bOSBS, bCB], w=[bptos])
        P.op("act", lambda e: e.activation(out=CATS[:, 4:8, :], in_=ptosv[:, 0:4 * NS].rearrange("p (h s) -> p h s", h=4),
                                           func=AF.Copy), r=[bptos], w=[bCATS])
        for d in range(KD):
            py, bpy = self.bank()
            for c in range(KD):
                P.op("pe", lambda e, d=d, c=c, py=py: e.matmul(py[:, 0:NS], WOUT[:, c, d * 128:(d + 1) * 128], CATS[:, c, :],
                                                               start=(c == 0), stop=(c == KD - 1)),
                     r=[bWO, bCATS], w=[bpy])
            self.resid(XS[:], bXS, NS, 1, d, py, bpy, True)


def make_constf():
    c = np.zeros((128, NCF), np.float32)
    j = np.arange(128)[:, None]
    i = np.arange(128)[None, :]
    c[:, 0:128] = (j <= i)
    c[:, 128:256] = 1.0
    c[:, 256:384] = np.where(i >= j, 0.0, NEG)
    c[:, 384:512] = np.where(i < j, 0.0, NEG)
    c[:, 512:640] = np.eye(128)
    c[:, 640:896] = np.eye(NS, dtype=np.float32).reshape(1, NS * NS)
    half = DH // 2
    inv = np.power(np.float32(10000.0), -np.arange(half, dtype=np.float32) * np.float32(2.0) / np.float32(DH)).astype(np.float32)
    ang = (np.float32(PAST_LEN) * inv).astype(np.float32)
    c[:, 896:928] = np.cos(ang)[None, :]
    c[:, 928:960] = np.sin(ang)[None, :]
    return c.astype(np.float32)


def make_constb():
    c = np.zeros((128, NCB), np.float32)
    j = np.arange(128)[:, None]
    i = np.arange(128)[None, :]
    c[:, 0:128] = np.eye(128)
    c[:, 128:640] = np.tile(np.where(j <= i, 0.0, NEG), (1, 4))
    c[:, 640:1152] = np.tile(np.where(j >= i, 0.0, NEG), (1, 4))
    c[:, 1152:1280] = (j // 8 == i // 8)
    for li, m in enumerate((8, 16, 32, 64)):
        ml = (j // (2 * m) == i // (2 * m)) & ((j // m) % 2 == 1) & ((i // m) % 2 == 0)
        c[:, 1280 + li * 128:1408 + li * 128] = ml
        c[:, 1792 + li * 128:1920 + li * 128] = ml.T
    return c.astype(np.float32)


def make_rope(seq):
    half = DH // 2
    inv = np.power(np.float32(10000.0), -np.arange(half, dtype=np.float32) * np.float32(2.0) / np.float32(DH)).astype(np.float32)
    pos = np.arange(seq, dtype=np.float32)
    ang = (pos[:, None] * inv[None, :]).astype(np.float32)
    t = np.concatenate([np.cos(ang), np.sin(ang)], -1).astype(np.float32)
    return np.ascontiguousarray(t.reshape(seq // 128, 128, 64).transpose(1, 0, 2))


def make_bcp(inp, L):
    b = np.zeros((L, 128, NBC), np.float32)
    for l in range(L):
        row = np.concatenate([
            np.asarray(inp["dn_A_log"][l]), np.asarray(inp["dn_dt_bias"][l]), np.asarray(inp["swa_sinks"][l]),
            np.tile(np.asarray(inp["dn_norm_w"][l]), 4), np.tile(np.asarray(inp["swa_q_norm"][l]), 8),
            np.tile(np.asarray(inp["swa_k_norm"][l]), 2)]).astype(np.float32)
        b[l] = row[None, :]
    return b


def make_smallp(inp, L):
    sp = np.zeros((L, 128, 144), np.float32)
    for l in range(L):
        sp[l, :, 0:72] = np.asarray(inp["b_ada"][l]).reshape(72, 128).T
        for s, nm in enumerate(("g_ffn1", "g_mix", "g_ffn2")):
            sp[l, :, 72 + 8 * s:80 + 8 * s] = np.asarray(inp[nm][l]).reshape(8, 128).T
        sp[l, :, 96:144] = np.asarray(inp["dn_conv_w"][l]).T.reshape(12, 128, 4).transpose(1, 0, 2).reshape(128, 48)
    return sp


_CACHE = {}


def get_nc(seq, depth, stop):
    key = (seq, depth, stop)
    if key not in _CACHE:
        b = Builder(seq, depth, stop)
        nc = b.build()
        _CACHE[key] = (nc, b)
    return _CACHE[key]


def make_in_maps(inputs, seq, depth):
    f = lambda a: np.ascontiguousarray(np.asarray(a, dtype=np.float32))
    shared = {
        "w_ada": f(inputs["w_ada"][:depth]),
        "smallp": make_smallp(inputs, depth),
        "w_ffn1_gu": f(inputs["w_ffn1_gu"][:depth]), "w_ffn2_gu": f(inputs["w_ffn2_gu"][:depth]),
        "w_ffn1_down": f(inputs["w_ffn1_down"][:depth]), "w_ffn2_down": f(inputs["w_ffn2_down"][:depth]),
        "w_in": f(inputs["w_in"][:depth]), "w_out": f(inputs["w_out"][:depth]),
        "constf": make_constf(), "constb": make_constb(), "ropeT": make_rope(seq),
        "bcp": make_bcp(inputs, depth),
    }
    xp = np.asarray(inputs["x_prompt"], np.float32)
    xs = np.asarray(inputs["x_sample"], np.float32)
    cp = np.asarray(inputs["c_prompt"], np.float32)
    cs = np.asarray(inputs["c_sample"], np.float32)
    in_maps = []
    for c in range(NCORES):
        m = dict(shared)
        m["xT"] = np.ascontiguousarray(xp[c, :seq].T)
        m["xsT"] = np.ascontiguousarray(xs[c * NS:(c + 1) * NS, 0].T)
        m["csT"] = np.ascontiguousarray(np.concatenate([cp[c:c + 1], cs[c * NS:(c + 1) * NS]], 0).T)
        sl = slice(c * NS, (c + 1) * NS)
        m["st_conv"] = f(inputs["state_dn_conv"][:depth, sl])
        m["st_S"] = f(inputs["state_dn_S"][:depth, sl])
        m["st_k"] = f(inputs["cache_swa_k"][:depth, sl]).reshape(depth, NS, 128, SW_KV * DH)
        m["st_v"] = f(inputs["cache_swa_v"][:depth, sl]).reshape(depth, NS, 128, SW_KV * DH)
        in_maps.append(m)
    return in_maps


def run(inputs, seq=SEQ, depth=DEPTH, stop=None, trace=False):
    nc, b = get_nc(seq, depth, stop)
    in_maps = make_in_maps(inputs, seq, depth)
    res = run_bass_kernel_spmd(nc, in_maps, core_ids=list(range(NCORES)), trace=trace)
    return res, b


def assemble(r, depth):
    g = lambda c, n: np.asarray(r[c][n])
    yp = np.stack([g(c, "yT").T for c in range(NCORES)], 0)
    ys = np.concatenate([g(c, "ysT").T for c in range(NCORES)], 0)[:, None, :]
    conv_p = np.stack([g(c, "o_conv_p").transpose(0, 2, 1) for c in range(NCORES)], 1)
    S_p = np.stack([g(c, "o_S_p") for c in range(NCORES)], 1)
    k_p = np.stack([g(c, "o_k_p").reshape(depth, 128, SW_KV, DH) for c in range(NCORES)], 1)
    v_p = np.stack([g(c, "o_v_p").reshape(depth, 128, SW_KV, DH) for c in range(NCORES)], 1)
    conv_s = np.concatenate([g(c, "o_conv_s") for c in range(NCORES)], 1)
    S_s = np.concatenate([g(c, "o_S_s") for c in range(NCORES)], 1)
    k_s = np.concatenate([g(c, "o_k_s").reshape(depth, NS, 128, SW_KV, DH) for c in range(NCORES)], 1)
    v_s = np.concatenate([g(c, "o_v_s").reshape(depth, NS, 128, SW_KV, DH) for c in range(NCORES)], 1)
    outs = (yp, ys, conv_p, S_p, k_p, v_p, conv_s, S_s, k_s, v_s)
    return tuple(np.ascontiguousarray(o, dtype=np.float32) for o in outs)


def kernel(**inputs):
    res, b = run(inputs)
    return assemble(res.results, DEPTH)
```

```python
import math
import os
from contextlib import ExitStack

import numpy as np
import ml_dtypes

import concourse.bass as bass
import concourse.mybir as mybir
from concourse.bass_utils import run_bass_kernel_spmd

F32 = mybir.dt.float32
BF16 = mybir.dt.bfloat16
AF = mybir.ActivationFunctionType
ALU = mybir.AluOpType
AX = mybir.AxisListType

NCORES = 8
D = 1024
KD = D // 128
SEQ = 4096
NS = 16
DEPTH = 2
DFF = 2816
NJ = DFF // 128
NMOD = 9
EPS = 1e-6
DN_H = 4
DK = 128
DV = 128
CONV_CH = 1536
NCC = CONV_CH // 128
SW_H = 8
SW_KV = 2
DH = 64
IN_DIM = 2824
C_Z = 1536
C_BA = 2048
C_Q = 2056
C_K = 2568
C_V = 2696
TT = 512
PAST_LEN = 16384
NEG = -30000.0
SKIP = set(os.environ.get('KSKIP', '').split(','))
DNCUT = int(os.environ.get('KDNCUT', '99'))


class Buf:
    __slots__ = ("name", "lw", "rd", "excl")

    def __init__(self, name, excl=False):
        self.name = name
        self.lw = None
        self.rd = []
        self.excl = excl


class Prog:
    ENGS = ("pe", "act", "dve", "pool", "sp")

    def __init__(self, nc, es):
        self.nc = nc
        self.eng = {"pe": nc.tensor, "act": nc.scalar, "dve": nc.vector,
                    "pool": nc.gpsimd, "sp": nc.sync}
        self.ops = []
        self.csem = {e: es.enter_context(nc.semaphore("c_" + e)) for e in self.ENGS}
        self.es = es
        self.last = {}
        self.lastd = {}
        self.pending_dma = []

    def _deps(self, r, w):
        deps = {}

        def add(o, raw):
            deps[o] = deps.get(o, False) or raw

        for b in r:
            if b.lw is not None:
                add(b.lw, True)
            if b.excl:
                for o in b.rd:
                    add(o, False)
        for b in w:
            if b.lw is not None:
                add(b.lw, False)
            for o in b.rd:
                add(o, False)
        return deps

    def _upd(self, oid, r, w):
        for b in w:
            b.lw = oid
            b.rd = []
        for b in r:
            if b in w:
                continue
            if b.excl:
                b.rd = [oid]
            else:
                b.rd.append(oid)

    def op(self, eng, fn, r=(), w=()):
        deps = self._deps(r, w)
        oid = len(self.ops)
        self.ops.append((eng, fn, deps, "c", None))
        self._upd(oid, r, w)
        self.last[eng] = oid
        return oid

    def barrier(self):
        deps = {o: True for o in (set(self.last.values()) | set(self.pending_dma))}
        self.pending_dma = []
        for e in self.ENGS:
            self.ops.append((e, None, dict(deps), "b", None))

    def dma(self, eng, out, in_, r=(), w=(), sem="g"):
        deps = self._deps(r, w)
        oid = len(self.ops)
        self.ops.append((eng, (out, in_), deps, "d", sem))
        self._upd(oid, r, w)
        self.lastd[sem] = oid
        self.pending_dma.append(oid)
        return oid

    DUR = {"pe": 0.12, "act": 0.45, "dve": 0.45, "pool": 0.6, "sp": 0.1}
    LAT = 1.2
    DMA_LAT = 6.0
    WINDOW = int(os.environ.get("KWIN", "1"))

    def schedule(self, ops):
        if self.WINDOW <= 1:
            return ops
        n = len(ops)
        order = []
        seg_start = 0
        i = 0
        segs = []
        while i < n:
            if ops[i][3] == "b":
                j = i
                while j < n and ops[j][3] == "b":
                    j += 1
                segs.append((seg_start, i, j))
                seg_start = j
                i = j
            else:
                i += 1
        segs.append((seg_start, n, n))
        fin = [0.0] * n
        for (a, b, c) in segs:
            self._sched_segment(ops, a, b, order, fin)
            order.extend(range(b, c))
            if c > b:
                t = max([fin[x] for x in order[-400:]] + [0.0])
                for x in range(b, c):
                    fin[x] = t
        remap = {old: new for new, old in enumerate(order)}
        segdeps = {}
        for (a, b, c) in segs:
            lastc = {}
            full = {}
            for x in range(a, b):
                if ops[x][3] == "d":
                    full[remap[x]] = True
                elif ops[x][3] == "c":
                    e_ = ops[x][0]
                    if e_ not in lastc or remap[x] > lastc[e_]:
                        lastc[e_] = remap[x]
            for v in lastc.values():
                full[v] = True
            for x in range(b, c):
                segdeps[x] = full
        out = []
        for old in order:
            eng, fn, deps, kind, dsem = ops[old]
            if kind == "b":
                out.append((eng, fn, segdeps[old], kind, dsem))
            else:
                out.append((eng, fn, {remap[d]: r for d, r in deps.items()}, kind, dsem))
        return out

    def _sched_segment(self, ops, a, b, order, fin):
        if b <= a:
            return
        W = self.WINDOW
        queues = {e: [] for e in self.ENGS}
        for i in range(a, b):
            queues[ops[i][0]].append(i)
        head = {e: 0 for e in self.ENGS}
        done = set()
        placed_before = a
        efree = {e: max([fin[x] for x in order[-50:]] + [0.0]) for e in self.ENGS}
        remaining = b - a
        taken = {e: set() for e in self.ENGS}
        pos_in_q = {i: k for k, i in enumerate(queues["pe"])}
        glue = None
        while remaining > 0:
            best = None
            for e in self.ENGS:
                q = queues[e]
                h = head[e]
                cnt = 0
                k = h
                Wm = W
                if e == "pe" and glue is not None:
                    k = pos_in_q[glue]
                    Wm = 1
                while k < len(q) and cnt < Wm:
                    i = q[k]
                    k += 1
                    if i in taken[e]:
                        continue
                    cnt += 1
                    ok = True
                    rdy = efree[e]
                    for d in ops[i][2]:
                        if d >= a and d not in done:
                            ok = False
                            break
                        if ops[d][0] == e and ops[d][3] != "d":
                            t = fin[d]
                        else:
                            t = fin[d] + self.LAT
                        if t > rdy:
                            rdy = t
                    if not ok:
                        continue
                    key = (rdy, i)
                    if best is None or key < best[0]:
                        best = (key, e, i)
                    if rdy <= efree[e]:
                        break
            if best is None and glue is not None:
                glue = None
                continue
            assert best is not None, "scheduler stuck"
            (rdy, _), e, i = best
            dur = self.DUR[e]
            fin[i] = rdy + (self.DMA_LAT if ops[i][3] == "d" else dur)
            efree[e] = rdy + dur
            order.append(i)
            done.add(i)
            taken[e].add(i)
            if e == "pe":
                glue = None
                kq = pos_in_q[i] + 1
                qpe = queues["pe"]
                if kq < len(qpe) and i in ops[qpe[kq]][2] and qpe[kq] not in taken["pe"]:
                    glue = qpe[kq]
            q = queues[e]
            while head[e] < len(q) and q[head[e]] in taken[e]:
                head[e] += 1
            remaining -= 1

    NPOOL = {"sp": 24, "pool": 12, "act": 4}

    def emit(self):
        ops = self.schedule(self.ops)
        self.ops = ops
        n = len(ops)
        marked = [False] * n
        for i in range(n):
            eng, _, deps, kind, _ = ops[i]
            for d, raw in deps.items():
                de, _, _, dk, _ = ops[d]
                if dk == "d":
                    continue
                if de == eng and eng == "pe":
                    continue
                marked[d] = True
        pools = {q: [self.es.enter_context(self.nc.semaphore("dq_%s%d" % (q, j))) for j in range(k)]
                 for q, k in self.NPOOL.items()}
        rr = {q: 0 for q in pools}
        pcount = {q: [0] * len(pools[q]) for q in pools}
        ev = [None] * n
        slot_of = [None] * n
        ccnt = {e: 0 for e in self.ENGS}
        for i in range(n):
            eng, _, _, kind, _ = ops[i]
            if kind == "d":
                j = rr[eng] % len(pools[eng])
                rr[eng] += 1
                slot_of[i] = (j, pcount[eng][j])
                pcount[eng][j] += 16
                ev[i] = (("d", eng, j), pcount[eng][j])
            elif kind == "c" and marked[i]:
                ccnt[eng] += 1
                ev[i] = (("c", eng), ccnt[eng])
        waited = {e: {} for e in self.ENGS}
        nwait = 0

        def semof(key):
            return self.csem[key[1]] if key[0] == "c" else pools[key[1]][key[2]]

        for i in range(n):
            eng, fn, deps, kind, _ = ops[i]
            E = self.eng[eng]
            need = {}
            for d, raw in deps.items():
                de = ops[d][0]
                if ops[d][3] == "c" and de == eng and eng == "pe":
                    continue
                if ev[d] is None:
                    continue
                key, val = ev[d]
                if need.get(key, 0) < val:
                    need[key] = val
            if kind == "d":
                j, prev = slot_of[i]
                if prev > 0:
                    key = ("d", eng, j)
                    if need.get(key, 0) < prev:
                        need[key] = prev
            for key, val in need.items():
                if waited[eng].get(key, 0) >= val:
                    continue
                waited[eng][key] = val
                E.wait_ge(semof(key), val)
                nwait += 1
            if kind == "b":
                continue
            if kind == "d":
                out, in_ = fn
                E.dma_start(out=out, in_=in_).then_inc(pools[eng][slot_of[i][0]], 16)
            else:
                ins = fn(E)
                if marked[i]:
                    ins.then_inc(self.csem[eng], 1)
        for q in pools:
            for j, c in enumerate(pcount[q]):
                if c > 0 and waited["sp"].get(("d", q, j), 0) < c:
                    self.eng["sp"].wait_ge(pools[q][j], c)
        self.stats = dict(n_ops=n, n_wait=nwait, n_marked=sum(marked))


def bc(ap, dims):
    return bass.AP(ap.tensor, ap.offset, [list(ap.ap[0])] + [list(d) for d in dims])


NAR = 51072
O_X0, O_X1, O_H, O_G, O_R, O_TMP = 33792, 37888, 41984, 44032, 49664, 50176
O_WIN, O_WOUT = 0, 11296
NCF = 1088
NCB = 2304
NBC = 1168


class Builder:
    def __init__(self, seq=SEQ, depth=DEPTH, stop=None):
        self.seq = seq
        self.depth = depth
        self.stop = stop
        self.ntile = seq // TT
        self.nblk = seq // 128
        self._rr = 0
        self.rr_set = list(range(8))

    def view(self, off, n32, dtype=F32, shape=None):
        v = self.AR[:, off:off + n32]
        if dtype == BF16:
            v = v.bitcast(BF16)
        if shape is not None:
            names = " ".join("a%d" % i for i in range(len(shape)))
            kw = {"a%d" % i: shape[i] for i in range(len(shape))}
            v = v.rearrange("p (%s) -> p %s" % (names, names), **kw)
        return v

    def mk(self, name, shape, dtype=F32):
        n = 1
        for x in shape:
            n *= x
        n32 = (n + 1) // 2 if dtype == BF16 else n
        n32 = (n32 + 7) // 8 * 8
        for seg in self.segs:
            if seg[1] - seg[0] >= n32:
                off = seg[0]
                seg[0] += n32
                return self.view(off, n32 if dtype == F32 else n32, dtype, None)[:, 0:n].rearrange(
                    "p (%s) -> p %s" % (" ".join("a%d" % i for i in range(len(shape))),
                                        " ".join("a%d" % i for i in range(len(shape)))),
                    **{"a%d" % i: shape[i] for i in range(len(shape))}), Buf(name)
        raise RuntimeError("arena full allocating " + name)

    def bank(self):
        i = self.rr_set[self._rr % len(self.rr_set)]
        self._rr += 1
        return self.PB[i], self.bPB[i]

    def build(self):
        nc = bass.Bass("TRN2", target_bir_lowering=False)
        self.nc = nc
        L = self.depth
        seq = self.seq
        dt = lambda name, shape, kind, d=F32: nc.dram_tensor(name, list(shape), d, kind=kind).ap()
        I, O = "ExternalInput", "ExternalOutput"
        self.xT = dt("xT", [D, seq], I)
        self.xsT = dt("xsT", [D, NS], I)
        self.csT = dt("csT", [D, 1 + NS], I)
        self.w_ada = dt("w_ada", [L, D, NMOD * D], I)
        self.smallp = dt("smallp", [L, 128, 144], I)
        self.bcp = dt("bcp", [L, 128, NBC], I)
        self.w_gu = [dt("w_ffn1_gu", [L, D, 2 * DFF], I), dt("w_ffn2_gu", [L, D, 2 * DFF], I)]
        self.w_dn = [dt("w_ffn1_down", [L, DFF, D], I), dt("w_ffn2_down", [L, DFF, D], I)]
        self.w_in = dt("w_in", [L, D, IN_DIM], I)
        self.w_out = dt("w_out", [L, D, D], I)
        self.constf = dt("constf", [128, NCF], I)
        self.constb = dt("constb", [128, NCB], I)
        self.ropeT = dt("ropeT", [128, seq // 128, 64], I)
        self.yT = dt("yT", [D, seq], O)
        self.ysT = dt("ysT", [D, NS], O)
        self.o_conv_p = dt("o_conv_p", [L, CONV_CH, 3], O)
        self.o_S_p = dt("o_S_p", [L, DN_H, DK, DV], O)
        self.o_k_p = dt("o_k_p", [L, 128, SW_KV * DH], O)
        self.o_v_p = dt("o_v_p", [L, 128, SW_KV * DH], O)
        self.st_conv = dt("st_conv", [L, NS, 3, CONV_CH], I)
        self.st_S = dt("st_S", [L, NS, DN_H, DK, DV], I)
        self.st_k = dt("st_k", [L, NS, 128, SW_KV * DH], I)
        self.st_v = dt("st_v", [L, NS, 128, SW_KV * DH], I)
        self.o_conv_s = dt("o_conv_s", [L, NS, 3, CONV_CH], O)
        self.o_S_s = dt("o_S_s", [L, NS, DN_H, DK, DV], O)
        self.o_k_s = dt("o_k_s", [L, NS, 128, SW_KV * DH], O)
        self.o_v_s = dt("o_v_s", [L, NS, 128, SW_KV * DH], O)

        with ExitStack() as es:
            self.es = es
            P = Prog(nc, es)
            self.P = P
            sbt = lambda name, shape, d=F32: es.enter_context(nc.sbuf_tensor(name, list(shape), d))
            self.AR = sbt("ARENA", [128, NAR])
            self.W = self.view(0, 33792, BF16)
            self.X = [self.view(O_X0, 4096, F32, [KD, TT]), self.view(O_X1, 4096, F32, [KD, TT])]
            self.H = self.view(O_H, 2048, BF16, [KD, TT])
            self.G = self.view(O_G, 5632, BF16, [NJ, TT])
            self.R = self.view(O_R, 512)
            self.TMP = self.view(O_TMP, 512)
            self.XS = sbt("XS", [128, KD, NS])
            self.HS = sbt("HS", [128, KD, NS], BF16)
            self.GS = sbt("GS", [128, NJ, NS], BF16)
            self.CS = sbt("CS", [128, KD, 1 + NS])
            self.SCS = sbt("SCS", [128, KD, 1 + NS], BF16)
            self.MODS = sbt("MODS", [128, NMOD * KD, 1 + NS])
            self.MA = lambda s_: self.MODS[:, (3 * s_ + 1) * KD:(3 * s_ + 2) * KD, :]
            self.MG = lambda s_: self.MODS[:, (3 * s_ + 2) * KD:(3 * s_ + 3) * KD, :]
            self.SMALL = sbt("SMALL", [128, 144])
            self.ONES = sbt("ONES", [128, 128], BF16)
            self.EPSD = sbt("EPSD", [128, 8])
            self.PB = [es.enter_context(nc.psum_tensor("pb%d" % i, [128, 512], F32)) for i in range(8)]
            self.bW = Buf("W")
            self.bX = [Buf("X0"), Buf("X1")]
            self.bH, self.bG, self.bR, self.bTMP = Buf("H"), Buf("G"), Buf("R"), Buf("TMP")
            self.bXS, self.bHS, self.bGS = Buf("XS"), Buf("HS"), Buf("GS")
            self.bCS, self.bSCS, self.bMODS = Buf("CS"), Buf("SCS"), Buf("MODS")
            self.bSMALL, self.bONES = Buf("SMALL"), Buf("ONES")
            self.bPB = [Buf("pb%d" % i, excl=True) for i in range(8)]
            self.bYT = [Buf("yT%d" % i) for i in range(self.ntile)]

            self.prologue()
            first = True
            for l in range(L):
                self.adaln(l)
                self.ffn(l, 0, src_is_input=first)
                first = False
                if self.stop == "ffn1":
                    break
                P.barrier()
                self.mixer(l)
                P.barrier()
                if self.stop == "mix":
                    break
                self.ffn(l, 2, src_is_input=False)
                P.barrier()
            self.epilogue()
            P.emit()
            self.stats = P.stats
        return nc

    def prologue(self):
        P = self.P
        P.dma("sp", self.XS[:], self.xsT.rearrange("(k p) m -> p k m", p=128), w=[self.bXS], sem="ld")
        P.dma("sp", self.CS[:], self.csT.rearrange("(k p) m -> p k m", p=128), w=[self.bCS], sem="ld")
        P.op("dve", lambda e: e.memset(self.ONES[:], 1.0), w=[self.bONES])
        for i, v in enumerate((D * EPS, DK * EPS, DH * EPS, 1.0, EPS, 0.0, 0.0, 0.0)):
            P.op("dve", lambda e, i=i, v=v: e.memset(self.EPSD[:, i:i + 1], v), w=[self.bONES])
        P.op("act", lambda e: e.activation(out=self.SCS[:], in_=self.CS[:], func=AF.Silu),
             r=[self.bCS], w=[self.bSCS])

    def epilogue(self):
        P = self.P
        P.dma("sp", self.ysT.rearrange("(k p) m -> p k m", p=128), self.XS[:], r=[self.bXS], sem="st")

    def adaln(self, l):
        P = self.P
        P.dma("sp", self.SMALL[:], self.smallp[l], w=[self.bSMALL], sem="ld")
        wv = self.w_ada[l].rearrange("(k p) n -> p k n", p=128)
        NP = 8
        PC = NMOD * D // NP
        CPP = PC // 128
        stage = [self.W[:, 0:KD * PC].rearrange("p (k n) -> p k n", k=KD),
                 self.W[:, KD * PC:2 * KD * PC].rearrange("p (k n) -> p k n", k=KD)]
        bst = [Buf("ada0"), Buf("ada1")]
        for q in range(NP):
            s = q % 2
            deps_w = [bst[s]] + ([self.bW] if q < 2 else [])
            P.dma("pool", stage[s], wv[:, :, q * PC:(q + 1) * PC], w=deps_w, sem="wl")
            pb, bpb = self.bank()
            pv = pb[:, 0:CPP * 17].rearrange("p (c m) -> p c m", c=CPP)
            for c in range(CPP):
                for k in range(KD):
                    P.op("pe", lambda e, s=s, c=c, k=k, pv=pv: e.matmul(
                        pv[:, c, :], stage[s][:, k, c * 128:(c + 1) * 128], self.SCS[:, k, :],
                        start=(k == 0), stop=(k == KD - 1)),
                        r=[bst[s], self.bSCS], w=[bpb])
            bias = bc(self.SMALL[:, q * CPP:q * CPP + 1], [[1, CPP], [0, 17]])
            P.op("dve", lambda e, q=q, pv=pv, bias=bias: e.tensor_tensor(
                out=self.MODS[:, q * CPP:(q + 1) * CPP, :], in0=pv, in1=bias, op=ALU.add),
                r=[bpb, self.bSMALL], w=[self.bMODS])
        self.bW.lw = None
        self.bW.rd = list(bst[0].rd) + list(bst[1].rd)
        for s in range(3):
            gcol = bc(self.SMALL[:, 72 + 8 * s:73 + 8 * s], [[1, KD], [0, 1 + NS]])
            P.op("dve", lambda e, s=s: e.tensor_scalar(
                out=self.MA(s), in0=self.MA(s), scalar1=1.0, scalar2=math.sqrt(D), op0=ALU.add, op1=ALU.mult),
                r=[self.bMODS], w=[self.bMODS])
            P.op("dve", lambda e, s=s, gcol=gcol: e.tensor_tensor(
                out=self.MA(s), in0=self.MA(s), in1=gcol, op=ALU.mult),
                r=[self.bMODS, self.bSMALL], w=[self.bMODS])
            P.op("dve", lambda e, s=s: e.tensor_scalar(
                out=self.MG(s), in0=self.MG(s), scalar1=(1.0 if s == 1 else 0.5), scalar2=None, op0=ALU.mult),
                r=[self.bMODS], w=[self.bMODS])

    def load_ffn_weights(self, l, which):
        P = self.P
        wi = 0 if which == 0 else 1
        gu = self.W[:, 0:KD * 2 * DFF].rearrange("p (k n) -> p k n", k=KD)
        dn = self.W[:, KD * 2 * DFF:KD * 2 * DFF + NJ * D].rearrange("p (j n) -> p j n", j=NJ)
        guv = self.w_gu[wi][l].rearrange("(k p) n -> p k n", p=128)
        dnv = self.w_dn[wi][l].rearrange("(j p) n -> p j n", p=128)
        self.bWgu = [Buf("Wgu%d" % i) for i in range(NJ // 2)]
        self.bWdn = [Buf("Wdn%d" % i) for i in range(NJ // 2)]
        for jb in range(NJ // 2):
            c0 = jb * 256
            for off in (0, DFF):
                P.dma("pool", gu[:, :, off + c0:off + c0 + 256], guv[:, :, off + c0:off + c0 + 256],
                      w=[self.bWgu[jb]] + ([self.bW] if jb == 0 else []), sem="wl")
        for jb in range(NJ // 2):
            P.dma("pool", dn[:, 2 * jb:2 * jb + 2, :], dnv[:, 2 * jb:2 * jb + 2, :], w=[self.bWdn[jb]], sem="wl")
        return gu, dn

    def norm_mod(self, xt, bx, T, s, hs, bh, sample, sqv, bsq):
        P = self.P
        ssq, bssq = self.bank()
        P.op("act", lambda e: e.activation(out=sqv, in_=xt, func=AF.Square), r=[bx], w=[bsq])
        for k in range(KD):
            P.op("pe", lambda e, k=k: e.matmul(ssq[:, 0:T], self.ONES[:], sqv[:, k, :],
                                               start=(k == 0), stop=(k == KD - 1)),
                 r=[self.bONES, bsq], w=[bssq])
        P.op("act", lambda e: e.activation(out=self.R[:, 0:T], in_=ssq[:, 0:T], func=AF.Ln, bias=self.EPSD[:, 0:1]),
             r=[bssq, self.bONES], w=[self.bR])
        P.op("act", lambda e: e.activation(out=self.R[:, 0:T], in_=self.R[:, 0:T], func=AF.Exp, scale=-0.5),
             r=[self.bR], w=[self.bR])
        shift = self.MODS[:, (3 * s) * KD:(3 * s + 1) * KD, :]
        if not sample:
            for k in range(KD):
                P.op("dve", lambda e, k=k: e.tensor_tensor(out=self.TMP[:, 0:T], in0=xt[:, k, :], in1=self.R[:, 0:T],
                                                           op=ALU.mult), r=[bx, self.bR], w=[self.bTMP])
                P.op("act", lambda e, k=k: e.activation(out=hs[:, k, :], in_=self.TMP[:, 0:T], func=AF.Identity,
                                                        scale=self.MA(s)[:, k, 0:1], bias=shift[:, k, 0:1]),
                     r=[self.bTMP, self.bMODS], w=[bh])
        else:
            rb = bc(self.R[:, 0:1], [[0, KD], [1, T]])
            tv = self.TMP[:, 0:KD * T].rearrange("p (k t) -> p k t", k=KD)
            P.op("dve", lambda e: e.tensor_tensor(out=tv, in0=xt, in1=rb, op=ALU.mult), r=[bx, self.bR], w=[self.bTMP])
            P.op("dve", lambda e: e.tensor_tensor(out=tv, in0=tv, in1=self.MA(s)[:, :, 1:1 + NS], op=ALU.mult),
                 r=[self.bTMP, self.bMODS], w=[self.bTMP])
            P.op("dve", lambda e: e.tensor_tensor(out=hs, in0=tv, in1=shift[:, :, 1:1 + NS], op=ALU.add),
                 r=[self.bTMP, self.bMODS], w=[bh])

    def ffn_tile(self, xt, bx, T, s, gu, dn, sample):
        P = self.P
        hs = self.HS[:] if sample else self.H[:, :, 0:T]
        bh = self.bHS if sample else self.bH
        gs = self.GS[:] if sample else self.G[:, :, 0:T]
        bg = self.bGS if sample else self.bG
        self.norm_mod(xt, bx, T, s, hs, bh, sample, gs[:, 0:KD, :], bg)
        for j in range(NJ):
            pa, bpa = self.bank()
            pbk, bpb = self.bank()
            for k in range(KD):
                P.op("pe", lambda e, j=j, k=k, pa=pa: e.matmul(pa[:, 0:T], gu[:, k, j * 128:(j + 1) * 128], hs[:, k, :],
                                                               start=(k == 0), stop=(k == KD - 1)),
                     r=[self.bWgu[j // 2], bh], w=[bpa])
            for k in range(KD):
                P.op("pe", lambda e, j=j, k=k, pbk=pbk: e.matmul(pbk[:, 0:T], gu[:, k, DFF + j * 128:DFF + (j + 1) * 128],
                                                                 hs[:, k, :], start=(k == 0), stop=(k == KD - 1)),
                     r=[self.bWgu[j // 2], bh], w=[bpb])
            P.op("act", lambda e, pa=pa: e.activation(out=self.TMP[:, 0:T], in_=pa[:, 0:T], func=AF.Silu),
                 r=[bpa], w=[self.bTMP])
            P.op("dve", lambda e, j=j, pbk=pbk: e.tensor_tensor(out=gs[:, j, :], in0=self.TMP[:, 0:T], in1=pbk[:, 0:T],
                                                                op=ALU.mult), r=[self.bTMP, bpb], w=[bg])
        for d in range(KD):
            py, bpy = self.bank()
            for j in range(NJ):
                P.op("pe", lambda e, d=d, j=j, py=py: e.matmul(py[:, 0:T], dn[:, j, d * 128:(d + 1) * 128], gs[:, j, :],
                                                               start=(j == 0), stop=(j == NJ - 1)),
                     r=[self.bWdn[j // 2], bg], w=[bpy])
            self.resid(xt, bx, T, s, d, py, bpy, sample)

    def resid(self, xt, bx, T, s, d, py, bpy, sample):
        P = self.P
        if not sample:
            P.op("dve", lambda e: e.scalar_tensor_tensor(
                out=xt[:, d, :], in0=py[:, 0:T], scalar=self.MG(s)[:, d, 0:1], in1=xt[:, d, :],
                op0=ALU.mult, op1=ALU.add), r=[bpy, self.bMODS, bx], w=[bx])
        else:
            P.op("dve", lambda e: e.tensor_tensor(out=self.TMP[:, 0:T], in0=py[:, 0:T],
                                                  in1=self.MG(s)[:, d, 1:1 + NS], op=ALU.mult),
                 r=[bpy, self.bMODS], w=[self.bTMP])
            P.op("dve", lambda e: e.tensor_tensor(out=xt[:, d, :], in0=xt[:, d, :], in1=self.TMP[:, 0:T],
                                                  op=ALU.add), r=[self.bTMP, bx], w=[bx])

    def ffn(self, l, s, src_is_input):
        P = self.P
        gu, dn = self.load_ffn_weights(l, s)
        src = (self.xT if src_is_input else self.yT).rearrange("(k p) t -> p k t", p=128)
        dst = self.yT.rearrange("(k p) t -> p k t", p=128)

        def load(i):
            rd = [] if src_is_input else [self.bYT[i]]
            P.dma("sp", self.X[i % 2][:], src[:, :, i * TT:(i + 1) * TT], r=rd, w=[self.bX[i % 2]], sem="xl")

        load(0)
        for i in range(self.ntile):
            if i + 1 < self.ntile:
                load(i + 1)
            self.ffn_tile(self.X[i % 2][:], self.bX[i % 2], TT, s, gu, dn, sample=False)
            P.dma("sp", dst[:, :, i * TT:(i + 1) * TT], self.X[i % 2][:], r=[self.bX[i % 2]], w=[self.bYT[i]], sem="xs")
        self.ffn_tile(self.XS[:], self.bXS, NS, s, gu, dn, sample=True)

    def mixer(self, l):
        P = self.P
        self.segs = [[15392, 33792], [O_X1, O_X1 + 4096], [O_G, O_G + 5632]]
        mk = self.mk
        self.rr_set = [3, 4, 5, 6, 7]
        WIN = self.view(O_WIN, 11296, BF16, [KD, IN_DIM])
        WOUT = self.view(O_WOUT, 4096, BF16, [KD, D])
        bWI, bWO = Buf("WIN"), Buf("WOUT")
        X0, bX0 = self.X[0], self.bX[0]
        H, bH = self.H, self.bH
        R, bR, TMP, bTMP = self.R, self.bR, self.TMP, self.bTMP
        CF, bCF = mk("CF", [NCF])
        CB, bCB = mk("CB", [NCB], BF16)
        BCP, bBCP = mk("BCP", [NBC])
        SC, bSC = mk("SC", [96])
        bSCc, bSCs = Buf("SCconst"), Buf("SCswa")
        segs_sample = [list(x) for x in self.segs]
        XP, bXP = mk("XP", [NCC, 3 + TT])
        QT, bQT = mk("QT", [DN_H, TT], BF16)
        KT, bKT = mk("KT", [DN_H, TT], BF16)
        VT, bVT = mk("VT", [DN_H, TT], BF16)
        CAT, bCAT = mk("CAT", [KD, TT], BF16)
        ROPE, bROPE = mk("ROPE", [4, 64])
        SQB, bSQB = mk("SQB", [TT], BF16)
        GU, bGU = mk("GU", [DN_H, 128])
        ET, bET = mk("ET", [DN_H, 128])
        EDS, bEDS = mk("EDS", [DN_H, 128])
        EGs = [mk("EGROW0", [DN_H, 128], BF16), mk("EGROW1", [DN_H, 128], BF16)]
        ONB, bONB = mk("ONB", [DN_H, 128], BF16)
        SC1, bSC1 = mk("SC1", [64])
        S32, bS32 = mk("S32", [DN_H, 128])
        bfn = ("DT", "DS", "A", "B", "QKT", "A0", "B0", "X2", "Y2", "T", "U", "NWR", "NWU",
               "KBG", "KDD", "VB", "NWT", "QDT", "VN", "SBF", "ON2", "ZG", "ZG1", "DT1", "DS1")
        bft = {}
        for nm in bfn:
            bft[nm] = mk(nm, [DN_H, 128], BF16)
        QS, bQS = mk("QS", [SW_H, DH])
        KS, bKS = mk("KS", [SW_KV, DH])
        KR, bKR = mk("KR", [SW_KV, DH])
        VR, bVR = mk("VR", [SW_KV, DH])
        QF, bQF = mk("QF", [4, 2, DH], BF16)
        KF, bKF = mk("KF", [SW_KV, DH], BF16)
        QTS, bQTS = mk("QTS", [4, 128], BF16)
        KTS, bKTS = mk("KTS", [2, 128], BF16)
        VE, bVE = mk("VE", [2, SW_KV, DH + 1], BF16)
        PT, bPT = {}, {}
        for hk in range(2):
            for kb in range(2):
                PT[hk, kb], bPT[hk, kb] = mk("PT%d%d" % (hk, kb), [4, 128], BF16)
        OSB, bOSB = mk("OSB", [SW_H, DH], BF16)

        UT = CF[:, 0:128]
        ONESF = CF[:, 128:256]
        NEGMT = bc(CF[:, 256:257], [[0, DN_H], [1, 128]])
        NEGMS = bc(CF[:, 384:385], [[0, DN_H], [1, 128]])
        h4 = lambda v: bc(v[:, 0:1], [[0, DN_H], [1, 128]])
        MD8 = h4(CB[:, 1152:1280])
        MLm = [h4(CB[:, 1280 + li * 128:1408 + li * 128]) for li in range(4)]
        MUm = [h4(CB[:, 1792 + li * 128:1920 + li * 128]) for li in range(4)]
        IDB4 = h4(CB[:, 0:128])
        IDB = CB[:, 0:128]
        MASKC = CB[:, 128:640]
        MASKP = CB[:, 640:1152]
        ALOG, DTB, SINK = BCP[:, 0:4], BCP[:, 4:8], BCP[:, 8:16]
        NORMW4, QNW8, KNW2 = BCP[:, 16:528], BCP[:, 528:1040], BCP[:, 1040:1168]
        NEGA, ESINK = SC[:, 64:68], SC[:, 68:76]
        sc = lambda i, n=4: SC[:, i * 4:i * 4 + n]
        SCV = [[sc(i) for i in range(16)], [SC1[:, i * 4:i * 4 + 4] for i in range(16)]]
        bSCp = [bSC, bSC1]
        DTs = [bft["DT"], bft["DT1"]]
        DSs = [bft["DS"], bft["DS1"]]
        SSQ10, RS10, DEN8 = SC[:, 76:86], SC[:, 86:96], SC[:, 96 - 8:96]
        DEN8, bDEN = mk("DEN8", [16])

        winv = self.w_in[l].rearrange("(k p) n -> p k n", p=128)
        woutv = self.w_out[l].rearrange("(k p) n -> p k n", p=128)
        for k in range(KD):
            P.dma("pool", WIN[:, k, :], winv[:, k, :], w=[bWI], sem="wl")
        P.dma("pool", WOUT[:, 0:4, :], woutv[:, 0:4, :], w=[bWO], sem="wl")
        P.dma("pool", WOUT[:, 4:8, :], woutv[:, 4:8, :], w=[bWO], sem="wl")
        P.dma("pool", CB, self.constb, w=[bCB], sem="wl")
        P.dma("sp", CF, self.constf, w=[bCF], sem="ld")
        P.dma("sp", BCP, self.bcp[l], w=[bBCP], sem="ld")
        P.op("act", lambda e: e.activation(out=NEGA, in_=ALOG, func=AF.Exp), r=[bBCP], w=[bSCc])
        P.op("dve", lambda e: e.tensor_scalar(out=NEGA, in0=NEGA, scalar1=-1.0, scalar2=None, op0=ALU.mult),
             r=[bSCc], w=[bSCc])
        P.op("act", lambda e: e.activation(out=ESINK, in_=SINK, func=AF.Exp), r=[bBCP], w=[bSCc])
        P.op("dve", lambda e: e.tensor_scalar(out=KNW2, in0=KNW2, scalar1=float(math.sqrt(DH)), scalar2=None,
                                              op0=ALU.mult), r=[bBCP], w=[bBCP])
        NORMWB, bNWB = mk("NORMWB", [DN_H * DV], BF16)
        P.op("act", lambda e: e.activation(out=NORMWB, in_=NORMW4, func=AF.Copy), r=[bBCP], w=[bNWB])
        P.op("dve", lambda e: e.memset(S32, 0.0), w=[bS32])
        P.op("dve", lambda e: e.memset(bft["SBF"][0], 0.0), w=[bft["SBF"][1]])
        P.op("dve", lambda e: e.memset(XP[:, :, 0:3], 0.0), w=[bXP])
        P.op("dve", lambda e: e.memset(VE[:, :, :, DH:DH + 1], 1.0), w=[bVE])

        src = self.yT.rearrange("(k p) t -> p k t", p=128)
        def tile(i):
            P.dma("sp", X0, src[:, :, i * TT:(i + 1) * TT], r=[self.bYT[i]], w=[bX0], sem="xl")
            P.dma("sp", ROPE, self.ropeT[:, i * 4:(i + 1) * 4, :], w=[bROPE], sem="xl")
            self.norm_mod(X0, bX0, TT, 1, H, bH, False, CAT, bCAT)
            for c in range(NCC):
                pp, bpp = self.bank()
                for k in range(KD):
                    P.op("pe", lambda e, c=c, k=k, pp=pp: e.matmul(pp[:, :], WIN[:, k, c * 128:(c + 1) * 128], H[:, k, :],
                                                                   start=(k == 0), stop=(k == KD - 1)),
                         r=[bWI, bH], w=[bpp])
                P.op("act", lambda e, c=c, pp=pp: e.activation(out=XP[:, c, 3:3 + TT], in_=pp[:, :], func=AF.Copy),
                     r=[bpp], w=[bXP])
                cw = lambda j, c=c: self.SMALL[:, 96 + c * 4 + j:97 + c * 4 + j]
                acc, bacc = (TMP, bTMP) if c % 2 == 0 else (R, bR)
                ceng = "dve"
                P.op(ceng, lambda e, c=c, cw=cw, acc=acc: e.tensor_scalar(out=acc, in0=XP[:, c, 0:TT], scalar1=cw(0),
                                                                           scalar2=None, op0=ALU.mult),
                     r=[bXP, self.bSMALL], w=[bacc])
                for j in range(1, 4):
                    P.op(ceng, lambda e, c=c, j=j, cw=cw, acc=acc: e.scalar_tensor_tensor(
                        out=acc, in0=XP[:, c, j:j + TT], scalar=cw(j), in1=acc, op0=ALU.mult, op1=ALU.add),
                        r=[bXP, self.bSMALL, bacc], w=[bacc])
                dst, bdst = ((QT, bQT), (KT, bKT), (VT, bVT))[c // 4]
                P.op("act", lambda e, c=c, dst=dst, acc=acc: e.activation(out=dst[:, c % 4, :], in_=acc, func=AF.Silu),
                     r=[bacc], w=[bdst])
            for c in range(8):
                isq = c < 4
                h = c % 4
                dst, bdst = (QT, bQT) if isq else (KT, bKT)
                P.op("act", lambda e, dst=dst, h=h: e.activation(out=SQB, in_=dst[:, h, :], func=AF.Square),
                     r=[bdst], w=[bSQB])
                ps, bps = self.bank()
                P.op("pe", lambda e, ps=ps: e.matmul(ps[:, :], self.ONES[:], SQB, start=True, stop=True),
                     r=[self.bONES, bSQB], w=[bps])
                P.op("act", lambda e, ps=ps, isq=isq: e.activation(
                    out=TMP, in_=ps[:, :], func=AF.Ln, scale=(float(DK) if isq else 1.0),
                    bias=(self.EPSD[:, 1:2] if isq else self.EPSD[:, 4:5])), r=[bps, self.bONES], w=[bTMP])
                P.op("act", lambda e: e.activation(out=TMP, in_=TMP, func=AF.Exp, scale=-0.5), r=[bTMP], w=[bTMP])
                P.op("dve", lambda e, dst=dst, h=h: e.tensor_tensor(out=dst[:, h, :], in0=dst[:, h, :], in1=TMP, op=ALU.mult),
                     r=[bdst, bTMP], w=[bdst])
            if i == self.ntile - 1 and "outs" not in SKIP:
                P.dma("sp", self.o_conv_p[l].rearrange("(c p) j -> p c j", p=128), XP[:, :, TT:TT + 3],
                      r=[bXP], sem="st")
            P.op("act", lambda e: e.activation(out=XP[:, :, 0:3], in_=XP[:, :, TT:TT + 3], func=AF.Copy),
                 r=[bXP], w=[bXP])

            PZ, bPZ = self.PB[0], self.bPB[0]
            PQ, bPQ = self.PB[1], self.bPB[1]
            PKV, bPKV = self.PB[2], self.bPB[2]
            ZGs = [bft["ZG"], bft["ZG1"]]

            def proj(bb):
                tsl_ = slice(bb * 128, bb * 128 + 128)
                for (pt, bpt, c0, ncol, o0) in ((PZ, bPZ, C_Z, 512, 0), (PQ, bPQ, C_Q, 512, 0),
                                                (PKV, bPKV, C_K, 256, 0), (PKV, bPKV, C_BA, 8, 256)):
                    for k in range(KD):
                        P.op("pe", lambda e, pt=pt, c0=c0, ncol=ncol, o0=o0, k=k: e.matmul(
                            pt[:, o0:o0 + ncol], H[:, k, tsl_], WIN[:, k, c0:c0 + ncol],
                            start=(k == 0), stop=(k == KD - 1)), r=[bWI, bH], w=[bpt])
                if "dn" not in SKIP:
                    ZG0, bZG0 = ZGs[(4 * i + bb) % 2]
                    P.op("act", lambda e: e.activation(out=ZG0.rearrange("p h i -> p (h i)"), in_=PZ[:, :], func=AF.Silu),
                         r=[bPZ], w=[bZG0])
                    P.op("pool", lambda e: e.tensor_tensor(out=ZG0.rearrange("p h i -> p (h i)"),
                                                           in0=ZG0.rearrange("p h i -> p (h i)"), in1=NORMWB, op=ALU.mult),
                         r=[bZG0, bNWB], w=[bZG0])

            def dn_pre(bb):
                par_ = (4 * i + bb) % 2
                (BETA, XA_, AXV, E1, L1, G4, GC, NGC, BGC, LNB, EGC, EKD, BEG, EGL, RS4, SSQ4) = SCV[par_]
                bSCx = bSCp[par_]
                DT_, bDT = DTs[par_]
                DS_, bDS = DSs[par_]
                EGROW, bEGROW = EGs[par_]
                PB_ = PKV[:, 256:260]
                PA_ = PKV[:, 260:264]
                P.op("act", lambda e: e.activation(out=BETA, in_=PB_, func=AF.Exp, scale=-1.0), r=[bPKV], w=[bSCx])
                P.op("dve", lambda e: e.tensor_scalar(out=BETA, in0=BETA, scalar1=1.0, scalar2=None, op0=ALU.add),
                     r=[bSCx], w=[bSCx])
                P.op("act", lambda e: e.activation(out=LNB, in_=BETA, func=AF.Ln), r=[bSCx], w=[bSCx])
                P.op("dve", lambda e: e.reciprocal(out=BETA, in_=BETA), r=[bSCx], w=[bSCx])
                P.op("dve", lambda e: e.tensor_tensor(out=XA_, in0=PA_, in1=DTB, op=ALU.add), r=[bPKV, bBCP], w=[bSCx])
                P.op("act", lambda e: e.activation(out=AXV, in_=XA_, func=AF.Abs), r=[bSCx], w=[bSCx])
                P.op("act", lambda e: e.activation(out=E1, in_=AXV, func=AF.Exp, scale=-1.0), r=[bSCx], w=[bSCx])
                P.op("act", lambda e: e.activation(out=L1, in_=E1, func=AF.Ln, bias=self.EPSD[:, 3:4]),
                     r=[bSCx, self.bONES], w=[bSCx])
                P.op("dve", lambda e: e.scalar_tensor_tensor(out=G4, in0=XA_, scalar=0.0, in1=L1, op0=ALU.max,
                                                             op1=ALU.add), r=[bSCx], w=[bSCx])
                P.op("dve", lambda e: e.tensor_tensor(out=G4, in0=G4, in1=NEGA, op=ALU.mult), r=[bSCx, bSCc], w=[bSCx])
                pg, bpg = self.bank()
                P.op("pe", lambda e, pg=pg: e.matmul(pg[:, 0:4], UT, G4, start=True, stop=True), r=[bCF, bSCx], w=[bpg])
                P.op("dve", lambda e, pg=pg: e.tensor_copy(out=GC, in_=pg[:, 0:4]), r=[bpg], w=[bSCx])
                P.op("dve", lambda e: e.tensor_tensor(out=GU, in0=bc(UT, [[0, DN_H], [1, 128]]),
                                                      in1=bc(G4, [[1, DN_H], [0, 128]]), op=ALU.mult),
                     r=[bCF, bSCx], w=[bGU])
                pr, bpr = self.bank()
                prv = pr[:, :].rearrange("p (h i) -> p h i", h=DN_H)
                P.op("pe", lambda e, pr=pr: e.matmul(pr[:, :], ONESF, GU.rearrange("p h i -> p (h i)"),
                                                     start=True, stop=True), r=[bCF, bGU], w=[bpr])
                P.op("dve", lambda e: e.tensor_scalar(out=NGC, in0=GC, scalar1=-1.0, scalar2=None, op0=ALU.mult),
                     r=[bSCx], w=[bSCx])
                P.op("dve", lambda e: e.tensor_tensor(out=BGC, in0=GC, in1=LNB, op=ALU.subtract), r=[bSCx], w=[bSCx])
                P.op("act", lambda e: e.activation(out=EGC, in_=GC, func=AF.Exp), r=[bSCx], w=[bSCx])
                P.op("dve", lambda e: e.tensor_tensor(out=BEG, in0=BETA, in1=EGC, op=ALU.mult), r=[bSCx], w=[bSCx])
                P.op("act", lambda e, prv=prv: e.activation(out=EGL, in_=prv[:, :, 127], func=AF.Exp), r=[bpr], w=[bSCx])
                P.op("dve", lambda e, prv=prv: e.tensor_tensor(out=EKD, in0=prv[:, :, 127], in1=GC, op=ALU.subtract),
                     r=[bpr, bSCx], w=[bSCx])
                P.op("act", lambda e: e.activation(out=EKD, in_=EKD, func=AF.Exp), r=[bSCx], w=[bSCx])
                P.op("act", lambda e, pr=pr: e.activation(out=EGROW.rearrange("p h i -> p (h i)"), in_=pr[:, :],
                                                          func=AF.Exp), r=[bpr], w=[bEGROW])
                P.op("dve", lambda e, prv=prv: e.tensor_tensor(out=ET, in0=prv, in1=NEGMT, op=ALU.add),
                     r=[bpr, bCF], w=[bET])
                P.op("dve", lambda e, prv=prv: e.scalar_tensor_tensor(
                    out=EDS, in0=prv, scalar=-1.0, in1=NEGMS, op0=ALU.mult, op1=ALU.add), r=[bpr, bCF], w=[bEDS])
                for h in range(DN_H):
                    P.op("act", lambda e, h=h: e.activation(out=DT_[:, h, :], in_=ET[:, h, :], func=AF.Exp,
                                                            bias=NGC[:, h:h + 1]), r=[bET, bSCx], w=[bDT])
                    P.op("act", lambda e, h=h: e.activation(out=DS_[:, h, :], in_=EDS[:, h, :], func=AF.Exp,
                                                            bias=BGC[:, h:h + 1]), r=[bEDS, bSCx], w=[bDS])

            def block(b):
                n = 4 * i + b
                t0 = b * 128
                tsl = slice(t0, t0 + 128)
                par = n % 2
                if b == 0:
                    proj(0)
                    if "dn" not in SKIP:
                        dn_pre(0)
                fl = lambda v: v.rearrange("p h i -> p (h i)")
                v3 = lambda v: v.rearrange("p (h i) -> p h i", h=DN_H)
                hb = lambda v: bc(v, [[1, DN_H], [0, 128]])

                def dn_gen():
                    (BETA, XA_, AXV, E1, L1, G4, GC, NGC, BGC, LNB, EGC, EKD, BEG, EGL, RS4, SSQ4) = SCV[n % 2]
                    bSCx = bSCp[n % 2]
                    DT_, bDT = DTs[n % 2]
                    DS_, bDS = DSs[n % 2]
                    EGROW, bEGROW = EGs[n % 2]
                    pkk, bpkk = self.bank()
                    pkq, bpkq = self.bank()
                    for h in range(DN_H):
                        P.op("pe", lambda e, h=h, pkk=pkk: e.matmul(pkk[:, h * 128:(h + 1) * 128], KT[:, h, tsl], KT[:, h, tsl],
                                                                    start=True, stop=True), r=[bKT], w=[bpkk])
                    for h in range(DN_H):
                        P.op("pe", lambda e, h=h, pkq=pkq: e.matmul(pkq[:, h * 128:(h + 1) * 128], KT[:, h, tsl], QT[:, h, tsl],
                                                                    start=True, stop=True), r=[bKT, bQT], w=[bpkq])
                    A_, bA = bft["A"]
                    B_, bB = bft["B"]
                    QKT, bQKT = bft["QKT"]
                    fl = lambda v: v.rearrange("p h i -> p (h i)")
                    P.op("dve", lambda e, pkk=pkk: e.tensor_tensor(out=fl(A_), in0=pkk[:, :], in1=fl(DS_), op=ALU.mult),
                         r=[bpkk, bDS], w=[bA])
                    P.op("dve", lambda e, pkq=pkq: e.tensor_tensor(out=fl(QKT), in0=pkq[:, :], in1=fl(DT_), op=ALU.mult),
                         r=[bpkq, bDT], w=[bQKT])
                    yield
                    ptb, bptb = self.bank()
                    ptbv = ptb[:, 0:256].bitcast(BF16)
                    for h in range(DN_H):
                        P.op("pe", lambda e, h=h, ptbv=ptbv: e.transpose(ptbv[:, h * 128:(h + 1) * 128], A_[:, h, :], IDB),
                             r=[bA, bCB], w=[bptb])
                    P.op("act", lambda e, ptbv=ptbv: e.activation(out=fl(B_), in_=ptbv, func=AF.Copy), r=[bptb], w=[bB])
                    if DNCUT <= 3:
                        return
                    yield
                    A0, bA0 = bft["A0"]
                    B0, bB0 = bft["B0"]
                    X2, bX2 = bft["X2"]
                    Y2, bY2 = bft["Y2"]
                    T_, bT_ = bft["T"]
                    U_, bU_ = bft["U"]
                    NWR, bNWR = bft["NWR"]
                    NWU, bNWU = bft["NWU"]
                    P.op("dve", lambda e: e.tensor_tensor(out=A0, in0=A_, in1=MD8, op=ALU.mult), r=[bA, bCB], w=[bA0])
                    P.op("dve", lambda e: e.tensor_tensor(out=B0, in0=B_, in1=MD8, op=ALU.mult), r=[bB, bCB], w=[bB0])
                    P.op("dve", lambda e: e.scalar_tensor_tensor(out=T_, in0=A0, scalar=-1.0, in1=IDB4, op0=ALU.mult,
                                                                 op1=ALU.add), r=[bA0, bCB], w=[bT_])
                    P.op("dve", lambda e: e.scalar_tensor_tensor(out=U_, in0=B0, scalar=-1.0, in1=IDB4, op0=ALU.mult,
                                                                 op1=ALU.add), r=[bB0, bCB], w=[bU_])

                    def mm4(lhs, blhs, rhs, brhs, pre=None):
                        pm, bpm = self.bank()
                        for h in range(DN_H):
                            if pre is not None:
                                P.op("pe", lambda e, h=h, pm=pm: e.matmul(pm[:, h * 128:(h + 1) * 128], IDB, pre[0][:, h, :],
                                                                          start=True, stop=False), r=[bCB, pre[1]], w=[bpm])
                            P.op("pe", lambda e, h=h, pm=pm: e.matmul(pm[:, h * 128:(h + 1) * 128], lhs[:, h, :], rhs[:, h, :],
                                                                      start=(pre is None), stop=True), r=[blhs, brhs], w=[bpm])
                        return pm, bpm

                    def evac(eng, pm, bpm, dst, bdst, neg=False):
                        if eng == "act":
                            P.op("act", lambda e: e.activation(out=fl(dst), in_=pm[:, :], func=AF.Identity,
                                                               scale=(-1.0 if neg else 1.0)), r=[bpm], w=[bdst])
                        else:
                            P.op("dve", lambda e: e.tensor_scalar(out=fl(dst), in0=pm[:, :], scalar1=(-1.0 if neg else 1.0),
                                                                  scalar2=None, op0=ALU.mult), r=[bpm], w=[bdst])
                    yield
                    px_ = mm4(B0, bB0, A0, bA0)
                    py2 = mm4(A0, bA0, B0, bB0)
                    evac("act", *px_, X2, bX2)
                    evac("dve", *py2, Y2, bY2)
                    yield
                    pt_ = mm4(Y2, bY2, T_, bT_, pre=(T_, bT_))
                    pu_ = mm4(X2, bX2, U_, bU_, pre=(U_, bU_))
                    evac("act", *pt_, T_, bT_)
                    evac("dve", *pu_, U_, bU_)
                    yield
                    px_ = mm4(Y2, bY2, X2, bX2)
                    py2 = mm4(X2, bX2, Y2, bY2)
                    evac("act", *px_, A0, bA0)
                    evac("dve", *py2, B0, bB0)
                    yield
                    pt_ = mm4(B0, bB0, T_, bT_, pre=(T_, bT_))
                    pu_ = mm4(A0, bA0, U_, bU_, pre=(U_, bU_))
                    evac("act", *pt_, T_, bT_)
                    evac("dve", *pu_, U_, bU_)
                    yield
                    ptk, bptk = self.bank()
                    ptkv = ptk[:, 0:256].bitcast(BF16)
                    ptv, bptv = self.bank()
                    ptvv = ptv[:, 0:256].bitcast(BF16)
                    for h in range(DN_H):
                        P.op("pe", lambda e, h=h, ptkv=ptkv: e.transpose(ptkv[:, h * 128:(h + 1) * 128], KT[:, h, tsl], IDB),
                             r=[bKT, bCB], w=[bptk])
                    for h in range(DN_H):
                        P.op("pe", lambda e, h=h, ptvv=ptvv: e.transpose(ptvv[:, h * 128:(h + 1) * 128], VT[:, h, tsl], IDB),
                             r=[bVT, bCB], w=[bptv])
                    KBG, bKBG = bft["KBG"]
                    KDD, bKDD = bft["KDD"]
                    VB, bVB = bft["VB"]
                    hb = lambda v: bc(v, [[1, DN_H], [0, 128]])
                    v3 = lambda v: v.rearrange("p (h i) -> p h i", h=DN_H)
                    P.op("dve", lambda e, ptkv=ptkv: e.tensor_tensor(out=KBG, in0=v3(ptkv), in1=hb(BEG), op=ALU.mult),
                         r=[bptk, bSCx], w=[bKBG])
                    P.op("dve", lambda e, ptkv=ptkv: e.tensor_tensor(out=KDD, in0=v3(ptkv), in1=hb(EKD), op=ALU.mult),
                         r=[bptk, bSCx], w=[bKDD])
                    P.op("dve", lambda e, ptvv=ptvv: e.tensor_tensor(out=VB, in0=v3(ptvv), in1=hb(BETA), op=ALU.mult),
                         r=[bptv, bSCx], w=[bVB])
                    QDT, bQDT = bft["QDT"]
                    P.op("dve", lambda e: e.tensor_tensor(out=QDT, in0=QT[:, :, tsl], in1=EGROW, op=ALU.mult),
                         r=[bQT, bEGROW], w=[bQDT])
                    def evacm(pm, bpm, dst, bdst, mask):
                        P.op("dve", lambda e: e.scalar_tensor_tensor(out=dst, in0=v3(pm[:, :]), scalar=-1.0, in1=mask,
                                                                     op0=ALU.mult, op1=ALU.mult), r=[bpm, bCB], w=[bdst])
                    for li in range(4):
                        last = li == 3
                        pwu = mm4(A_, bA, U_, bU_)
                        if not last:
                            pwr = mm4(B_, bB, T_, bT_)
                            evacm(*pwr, NWR, bNWR, MLm[li])
                        evacm(*pwu, NWU, bNWU, MUm[li])
                        yield
                        pu_ = mm4(T_, bT_, NWU, bNWU, pre=(U_, bU_))
                        if not last:
                            pt_ = mm4(U_, bU_, NWR, bNWR, pre=(T_, bT_))
                        evac("dve", *pu_, U_, bU_)
                        if not last:
                            evac("act", *pt_, T_, bT_)
                        yield
                    PBF, bPBF = U_, bU_
                    if DNCUT <= 4:
                        return
                    yield
                    pw, bpw = self.bank()
                    for h in range(DN_H):
                        P.op("pe", lambda e, h=h, pw=pw: e.matmul(pw[:, h * 128:(h + 1) * 128], KBG[:, h, :], PBF[:, h, :],
                                                                  start=True, stop=True), r=[bKBG, bPBF], w=[bpw])
                    NWT, bNWT = bft["NWT"]
                    P.op("act", lambda e, pw=pw: e.activation(out=fl(NWT), in_=pw[:, :], func=AF.Identity, scale=-1.0),
                         r=[bpw], w=[bNWT])
                    SBF, bSBF = bft["SBF"]
                    VN, bVN = bft["VN"]
                    pvn, bpvn = self.bank()
                    for h in range(DN_H):
                        P.op("pe", lambda e, h=h, pvn=pvn: e.matmul(pvn[:, h * 128:(h + 1) * 128], PBF[:, h, :], VB[:, h, :],
                                                                    start=True, stop=False), r=[bPBF, bVB], w=[bpvn])
                        P.op("pe", lambda e, h=h, pvn=pvn: e.matmul(pvn[:, h * 128:(h + 1) * 128], NWT[:, h, :], SBF[:, h, :],
                                                                    start=False, stop=True), r=[bNWT, bSBF], w=[bpvn])
                    P.op("act", lambda e, pvn=pvn: e.activation(out=fl(VN), in_=pvn[:, :], func=AF.Copy), r=[bpvn], w=[bVN])
                    yield
                    po, bpo = self.bank()
                    for h in range(DN_H):
                        P.op("pe", lambda e, h=h, po=po: e.matmul(po[:, h * 128:(h + 1) * 128], QDT[:, h, :], SBF[:, h, :],
                                                                  start=True, stop=False), r=[bQDT, bSBF], w=[bpo])
                        P.op("pe", lambda e, h=h, po=po: e.matmul(po[:, h * 128:(h + 1) * 128], QKT[:, h, :], VN[:, h, :],
                                                                  start=False, stop=True), r=[bQKT, bVN], w=[bpo])
                    psu, bpsu = self.bank()
                    for h in range(DN_H):
                        P.op("pe", lambda e, h=h, psu=psu: e.matmul(psu[:, h * 128:(h + 1) * 128], KDD[:, h, :], VN[:, h, :],
                                                                    start=True, stop=True), r=[bKDD, bVN], w=[bpsu])
                    P.op("dve", lambda e: e.tensor_tensor(out=S32, in0=S32, in1=hb(EGL), op=ALU.mult),
                         r=[bS32, bSCx], w=[bS32])
                    P.op("dve", lambda e, psu=psu: e.tensor_tensor(out=fl(S32), in0=fl(S32), in1=psu[:, :], op=ALU.add),
                         r=[bS32, bpsu], w=[bS32])
                    P.op("act", lambda e: e.activation(out=fl(SBF), in_=fl(S32), func=AF.Copy), r=[bS32], w=[bSBF])
                    if DNCUT <= 6:
                        return
                    yield
                    ZG, bZG = ZGs[n % 2]
                    P.op("act", lambda e, po=po: e.activation(out=SQB, in_=po[:, :], func=AF.Square), r=[bpo], w=[bSQB])
                    P.op("dve", lambda e: e.tensor_reduce(out=SSQ4, in_=v3(SQB), axis=AX.X, op=ALU.add), r=[bSQB], w=[bSCx])
                    P.op("act", lambda e: e.activation(out=RS4, in_=SSQ4, func=AF.Ln, scale=1.0 / DV,
                                                       bias=self.EPSD[:, 4:5]), r=[bSCx, self.bONES], w=[bSCx])
                    P.op("act", lambda e: e.activation(out=RS4, in_=RS4, func=AF.Exp, scale=-0.5), r=[bSCx], w=[bSCx])
                    P.op("dve", lambda e, po=po: e.tensor_tensor(out=ONB, in0=v3(po[:, :]), in1=hb(RS4), op=ALU.mult),
                         r=[bpo, bSCx], w=[bONB])
                    ON2, bON2 = bft["ON2"]
                    P.op("dve", lambda e: e.tensor_tensor(out=ON2, in0=ONB, in1=ZG, op=ALU.mult), r=[bONB, bZG], w=[bON2])
                    yield
                    pto, bpto = self.bank()
                    ptov = pto[:, 0:256].bitcast(BF16)
                    for h in range(DN_H):
                        P.op("pe", lambda e, h=h, ptov=ptov: e.transpose(ptov[:, h * 128:(h + 1) * 128], ON2[:, h, :], IDB),
                             r=[bON2, bCB], w=[bpto])
                    P.op("act", lambda e, ptov=ptov: e.activation(out=CAT[:, 0:4, tsl], in_=v3(ptov), func=AF.Copy),
                         r=[bpto], w=[bCAT])

                def swa_gen():
                    v8 = lambda v: v.rearrange("p (h d) -> p h d", h=SW_H)
                    v2 = lambda v: v.rearrange("p (h d) -> p h d", h=SW_KV)
                    P.op("act", lambda e, PQ=PQ: e.activation(out=TMP, in_=PQ[:, :], func=AF.Square), r=[bPQ], w=[bTMP])
                    P.op("dve", lambda e: e.tensor_reduce(out=SSQ10[:, 0:8], in_=v8(TMP), axis=AX.X, op=ALU.add),
                         r=[bTMP], w=[bSCs])
                    P.op("act", lambda e, PKV=PKV: e.activation(out=R[:, 0:128], in_=PKV[:, 0:128], func=AF.Square),
                         r=[bPKV], w=[bR])
                    P.op("dve", lambda e: e.tensor_reduce(out=SSQ10[:, 8:10], in_=v2(R[:, 0:128]), axis=AX.X, op=ALU.add),
                         r=[bR], w=[bSCs])
                    P.op("act", lambda e: e.activation(out=RS10, in_=SSQ10, func=AF.Ln, bias=self.EPSD[:, 2:3]),
                         r=[bSCs, self.bONES], w=[bSCs])
                    P.op("act", lambda e: e.activation(out=RS10, in_=RS10, func=AF.Exp, scale=-0.5), r=[bSCs], w=[bSCs])
                    yield
                    P.op("dve", lambda e, PQ=PQ: e.tensor_tensor(out=QS, in0=v8(PQ[:, :]), in1=bc(RS10[:, 0:8], [[1, 8], [0, DH]]),
                                                                 op=ALU.mult), r=[bPQ, bSCs], w=[bQS])
                    P.op("dve", lambda e: e.tensor_tensor(out=QS, in0=QS, in1=v8(QNW8), op=ALU.mult), r=[bQS, bBCP], w=[bQS])
                    P.op("dve", lambda e, PKV=PKV: e.tensor_tensor(out=KS, in0=v2(PKV[:, 0:128]),
                                                                   in1=bc(RS10[:, 8:10], [[1, 2], [0, DH]]), op=ALU.mult),
                         r=[bPKV, bSCs], w=[bKS])
                    P.op("dve", lambda e: e.tensor_tensor(out=KS, in0=KS, in1=v2(KNW2), op=ALU.mult), r=[bKS, bBCP], w=[bKS])
                    yield
                    def rope(x1, x2, cosb, sinb, t1, t2, d1, d2, bsrc, bdst_):
                        P.op("pool", lambda e: e.tensor_tensor(out=t1, in0=x1, in1=cosb, op=ALU.mult),
                             r=[bsrc, bROPE], w=[bTMP])
                        P.op("pool", lambda e: e.tensor_tensor(out=t2, in0=x2, in1=sinb, op=ALU.mult),
                             r=[bsrc, bROPE], w=[bTMP])
                        P.op("pool", lambda e: e.tensor_tensor(out=d1, in0=t1, in1=t2, op=ALU.subtract),
                             r=[bTMP], w=[bdst_])
                        P.op("pool", lambda e: e.tensor_tensor(out=t1, in0=x2, in1=cosb, op=ALU.mult),
                             r=[bsrc, bROPE, bTMP], w=[bTMP])
                        P.op("pool", lambda e: e.tensor_tensor(out=t2, in0=x1, in1=sinb, op=ALU.mult),
                             r=[bsrc, bROPE, bTMP], w=[bTMP])
                        P.op("pool", lambda e: e.tensor_tensor(out=d2, in0=t1, in1=t2, op=ALU.add),
                             r=[bTMP], w=[bdst_])
                    q4 = QS.rearrange("p (a b) d -> p a b d", a=2)
                    rope(q4[:, :, :, 0:32], q4[:, :, :, 32:64],
                         bc(ROPE[:, b, 0:1], [[0, 2], [0, 4], [1, 32]]), bc(ROPE[:, b, 32:33], [[0, 2], [0, 4], [1, 32]]),
                         TMP[:, 0:256].rearrange("p (a b d) -> p a b d", a=2, b=4),
                         TMP[:, 256:512].rearrange("p (a b d) -> p a b d", a=2, b=4),
                         bc(QF[:, 0, 0, 0:1], [[DH, 2], [2 * DH, 4], [1, 32]]),
                         bc(QF[:, 0, 0, 32:33], [[DH, 2], [2 * DH, 4], [1, 32]]), bQS, bQF)
                    rope(KS[:, :, 0:32], KS[:, :, 32:64],
                         bc(ROPE[:, b, 0:1], [[0, 2], [1, 32]]), bc(ROPE[:, b, 32:33], [[0, 2], [1, 32]]),
                         TMP[:, 0:64].rearrange("p (h d) -> p h d", h=2), TMP[:, 256:320].rearrange("p (h d) -> p h d", h=2),
                         KR[:, :, 0:32], KR[:, :, 32:64], bKS, bKR)
                    yield
                    P.op("act", lambda e: e.activation(out=KF, in_=KR, func=AF.Copy), r=[bKR], w=[bKF])
                    P.op("act", lambda e, PKV=PKV: e.activation(out=VE[:, par, :, 0:DH], in_=v2(PKV[:, 128:256]), func=AF.Copy),
                         r=[bPKV], w=[bVE])
                    if n == self.nblk - 1 and "outs" not in SKIP:
                        P.op("act", lambda e, PKV=PKV: e.activation(out=VR, in_=v2(PKV[:, 128:256]), func=AF.Copy),
                             r=[bPKV], w=[bVR])
                        P.dma("sp", self.o_k_p[l], KR.rearrange("p h d -> p (h d)"), r=[bKR], sem="st")
                        P.dma("sp", self.o_v_p[l], VR.rearrange("p h d -> p (h d)"), r=[bVR], sem="st")
                    yield
                    ptq, bptq = self.bank()
                    ptqv = ptq[:, 0:320].bitcast(BF16)
                    for g in range(4):
                        qin = QF[:, g, :, :].rearrange("p a d -> p (a d)")
                        P.op("pe", lambda e, g=g, qin=qin, ptqv=ptqv: e.transpose(ptqv[:, g * 128:(g + 1) * 128], qin, IDB),
                             r=[bQF, bCB], w=[bptq])
                    P.op("pe", lambda e, ptqv=ptqv: e.transpose(ptqv[:, 512:640], KF.rearrange("p h d -> p (h d)"), IDB),
                         r=[bKF, bCB], w=[bptq])
                    P.op("act", lambda e, ptqv=ptqv: e.activation(out=QTS.rearrange("p g t -> p (g t)"), in_=ptqv[:, 0:512],
                                                                  func=AF.Copy), r=[bptq], w=[bQTS])
                    P.op("dve", lambda e, ptqv=ptqv: e.tensor_copy(out=KTS[:, par, :], in_=ptqv[:, 512:640]),
                         r=[bptq], w=[bKTS])
                    yield
                    kbs = ((1, par, MASKC),) if n == 0 else ((0, 1 - par, MASKP), (1, par, MASKC))
                    pos_, bpos_ = [], []
                    for hk in range(SW_KV):
                        psl = slice(hk * DH, (hk + 1) * DH)
                        for (kb, kpar, msk) in kbs:
                            pscr, bpscr = self.bank()
                            P.op("pe", lambda e, pscr=pscr, psl=psl, kpar=kpar: e.matmul(
                                pscr[:, :], KTS[psl, kpar, :], QTS[psl, :, :].rearrange("p g t -> p (g t)"),
                                start=True, stop=False), r=[bKTS, bQTS], w=[bpscr])
                            P.op("pe", lambda e, pscr=pscr, msk=msk: e.matmul(pscr[:, :], IDB, msk, start=False, stop=True),
                                 r=[bCB], w=[bpscr])
                            P.op("act", lambda e, pscr=pscr, hk=hk, kb=kb: e.activation(
                                out=PT[hk, kb].rearrange("p g t -> p (g t)"), in_=pscr[:, :], func=AF.Exp),
                                r=[bpscr], w=[bPT[hk, kb]])
                        yield
                        posb, bposb = self.bank()
                        pov = posb[:, 0:4 * (DH + 1)].rearrange("p (g d) -> p g d", g=4)
                        for g in range(4):
                            for ii, (kb, kpar, msk) in enumerate(kbs):
                                P.op("pe", lambda e, pov=pov, g=g, hk=hk, kb=kb, kpar=kpar, ii=ii: e.matmul(
                                    pov[:, g, :], PT[hk, kb][:, g, :], VE[:, kpar, hk, :],
                                    start=(ii == 0), stop=(ii == len(kbs) - 1)),
                                    r=[bPT[hk, kb], bVE], w=[bposb])
                        P.op("dve", lambda e, pov=pov, hk=hk: e.tensor_tensor(
                            out=DEN8[:, hk * 4:(hk + 1) * 4], in0=pov[:, :, DH], in1=ESINK[:, hk * 4:(hk + 1) * 4], op=ALU.add),
                            r=[bposb, bSCc], w=[bDEN])
                        P.op("dve", lambda e, hk=hk: e.reciprocal(out=DEN8[:, 8 + hk * 4:8 + (hk + 1) * 4],
                                                                  in_=DEN8[:, hk * 4:(hk + 1) * 4]), r=[bDEN], w=[bDEN])
                        P.op("dve", lambda e, pov=pov, hk=hk: e.tensor_tensor(
                            out=OSB[:, hk * 4:(hk + 1) * 4, :], in0=pov[:, :, 0:DH],
                            in1=bc(DEN8[:, 8 + hk * 4:9 + hk * 4], [[1, 4], [0, DH]]), op=ALU.mult),
                            r=[bposb, bDEN], w=[bOSB])
                    yield
                    ptos, bptos = self.bank()
                    ptosv = ptos[:, 0:256].bitcast(BF16)
                    osf = OSB.rearrange("p h d -> p (h d)")
                    for c in range(4):
                        P.op("pe", lambda e, c=c, ptosv=ptosv: e.transpose(ptosv[:, c * 128:(c + 1) * 128],
                                                                           osf[:, c * 128:(c + 1) * 128], IDB),
                             r=[bOSB, bCB], w=[bptos])
                    P.op("dve", lambda e, ptosv=ptosv: e.tensor_copy(out=CAT[:, 4:8, tsl], in_=v3(ptosv)),
                         r=[bptos], w=[bCAT])
                    yield
                    if b < 3:
                        proj(b + 1)
                        if "dn" not in SKIP:
                            dn_pre(b + 1)

                gens = []
                if "dn" not in SKIP:
                    gens.append(dn_gen())
                if "swa" not in SKIP:
                    gens.append(swa_gen())
                if os.environ.get("KSEQ", "0") == "1":
                    for g_ in gens:
                        for _ in g_:
                            pass
                else:
                    DNR = int(os.environ.get("KDNR", "2"))
                    step = 0
                    while gens:
                        for gi, g_ in enumerate(list(gens)):
                            reps = DNR if (gi == 0 and len(gens) > 1) else 1
                            for _ in range(reps):
                                try:
                                    next(g_)
                                except StopIteration:
                                    if g_ in gens:
                                        gens.remove(g_)
                                    break
            for b in range(4):
                block(b)
            for d in range(KD):
                py, bpy = self.bank()
                for c in range(KD):
                    P.op("pe", lambda e, d=d, c=c, py=py: e.matmul(py[:, :], WOUT[:, c, d * 128:(d + 1) * 128], CAT[:, c, :],
                                                                   start=(c == 0), stop=(c == KD - 1)),
                         r=[bWO, bCAT], w=[bpy])
                self.resid(X0, bX0, TT, 1, d, py, bpy, False)
            P.dma("sp", src[:, :, i * TT:(i + 1) * TT], X0, r=[bX0], w=[self.bYT[i]], sem="xs")
        for i in range(self.ntile):
            tile(i)
        if "outs" not in SKIP:
            P.dma("sp", self.o_S_p[l].rearrange("h k v -> k h v"), S32, r=[bS32], sem="st")
        P.barrier()
        if "samp" not in SKIP:
            self.sample_mixer(l, WIN, bWI, WOUT, bWO, CF, bCF, CB, bCB, BCP, bBCP, SC, bSC, segs_sample)
        self.rr_set = list(range(8))


    def sample_mixer(self, l, WIN, bWI, WOUT, bWO, CF, bCF, CB, bCB, BCP, bBCP, SC, bSC, segs):
        P = self.P
        self.segs = [list(x) for x in segs]
        mk = self.mk
        self.rr_set = [4, 5, 6, 7]
        PB, bPB = self.PB, self.bPB
        HS, bHS, XS, bXS = self.HS, self.bHS, self.XS, self.bXS
        p16 = slice(0, NS)
        IDB = CB[:, 0:128]
        ONESF = CF[:, 128:256]
        IDF = CF[:, 512:640]
        SEL = CF[:, 640:896]
        ROPS = CF[:, 896:960]
        ZEROF = CF[:, 960:1088]
        DTB, SINK = BCP[:, 4:8], BCP[:, 8:16]
        NORMW4, QNW8, KNW2 = BCP[:, 16:528], BCP[:, 528:1040], BCP[:, 1040:1168]
        NEGA, ESINK = SC[:, 64:68], SC[:, 68:76]
        SELB, bSELB = mk("SELB", [256], BF16)
        CSTK, bCSTK = mk("CSTK", [CONV_CH])
        CSTT, bCSTT = mk("CSTT", [NCC, 48])
        PCT, bPCT = mk("PCT", [NCC, NS])
        ACC, bACC = mk("ACC", [NCC, NS])
        T12, bT12 = mk("T12", [NCC, NS])
        PCTOK, bPCTOK = mk("PCTOK", [CONV_CH])
        SQS, bSQS = mk("SQS", [8, NS], BF16)
        RN, bRN = mk("RN", [8, NS])
        QKN, bQKN = mk("QKN", [8, NS])
        KM, bKM = mk("KM", [DN_H, NS, NS])
        QM, bQM = mk("QM", [DN_H, NS, NS])
        VTOK, bVTOK = mk("VTOK", [DN_H, DV])
        KTOK, bKTOK = mk("KTOK", [DN_H, DK])
        VNS, bVNS = mk("VNS", [DN_H, DV])
        T1, bT1 = mk("T1", [DN_H, DV])
        KMS = [mk("KMS%d" % i, [DN_H, DK]) for i in range(2)]
        S0 = [mk("S0%d" % i, [DN_H, DV]) for i in range(2)]
        SN = [mk("SN%d" % i, [DN_H, DV]) for i in range(2)]
        AD, bAD = mk("AD", [NS, DN_H])
        AB, bAB = mk("AB", [NS * DN_H])
        SCS_, bSCS_ = mk("SCS", [96])
        ZGS, bZGS = mk("ZGS", [DN_H, DV])
        ONS, bONS = mk("ONS", [DN_H, DV])
        ON2S, bON2S = mk("ON2S", [DN_H, DV], BF16)
        QSS, bQSS = mk("QSS", [SW_H, DH])
        KSS, bKSS = mk("KSS", [SW_KV, DH])
        KRS, bKRS = mk("KRS", [SW_KV, DH])
        VRS, bVRS = mk("VRS", [SW_KV, DH])
        QRS, bQRS = mk("QRS", [SW_H, DH])
        QFS, bQFS = mk("QFS", [4, 2, DH], BF16)
        QTSS, bQTSS = mk("QTSS", [4, NS], BF16)
        KC, bKC = mk("KC", [NS, 128], BF16)
        VCE, bVCE = mk("VCE", [NS, SW_KV, DH + 1], BF16)
        KCT = [mk("KCT%d" % i, [128], BF16) for i in range(2)]
        PTS, bPTS = mk("PTS", [NS, SW_H], BF16)
        PTM = [mk("PTM%d" % i, [SW_H, NS], BF16) for i in range(2)]
        PROD, bPROD = mk("PROD", [SW_H, DH])
        OSS, bOSS = mk("OSS", [SW_H, DH])
        OSBS, bOSBS = mk("OSBS", [SW_H, DH], BF16)
        CATS, bCATS = mk("CATS", [KD, NS], BF16)
        TMP, bTMP = self.TMP, self.bTMP
        sc = lambda i, n=4: SCS_[:, i * 4:i * 4 + n]
        BETA, XA_, AXV, E1, L1, G4, AEX, LNB, RS4, SSQ4 = [sc(i) for i in range(10)]
        SSQ10, RS10, SN8, PN8, DEN8, RDEN8 = (SCS_[:, 40:50], SCS_[:, 50:60], SCS_[:, 60:68], SCS_[:, 68:76],
                                              SCS_[:, 76:84], SCS_[:, 84:92])
        fl = lambda v: v.rearrange("p h i -> p (h i)")
        v3 = lambda v: v.rearrange("p (h i) -> p h i", h=DN_H)
        v8 = lambda v: v.rearrange("p (h d) -> p h d", h=SW_H)
        v2 = lambda v: v.rearrange("p (h d) -> p h d", h=SW_KV)
        hb = lambda v: bc(v, [[1, DN_H], [0, 128]])

        P.dma("sp", CSTK[0:48, :], self.st_conv[l].rearrange("s j c -> (s j) c"), w=[bCSTK])
        for q4 in range(0, NS, 4):
            P.dma("pool", KC[:, q4:q4 + 4, :], self.st_k[l][q4:q4 + 4].rearrange("s k c -> k s c"), w=[bKC])
        P.op("dve", lambda e: e.memset(VCE[:, :, :, DH:DH + 1], 1.0), w=[bVCE])
        for q4 in range(0, NS, 4):
            for hk in range(SW_KV):
                P.dma("pool", VCE[:, q4:q4 + 4, hk, 0:DH],
                      self.st_v[l][q4:q4 + 4, :, hk * DH:(hk + 1) * DH].rearrange("s k d -> k s d"), w=[bVCE])
        P.op("act", lambda e: e.activation(out=SELB, in_=SEL, func=AF.Copy), r=[bCF], w=[bSELB])
        P.dma("sp", self.o_k_s[l][:, 0:127, :], self.st_k[l][:, 1:128, :])
        P.dma("sp", self.o_v_s[l][:, 0:127, :], self.st_v[l][:, 1:128, :])
        P.dma("sp", self.o_conv_s[l][:, 0:2, :], self.st_conv[l][:, 1:3, :])

        self.norm_mod(XS[:], bXS, NS, 1, HS[:], bHS, True, self.GS[:, 0:KD, :], self.bGS)

        pc, bpc = PB[0], bPB[0]
        pcv = pc[:, 0:NCC * NS].rearrange("p (c s) -> p c s", c=NCC)
        for c in range(NCC):
            for k in range(KD):
                P.op("pe", lambda e, c=c, k=k: e.matmul(pcv[:, c, :], WIN[:, k, c * 128:(c + 1) * 128], HS[:, k, :],
                                                        start=(k == 0), stop=(k == KD - 1)), r=[bWI, bHS], w=[bpc])
        P.op("act", lambda e: e.activation(out=PCT, in_=pcv, func=AF.Copy), r=[bpc], w=[bPCT])
        for half in range(2):
            pt, bpt = self.bank()
            for cc in range(6):
                c = half * 6 + cc
                P.op("pe", lambda e, c=c, cc=cc, pt=pt: e.transpose(pt[:, cc * 48:(cc + 1) * 48],
                                                                    CSTK[0:48, c * 128:(c + 1) * 128], IDF[0:48, 0:48]),
                     r=[bCSTK, bCF], w=[bpt])
            P.op("dve", lambda e, half=half, pt=pt: e.tensor_copy(out=fl(CSTT[:, half * 6:(half + 1) * 6, :]),
                                                                  in_=pt[:, 0:6 * 48]), r=[bpt], w=[bCSTT])
        cwv = lambda j: bc(self.SMALL[:, 96 + j:97 + j], [[4, NCC], [0, NS]])
        stv = lambda j: bc(CSTT[:, 0, j:j + 1], [[48, NCC], [3, NS]])
        P.op("dve", lambda e: e.tensor_tensor(out=ACC, in0=PCT, in1=cwv(3), op=ALU.mult), r=[bPCT, self.bSMALL], w=[bACC])
        for j in range(3):
            P.op("dve", lambda e, j=j: e.tensor_tensor(out=T12, in0=stv(j), in1=cwv(j), op=ALU.mult),
                 r=[bCSTT, self.bSMALL], w=[bT12])
            P.op("dve", lambda e: e.tensor_tensor(out=ACC, in0=ACC, in1=T12, op=ALU.add), r=[bT12, bACC], w=[bACC])
        P.op("act", lambda e: e.activation(out=ACC, in_=ACC, func=AF.Silu), r=[bACC], w=[bACC])
        P.op("act", lambda e: e.activation(out=SQS, in_=ACC[:, 0:8, :], func=AF.Square), r=[bACC], w=[bSQS])
        pn, bpn = self.bank()
        P.op("pe", lambda e: e.matmul(pn[:, 0:8 * NS], self.ONES[:], SQS.rearrange("p c s -> p (c s)"),
                                      start=True, stop=True), r=[self.bONES, bSQS], w=[bpn])
        P.op("act", lambda e: e.activation(out=fl(RN)[:, 0:4 * NS], in_=pn[:, 0:4 * NS], func=AF.Ln, scale=float(DK),
                                           bias=self.EPSD[:, 1:2]), r=[bpn, self.bONES], w=[bRN])
        P.op("act", lambda e: e.activation(out=fl(RN)[:, 4 * NS:8 * NS], in_=pn[:, 4 * NS:8 * NS], func=AF.Ln,
                                           bias=self.EPSD[:, 4:5]), r=[bpn, self.bONES], w=[bRN])
        P.op("act", lambda e: e.activation(out=RN, in_=RN, func=AF.Exp, scale=-0.5), r=[bRN], w=[bRN])
        P.op("dve", lambda e: e.tensor_tensor(out=QKN, in0=ACC[:, 0:8, :], in1=RN, op=ALU.mult), r=[bACC, bRN], w=[bQKN])
        selb = bc(SEL[:, 0:1], [[0, DN_H], [NS, NS], [1, NS]])
        P.op("dve", lambda e: e.tensor_tensor(out=KM, in0=bc(QKN[:, 4, 0:1], [[NS, DN_H], [1, NS], [0, NS]]), in1=selb,
                                              op=ALU.mult), r=[bQKN, bCF], w=[bKM])
        P.op("dve", lambda e: e.tensor_tensor(out=QM, in0=bc(QKN[:, 0, 0:1], [[NS, DN_H], [1, NS], [0, NS]]), in1=selb,
                                              op=ALU.mult), r=[bQKN, bCF], w=[bQM])
        ptv, bptv = self.bank()
        ptk, bptk = self.bank()
        for h in range(DN_H):
            P.op("pe", lambda e, h=h: e.transpose(ptv[p16, h * 128:(h + 1) * 128], ACC[:, 8 + h, :], IDF),
                 r=[bACC, bCF], w=[bptv])
            P.op("pe", lambda e, h=h: e.transpose(ptk[p16, h * 128:(h + 1) * 128], QKN[:, 4 + h, :], IDF),
                 r=[bQKN, bCF], w=[bptk])
        P.op("act", lambda e: e.activation(out=fl(VTOK)[p16], in_=ptv[p16, :], func=AF.Copy), r=[bptv], w=[bVTOK])
        P.op("dve", lambda e: e.tensor_copy(out=fl(KTOK)[p16], in_=ptk[p16, :]), r=[bptk], w=[bKTOK])

        def tokproj(pt_, bpt_, c0, ncol, o0):
            for k in range(KD):
                P.op("pe", lambda e, k=k: e.matmul(pt_[p16, o0:o0 + ncol], HS[:, k, :], WIN[:, k, c0:c0 + ncol],
                                                   start=(k == 0), stop=(k == KD - 1)), r=[bWI, bHS], w=[bpt_])
        for cg in range(3):
            pp, bpp = self.bank()
            tokproj(pp, bpp, cg * 512, 512, 0)
            P.op("act", lambda e, pp=pp, cg=cg: e.activation(out=PCTOK[p16, cg * 512:(cg + 1) * 512], in_=pp[p16, :],
                                                             func=AF.Copy), r=[bpp], w=[bPCTOK])
        P.dma("sp", self.o_conv_s[l][:, 2, :], PCTOK[p16, :], r=[bPCTOK])
        PZ, bPZ = self.bank()
        tokproj(PZ, bPZ, C_Z, 512, 0)
        P.op("act", lambda e: e.activation(out=fl(ONS)[p16], in_=PZ[p16, :], func=AF.Silu), r=[bPZ], w=[bONS])
        P.op("dve", lambda e: e.tensor_tensor(out=fl(ZGS)[p16], in0=fl(ONS)[p16], in1=NORMW4[p16], op=ALU.mult),
             r=[bONS, bBCP], w=[bZGS])
        PQ, bPQ = PB[2], bPB[2]
        PKV, bPKV = PB[3], bPB[3]
        tokproj(PQ, bPQ, C_Q, 512, 0)
        tokproj(PKV, bPKV, C_K, 256, 0)
        tokproj(PKV, bPKV, C_BA, 8, 256)
        P.op("act", lambda e: e.activation(out=BETA[p16], in_=PKV[p16, 256:260], func=AF.Exp, scale=-1.0),
             r=[bPKV], w=[bSCS_])
        P.op("dve", lambda e: e.tensor_scalar(out=BETA[p16], in0=BETA[p16], scalar1=1.0, scalar2=None, op0=ALU.add),
             r=[bSCS_], w=[bSCS_])
        P.op("dve", lambda e: e.reciprocal(out=BETA[p16], in_=BETA[p16]), r=[bSCS_], w=[bSCS_])
        P.op("dve", lambda e: e.tensor_tensor(out=XA_[p16], in0=PKV[p16, 260:264], in1=DTB[p16], op=ALU.add),
             r=[bPKV, bBCP], w=[bSCS_])
        P.op("act", lambda e: e.activation(out=AXV[p16], in_=XA_[p16], func=AF.Abs), r=[bSCS_], w=[bSCS_])
        P.op("act", lambda e: e.activation(out=E1[p16], in_=AXV[p16], func=AF.Exp, scale=-1.0), r=[bSCS_], w=[bSCS_])
        P.op("act", lambda e: e.activation(out=L1[p16], in_=E1[p16], func=AF.Ln, bias=self.EPSD[p16, 3:4]),
             r=[bSCS_, self.bONES], w=[bSCS_])
        P.op("dve", lambda e: e.scalar_tensor_tensor(out=G4[p16], in0=XA_[p16], scalar=0.0, in1=L1[p16], op0=ALU.max,
                                                     op1=ALU.add), r=[bSCS_], w=[bSCS_])
        P.op("dve", lambda e: e.tensor_tensor(out=G4[p16], in0=G4[p16], in1=NEGA[p16], op=ALU.mult), r=[bSCS_, bSC], w=[bSCS_])
        P.op("act", lambda e: e.activation(out=AEX[p16], in_=G4[p16], func=AF.Exp), r=[bSCS_], w=[bSCS_])
        P.op("dve", lambda e: e.tensor_tensor(out=AD[p16], in0=bc(AEX[p16], [[0, NS], [1, DN_H]]),
                                              in1=bc(IDF[p16, 0:1], [[1, NS], [0, DN_H]]), op=ALU.mult),
             r=[bSCS_, bCF], w=[bAD])
        pab, bpab = self.bank()
        P.op("pe", lambda e: e.matmul(pab[:, 0:NS * DN_H], ONESF[p16, :], AD[p16].rearrange("p s h -> p (s h)"),
                                      start=True, stop=True), r=[bCF, bAD], w=[bpab])
        P.op("dve", lambda e: e.tensor_copy(out=AB, in_=pab[:, 0:NS * DN_H]), r=[bpab], w=[bAB])

        P.op("act", lambda e: e.activation(out=TMP[p16], in_=PQ[p16, :], func=AF.Square), r=[bPQ], w=[bTMP])
        P.op("dve", lambda e: e.tensor_reduce(out=SSQ10[p16, 0:8], in_=v8(TMP[p16]), axis=AX.X, op=ALU.add),
             r=[bTMP], w=[bSCS_])
        P.op("act", lambda e: e.activation(out=self.R[p16, 0:128], in_=PKV[p16, 0:128], func=AF.Square),
             r=[bPKV], w=[self.bR])
        P.op("dve", lambda e: e.tensor_reduce(out=SSQ10[p16, 8:10], in_=v2(self.R[p16, 0:128]), axis=AX.X, op=ALU.add),
             r=[self.bR], w=[bSCS_])
        P.op("act", lambda e: e.activation(out=RS10[p16], in_=SSQ10[p16], func=AF.Ln, bias=self.EPSD[p16, 2:3]),
             r=[bSCS_, self.bONES], w=[bSCS_])
        P.op("act", lambda e: e.activation(out=RS10[p16], in_=RS10[p16], func=AF.Exp, scale=-0.5), r=[bSCS_], w=[bSCS_])
        P.op("dve", lambda e: e.tensor_tensor(out=QSS[p16], in0=v8(PQ[p16, :]), in1=bc(RS10[p16, 0:8], [[1, 8], [0, DH]]),
                                              op=ALU.mult), r=[bPQ, bSCS_], w=[bQSS])
        P.op("dve", lambda e: e.tensor_tensor(out=QSS[p16], in0=QSS[p16], in1=v8(QNW8[p16]), op=ALU.mult),
             r=[bQSS, bBCP], w=[bQSS])
        P.op("dve", lambda e: e.tensor_tensor(out=KSS[p16], in0=v2(PKV[p16, 0:128]),
                                              in1=bc(RS10[p16, 8:10], [[1, 2], [0, DH]]), op=ALU.mult),
             r=[bPKV, bSCS_], w=[bKSS])
        P.op("dve", lambda e: e.tensor_tensor(out=KSS[p16], in0=KSS[p16], in1=v2(KNW2[p16]), op=ALU.mult),
             r=[bKSS, bBCP], w=[bKSS])
        P.op("act", lambda e: e.activation(out=VRS[p16], in_=v2(PKV[p16, 128:256]), func=AF.Copy), r=[bPKV], w=[bVRS])

        def rope(x1, x2, nh, d1, d2, bsrc, bdst_):
            cosb = bc(ROPS[p16, 0:1], [[0, nh], [1, 32]])
            sinb = bc(ROPS[p16, 32:33], [[0, nh], [1, 32]])
            t1 = TMP[p16, 0:nh * 32].rearrange("p (h d) -> p h d", h=nh)
            t2 = TMP[p16, 256:256 + nh * 32].rearrange("p (h d) -> p h d", h=nh)
            for (xa, xb, dd, op_) in ((x1, x2, d1, ALU.subtract), (x2, x1, d2, ALU.add)):
                P.op("dve", lambda e, xa=xa: e.tensor_tensor(out=t1, in0=xa, in1=cosb, op=ALU.mult),
                     r=[bsrc, bCF, bTMP], w=[bTMP])
                P.op("dve", lambda e, xb=xb: e.tensor_tensor(out=t2, in0=xb, in1=sinb, op=ALU.mult),
                     r=[bsrc, bCF, bTMP], w=[bTMP])
                P.op("dve", lambda e, dd=dd, op_=op_: e.tensor_tensor(out=dd, in0=t1, in1=t2, op=op_),
                     r=[bTMP], w=[bdst_])
        rope(QSS[p16, :, 0:32], QSS[p16, :, 32:64], SW_H, QRS[p16, :, 0:32], QRS[p16, :, 32:64], bQSS, bQRS)
        rope(KSS[p16, :, 0:32], KSS[p16, :, 32:64], SW_KV, KRS[p16, :, 0:32], KRS[p16, :, 32:64], bKSS, bKRS)
        P.dma("sp", self.o_k_s[l][:, 127, :], KRS[p16].rearrange("p h d -> p (h d)"), r=[bKRS])
        P.dma("sp", self.o_v_s[l][:, 127, :], VRS[p16].rearrange("p h d -> p (h d)"), r=[bVRS])
        P.op("act", lambda e: e.activation(out=bc(QFS[p16, 0, 0, 0:1], [[DH, 2], [2 * DH, 4], [1, DH]]),
                                           in_=QRS[p16].rearrange("p (a b) d -> p a b d", a=2), func=AF.Copy),
             r=[bQRS], w=[bQFS])
        ptq, bptq = self.bank()
        ptqv = ptq[:, 0:32].bitcast(BF16)
        for g in range(4):
            P.op("pe", lambda e, g=g: e.transpose(ptqv[:, g * NS:(g + 1) * NS],
                                                  QFS[p16, g, :, :].rearrange("p a d -> p (a d)"), IDB[p16, p16]),
                 r=[bQFS, bCB], w=[bptq])
        P.op("act", lambda e: e.activation(out=QTSS.rearrange("p g s -> p (g s)"), in_=ptqv[:, 0:4 * NS], func=AF.Copy),
             r=[bptq], w=[bQTSS])
        P.op("dve", lambda e: e.tensor_tensor(out=PROD[p16].rearrange("p (a b) d -> p a b d", a=2),
                                              in0=QRS[p16].rearrange("p (a b) d -> p a b d", a=2),
                                              in1=bc(KRS[p16, 0, 0:1], [[DH, 2], [0, 4], [1, DH]]), op=ALU.mult),
             r=[bQRS, bKRS], w=[bPROD])
        P.op("dve", lambda e: e.tensor_reduce(out=SN8[p16], in_=PROD[p16], axis=AX.X, op=ALU.add), r=[bPROD], w=[bSCS_])
        P.op("act", lambda e: e.activation(out=PN8[p16], in_=SN8[p16], func=AF.Exp), r=[bSCS_], w=[bSCS_])

        pks, bpks = PB[0], bPB[0]
        pos_, bpos_ = PB[1], bPB[1]
        for (pz_, bpz_) in ((pks, bpks), (pos_, bpos_)):
            P.op("pe", lambda e, pz_=pz_: e.matmul(pz_[p16, :], ZEROF[:, 0:NS], CF[:, 0:512], start=True, stop=False,
                                                   skip_group_check=True), r=[bCF], w=[bpz_])
        stS = self.st_S[l]
        for s_ in range(NS):
            S0b, bS0b = S0[s_ % 2]
            P.dma("sp", S0b, stS[s_].rearrange("h k v -> k h v"), w=[bS0b])
            for h in range(DN_H):
                P.op("pe", lambda e, h=h, s_=s_, S0b=S0b: e.matmul(
                    pks[p16, h * 128:(h + 1) * 128], KM[:, h, s_, :], S0b[:, h, :], start=False, stop=(s_ == NS - 1),
                    skip_group_check=True), r=[bKM, bS0b], w=[bpks])
        P.op("dve", lambda e: e.tensor_tensor(out=T1[p16], in0=v3(pks[p16, :]), in1=hb(AEX[p16]), op=ALU.mult),
             r=[bpks, bSCS_], w=[bT1])
        P.op("dve", lambda e: e.tensor_tensor(out=T1[p16], in0=VTOK[p16], in1=T1[p16], op=ALU.subtract),
             r=[bVTOK, bT1], w=[bT1])
        P.op("dve", lambda e: e.tensor_tensor(out=VNS[p16], in0=T1[p16], in1=hb(BETA[p16]), op=ALU.mult),
             r=[bT1, bSCS_], w=[bVNS])
        for s_ in range(NS):
            S0b, bS0b = S0[s_ % 2]
            SNb, bSNb = SN[s_ % 2]
            KMSb, bKMSb = KMS[s_ % 2]
            P.dma("sp", S0b, stS[s_].rearrange("h k v -> k h v"), w=[bS0b])
            P.op("dve", lambda e, s_=s_, KMSb=KMSb: e.tensor_scalar(out=fl(KMSb)[p16], in0=fl(KTOK)[p16],
                                                                    scalar1=IDF[p16, s_:s_ + 1], scalar2=None, op0=ALU.mult),
                 r=[bKTOK, bCF], w=[bKMSb])
            pu, bpu = self.bank()
            for h in range(DN_H):
                P.op("pe", lambda e, h=h, pu=pu, KMSb=KMSb: e.matmul(pu[:, h * 128:(h + 1) * 128], KMSb[p16, h, :],
                                                                    VNS[p16, h, :], start=True, stop=True),
                     r=[bKMSb, bVNS], w=[bpu])
            P.op("dve", lambda e, s_=s_, S0b=S0b, SNb=SNb: e.tensor_tensor(
                out=SNb, in0=S0b, in1=bc(AB[:, s_ * DN_H:s_ * DN_H + 1], [[1, DN_H], [0, DV]]), op=ALU.mult),
                r=[bS0b, bAB], w=[bSNb])
            P.op("dve", lambda e, pu=pu, SNb=SNb: e.tensor_tensor(out=fl(SNb), in0=fl(SNb), in1=pu[:, :], op=ALU.add),
                 r=[bSNb, bpu], w=[bSNb])
            P.dma("sp", self.o_S_s[l][s_].rearrange("h k v -> k h v"), SNb, r=[bSNb])
            for h in range(DN_H):
                P.op("pe", lambda e, h=h, s_=s_, SNb=SNb: e.matmul(
                    pos_[p16, h * 128:(h + 1) * 128], QM[:, h, s_, :], SNb[:, h, :], start=False, stop=(s_ == NS - 1),
                    skip_group_check=True), r=[bQM, bSNb], w=[bpos_])
        P.op("act", lambda e: e.activation(out=fl(T1)[p16], in_=pos_[p16, :], func=AF.Square), r=[bpos_], w=[bT1])
        P.op("dve", lambda e: e.tensor_reduce(out=SSQ4[p16], in_=T1[p16], axis=AX.X, op=ALU.add), r=[bT1], w=[bSCS_])
        P.op("act", lambda e: e.activation(out=RS4[p16], in_=SSQ4[p16], func=AF.Ln, scale=1.0 / DV,
                                           bias=self.EPSD[p16, 4:5]), r=[bSCS_, self.bONES], w=[bSCS_])
        P.op("act", lambda e: e.activation(out=RS4[p16], in_=RS4[p16], func=AF.Exp, scale=-0.5), r=[bSCS_], w=[bSCS_])
        P.op("dve", lambda e: e.tensor_tensor(out=ONS[p16], in0=v3(pos_[p16, :]), in1=hb(RS4[p16]), op=ALU.mult),
             r=[bpos_, bSCS_], w=[bONS])
        P.op("dve", lambda e: e.tensor_tensor(out=ON2S[p16], in0=ONS[p16], in1=ZGS[p16], op=ALU.mult),
             r=[bONS, bZGS], w=[bON2S])
        pto, bpto = self.bank()
        ptov = pto[:, 0:32].bitcast(BF16)
        for h in range(DN_H):
            P.op("pe", lambda e, h=h: e.transpose(ptov[:, h * NS:(h + 1) * NS], ON2S[p16, h, :], IDB[p16, p16]),
                 r=[bON2S, bCB], w=[bpto])
        P.op("act", lambda e: e.activation(out=CATS[:, 0:4, :], in_=ptov[:, 0:4 * NS].rearrange("p (h s) -> p h s", h=4),
                                           func=AF.Copy), r=[bpto], w=[bCATS])

        psc, bpsc = PB[2], bPB[2]
        for s_ in range(NS):
            KCTb, bKCTb = KCT[s_ % 2]
            ptc, bptc = self.bank()
            ptcv = ptc[:, 0:64].bitcast(BF16)
            P.op("pe", lambda e, s_=s_, ptcv=ptcv: e.transpose(ptcv, KC[:, s_, :], IDB), r=[bKC, bCB], w=[bptc])
            P.op("act", lambda e, ptcv=ptcv, KCTb=KCTb: e.activation(out=KCTb, in_=ptcv, func=AF.Copy),
                 r=[bptc], w=[bKCTb])
            for hk in range(SW_KV):
                psl = slice(hk * DH, (hk + 1) * DH)
                P.op("pe", lambda e, s_=s_, hk=hk, psl=psl, KCTb=KCTb: e.matmul(
                    psc[:, s_ * 8 + hk * 4:s_ * 8 + hk * 4 + 4], KCTb[psl, :], QTSS[psl, :, s_], start=True, stop=True),
                    r=[bKCTb, bQTSS], w=[bpsc])
        P.op("act", lambda e: e.activation(out=PTS.rearrange("p s h -> p (s h)"), in_=psc[:, 0:NS * SW_H], func=AF.Exp),
             r=[bpsc], w=[bPTS])
        pov_ = [(PB[0], bPB[0]), (PB[1], bPB[1])]
        for (pz_, bpz_) in pov_:
            P.op("pe", lambda e, pz_=pz_: e.matmul(pz_[p16, :], ZEROF[:, 0:NS], CF[:, 0:512], start=True, stop=False,
                                                   skip_group_check=True), r=[bCF], w=[bpz_])
        for s_ in range(NS):
            PTMb, bPTMb = PTM[s_ % 2]
            P.op("dve", lambda e, s_=s_, PTMb=PTMb: e.tensor_tensor(
                out=PTMb, in0=bc(PTS[:, s_, 0:1], [[1, SW_H], [0, NS]]),
                in1=bc(SELB[:, s_ * NS:s_ * NS + 1], [[0, SW_H], [1, NS]]), op=ALU.mult),
                r=[bPTS, bSELB], w=[bPTMb])
            for hk in range(SW_KV):
                po_, bpo_ = pov_[hk]
                for g in range(4):
                    P.op("pe", lambda e, s_=s_, hk=hk, g=g, po_=po_, PTMb=PTMb: e.matmul(
                        po_[p16, g * (DH + 1):(g + 1) * (DH + 1)], PTMb[:, hk * 4 + g, :], VCE[:, s_, hk, :],
                        start=False, stop=(s_ == NS - 1), skip_group_check=True), r=[bPTMb, bVCE], w=[bpo_])
        for hk in range(SW_KV):
            po_, bpo_ = pov_[hk]
            pov = po_[p16, 0:4 * (DH + 1)].rearrange("p (g d) -> p g d", g=4)
            hs_ = slice(hk * 4, (hk + 1) * 4)
            P.op("dve", lambda e, hk=hk, hs_=hs_: e.tensor_tensor(
                out=PROD[p16, hs_, :], in0=bc(VRS[p16, hk, 0:1], [[0, 4], [1, DH]]),
                in1=bc(PN8[p16, hk * 4:hk * 4 + 1], [[1, 4], [0, DH]]), op=ALU.mult), r=[bVRS, bSCS_], w=[bPROD])
            P.op("dve", lambda e, pov=pov, hs_=hs_: e.tensor_tensor(out=OSS[p16, hs_, :], in0=pov[:, :, 0:DH],
                                                                   in1=PROD[p16, hs_, :], op=ALU.add),
                 r=[bpo_, bPROD], w=[bOSS])
            P.op("dve", lambda e, pov=pov, hs_=hs_: e.tensor_tensor(out=DEN8[p16, hs_], in0=pov[:, :, DH],
                                                                   in1=PN8[p16, hs_], op=ALU.add),
                 r=[bpo_, bSCS_], w=[bSCS_])
        P.op("dve", lambda e: e.tensor_tensor(out=DEN8[p16], in0=DEN8[p16], in1=ESINK[p16], op=ALU.add),
             r=[bSCS_, bSC], w=[bSCS_])
        P.op("dve", lambda e: e.reciprocal(out=RDEN8[p16], in_=DEN8[p16]), r=[bSCS_], w=[bSCS_])
        P.op("dve", lambda e: e.tensor_tensor(out=OSBS[p16], in0=OSS[p16], in1=bc(RDEN8[p16, 0:1], [[1, 8], [0, DH]]),
                                              op=ALU.mult), r=[bOSS, bSCS_], w=[bOSBS])
        ptos, bptos = self.bank()
        ptosv = ptos[:, 0:32].bitcast(BF16)
        osf = OSBS.rearrange("p h d -> p (h d)")
        for c in range(4):
            P.op("pe", lambda e, c=c: e.transpose(ptosv[:, c * NS:(c + 1) * NS], osf[p16, c * 128:(c + 1) * 128],
                                                  IDB[p16, p16]), r=[bOSBS, bCB], w=[bptos])
        P.op("act", lambda e: e.activation(out=CATS[:, 4:8, :], in_=ptosv[:, 0:4 * NS].rearrange("p (h s) -> p h s", h=4),
                                           func=AF.Copy), r=[bptos], w=[bCATS])
        for d in range(KD):
            py, bpy = self.bank()
            for c in range(KD):
                P.op("pe", lambda e, d=d, c=c, py=py: e.matmul(py[:, 0:NS], WOUT[:, c, d * 128:(d + 1) * 128], CATS[:, c, :],
                                                               start=(c == 0), stop=(c == KD - 1)),
                     r=[bWO, bCATS], w=[bpy])
            self.resid(XS[:], bXS, NS, 1, d, py, bpy, True)


def make_constf():
    c = np.zeros((128, NCF), np.float32)
    j = np.arange(128)[:, None]
    i = np.arange(128)[None, :]
    c[:, 0:128] = (j <= i)
    c[:, 128:256] = 1.0
    c[:, 256:384] = np.where(i >= j, 0.0, NEG)
    c[:, 384:512] = np.where(i < j, 0.0, NEG)
    c[:, 512:640] = np.eye(128)
    c[:, 640:896] = np.eye(NS, dtype=np.float32).reshape(1, NS * NS)
    half = DH // 2
    inv = np.power(np.float32(10000.0), -np.arange(half, dtype=np.float32) * np.float32(2.0) / np.float32(DH)).astype(np.float32)
    ang = (np.float32(PAST_LEN) * inv).astype(np.float32)
    c[:, 896:928] = np.cos(ang)[None, :]
    c[:, 928:960] = np.sin(ang)[None, :]
    return c.astype(np.float32)


def make_constb():
    c = np.zeros((128, NCB), np.float32)
    j = np.arange(128)[:, None]
    i = np.arange(128)[None, :]
    c[:, 0:128] = np.eye(128)
    c[:, 128:640] = np.tile(np.where(j <= i, 0.0, NEG), (1, 4))
    c[:, 640:1152] = np.tile(np.where(j >= i, 0.0, NEG), (1, 4))
    c[:, 1152:1280] = (j // 8 == i // 8)
    for li, m in enumerate((8, 16, 32, 64)):
        ml = (j // (2 * m) == i // (2 * m)) & ((j // m) % 2 == 1) & ((i // m) % 2 == 0)
        c[:, 1280 + li * 128:1408 + li * 128] = ml
        c[:, 1792 + li * 128:1920 + li * 128] = ml.T
    return c.astype(np.float32)


def make_rope(seq):
    half = DH // 2
    inv = np.power(np.float32(10000.0), -np.arange(half, dtype=np.float32) * np.float32(2.0) / np.float32(DH)).astype(np.float32)
    pos = np.arange(seq, dtype=np.float32)
    ang = (pos[:, None] * inv[None, :]).astype(np.float32)
    t = np.concatenate([np.cos(ang), np.sin(ang)], -1).astype(np.float32)
    return np.ascontiguousarray(t.reshape(seq // 128, 128, 64).transpose(1, 0, 2))


def make_bcp(inp, L):
    b = np.zeros((L, 128, NBC), np.float32)
    for l in range(L):
        row = np.concatenate([
            np.asarray(inp["dn_A_log"][l]), np.asarray(inp["dn_dt_bias"][l]), np.asarray(inp["swa_sinks"][l]),
            np.tile(np.asarray(inp["dn_norm_w"][l]), 4), np.tile(np.asarray(inp["swa_q_norm"][l]), 8),
            np.tile(np.asarray(inp["swa_k_norm"][l]), 2)]).astype(np.float32)
        b[l] = row[None, :]
    return b


def make_smallp(inp, L):
    sp = np.zeros((L, 128, 144), np.float32)
    for l in range(L):
        sp[l, :, 0:72] = np.asarray(inp["b_ada"][l]).reshape(72, 128).T
        for s, nm in enumerate(("g_ffn1", "g_mix", "g_ffn2")):
            sp[l, :, 72 + 8 * s:80 + 8 * s] = np.asarray(inp[nm][l]).reshape(8, 128).T
        sp[l, :, 96:144] = np.asarray(inp["dn_conv_w"][l]).T.reshape(12, 128, 4).transpose(1, 0, 2).reshape(128, 48)
    return sp


_CACHE = {}


def get_nc(seq, depth, stop):
    key = (seq, depth, stop)
    if key not in _CACHE:
        b = Builder(seq, depth, stop)
        nc = b.build()
        _CACHE[key] = (nc, b)
    return _CACHE[key]


def make_in_maps(inputs, seq, depth):
    f = lambda a: np.ascontiguousarray(np.asarray(a, dtype=np.float32))
    shared = {
        "w_ada": f(inputs["w_ada"][:depth]),
        "smallp": make_smallp(inputs, depth),
        "w_ffn1_gu": f(inputs["w_ffn1_gu"][:depth]), "w_ffn2_gu": f(inputs["w_ffn2_gu"][:depth]),
        "w_ffn1_down": f(inputs["w_ffn1_down"][:depth]), "w_ffn2_down": f(inputs["w_ffn2_down"][:depth]),
        "w_in": f(inputs["w_in"][:depth]), "w_out": f(inputs["w_out"][:depth]),
        "constf": make_constf(), "constb": make_constb(), "ropeT": make_rope(seq),
        "bcp": make_bcp(inputs, depth),
    }
    xp = np.asarray(inputs["x_prompt"], np.float32)
    xs = np.asarray(inputs["x_sample"], np.float32)
    cp = np.asarray(inputs["c_prompt"], np.float32)
    cs = np.asarray(inputs["c_sample"], np.float32)
    in_maps = []
    for c in range(NCORES):
        m = dict(shared)
        m["xT"] = np.ascontiguousarray(xp[c, :seq].T)
        m["xsT"] = np.ascontiguousarray(xs[c * NS:(c + 1) * NS, 0].T)
        m["csT"] = np.ascontiguousarray(np.concatenate([cp[c:c + 1], cs[c * NS:(c + 1) * NS]], 0).T)
        sl = slice(c * NS, (c + 1) * NS)
        m["st_conv"] = f(inputs["state_dn_conv"][:depth, sl])
        m["st_S"] = f(inputs["state_dn_S"][:depth, sl])
        m["st_k"] = f(inputs["cache_swa_k"][:depth, sl]).reshape(depth, NS, 128, SW_KV * DH)
        m["st_v"] = f(inputs["cache_swa_v"][:depth, sl]).reshape(depth, NS, 128, SW_KV * DH)
        in_maps.append(m)
    return in_maps


def run(inputs, seq=SEQ, depth=DEPTH, stop=None, trace=False):
    nc, b = get_nc(seq, depth, stop)
    in_maps = make_in_maps(inputs, seq, depth)
    res = run_bass_kernel_spmd(nc, in_maps, core_ids=list(range(NCORES)), trace=trace)
    return res, b


def assemble(r, depth):
    g = lambda c, n: np.asarray(r[c][n])
    yp = np.stack([g(c, "yT").T for c in range(NCORES)], 0)
    ys = np.concatenate([g(c, "ysT").T for c in range(NCORES)], 0)[:, None, :]
    conv_p = np.stack([g(c, "o_conv_p").transpose(0, 2, 1) for c in range(NCORES)], 1)
    S_p = np.stack([g(c, "o_S_p") for c in range(NCORES)], 1)
    k_p = np.stack([g(c, "o_k_p").reshape(depth, 128, SW_KV, DH) for c in range(NCORES)], 1)
    v_p = np.stack([g(c, "o_v_p").reshape(depth, 128, SW_KV, DH) for c in range(NCORES)], 1)
    conv_s = np.concatenate([g(c, "o_conv_s") for c in range(NCORES)], 1)
    S_s = np.concatenate([g(c, "o_S_s") for c in range(NCORES)], 1)
    k_s = np.concatenate([g(c, "o_k_s").reshape(depth, NS, 128, SW_KV, DH) for c in range(NCORES)], 1)
    v_s = np.concatenate([g(c, "o_v_s").reshape(depth, NS, 128, SW_KV, DH) for c in range(NCORES)], 1)
    outs = (yp, ys, conv_p, S_p, k_p, v_p, conv_s, S_s, k_s, v_s)
    return tuple(np.ascontiguousarray(o, dtype=np.float32) for o in outs)


def kernel(**inputs):
    res, b = run(inputs)
    return assemble(res.results, DEPTH)
```
